# Optimizing a Trainium2 kernel written in Bass

```python
import math
import jax, jax.numpy as jnp
from jax import lax
import numpy as np

D_MODEL = 2048
BATCH = 4
SEQ = 8192
DEPTH = 2

CHUNK = 64
HG_HEADS = 6
HG_DK = 128
HG_DV = 128
HG_W = HG_HEADS * HG_DV
S5_GROUP_CH = 16
S5_W = 512
S5_GROUPS = S5_W // S5_GROUP_CH
S5_STATE = 64
RET_HEADS = 6
RET_DK = 64
RET_DV = 128
RET_W = RET_HEADS * RET_DV
D_MIX = HG_W + S5_W + RET_W
IN_SIZES = (HG_W, HG_W, HG_W, HG_W,
            S5_W,
            RET_HEADS * RET_DK, RET_HEADS * RET_DK,
            RET_W, RET_W)
N_IN = sum(IN_SIZES)
D_FF = 5632
CONV_W = 3
NORM_EPS = 1e-6
ROPE_BASE = 10000.0

kernel_name = "hymba_style_hgrn2_s5_retention_convffn"


def _rms(x):
    xf = x.astype(jnp.float32)
    return xf * lax.rsqrt(jnp.mean(xf * xf, axis=-1, keepdims=True) + NORM_EPS)


def rms_norm(x, gain):
    return (_rms(x) * gain.astype(jnp.float32)).astype(x.dtype)


def hgrn2_mixer(q_raw, f_raw, i_raw, g_raw, lb, onorm):
    B, T, _ = q_raw.shape
    N = T // CHUNK
    f32 = jnp.float32

    def heads(z, d):
        return z.astype(f32).reshape(B, N, CHUNK, HG_HEADS, d).transpose(0, 1, 3, 2, 4)

    q = jax.nn.silu(heads(q_raw, HG_DK))
    fr = heads(f_raw, HG_DK)
    v = heads(i_raw, HG_DV)
    lb = lb.astype(f32).reshape(HG_HEADS, 1, HG_DK)
    log_f = jnp.logaddexp(jnp.log(lb), jnp.log1p(-lb) + jax.nn.log_sigmoid(fr))
    k = (1.0 - lb) * jax.nn.sigmoid(-fr)
    b = jnp.cumsum(log_f, axis=3)
    b_last = b[:, :, :, -1:, :]
    q_dec = q * jnp.exp(b)
    k_dec = k * jnp.exp(b_last - b)
    kv = jnp.einsum('bnhcd,bnhce->bnhde', k_dec, v)
    decay = jnp.exp(b_last[:, :, :, 0, :])

    def step(S, inp):
        dec, kv_n = inp
        return dec[..., None] * S + kv_n, S

    S0 = jnp.zeros((B, HG_HEADS, HG_DK, HG_DV), f32)
    _, S_prev = lax.scan(step, S0, (jnp.moveaxis(decay, 1, 0), jnp.moveaxis(kv, 1, 0)))
    S_prev = jnp.moveaxis(S_prev, 0, 1)
    o_inter = jnp.einsum('bnhcd,bnhde->bnhce', q_dec, S_prev)

    causal = jnp.tril(jnp.ones((CHUNK, CHUNK), dtype=bool))[:, :, None]

    def intra(args):
        qn, kn, vn, bn = args
        diff = bn[:, :, :, None, :] - bn[:, :, None, :, :]
        w = jnp.exp(jnp.where(causal, diff, -jnp.inf))
        A = jnp.einsum('bhtd,bhsd,bhtsd->bhts', qn, kn, w)
        return jnp.einsum('bhts,bhse->bhte', A, vn)

    o_intra = lax.map(intra, (jnp.moveaxis(q, 1, 0), jnp.moveaxis(k, 1, 0),
                              jnp.moveaxis(v, 1, 0), jnp.moveaxis(b, 1, 0)))
    o = o_inter + jnp.moveaxis(o_intra, 0, 1)
    o = o.transpose(0, 1, 3, 2, 4).reshape(B, T, HG_HEADS, HG_DV)
    gate = jax.nn.silu(g_raw.astype(f32).reshape(B, T, HG_HEADS, HG_DV))
    o = _rms(o) * onorm.astype(f32) * gate
    return o.reshape(B, T, HG_W)


def s5_mixer(u, lam_re, lam_im, log_dt, b_re, b_im, c_re, c_im, d_skip, w_glu, b_glu):
    B, T, _ = u.shape
    f32 = jnp.float32
    uf = u.astype(f32).reshape(B, T, S5_GROUPS, S5_GROUP_CH)
    dt = jnp.exp(log_dt.astype(f32))[:, None]
    lr = jnp.minimum(lam_re.astype(f32), -1e-4)
    li = lam_im.astype(f32)
    mag = jnp.exp(lr * dt)
    ar, ai = mag * jnp.cos(li * dt), mag * jnp.sin(li * dt)
    den = lr * lr + li * li
    nr, ni = ar - 1.0, ai
    cr, ci = (nr * lr + ni * li) / den, (ni * lr - nr * li) / den
    br_, bi_ = b_re.astype(f32), b_im.astype(f32)
    bbr = cr[..., None] * br_ - ci[..., None] * bi_
    bbi = cr[..., None] * bi_ + ci[..., None] * br_
    xr = jnp.einsum('btgc,gpc->btgp', uf, bbr)
    xi = jnp.einsum('btgc,gpc->btgp', uf, bbi)
    a_r = jnp.broadcast_to(ar[None, None], (1, T, S5_GROUPS, S5_STATE))
    a_i = jnp.broadcast_to(ai[None, None], (1, T, S5_GROUPS, S5_STATE))

    def combine(e1, e2):
        a1r, a1i, b1r, b1i = e1
        a2r, a2i, b2r, b2i = e2
        return (a2r * a1r - a2i * a1i, a2r * a1i + a2i * a1r,
                a2r * b1r - a2i * b1i + b2r, a2r * b1i + a2i * b1r + b2i)

    _, _, sr, si = lax.associative_scan(combine, (a_r, a_i, xr, xi), axis=1)
    y = (jnp.einsum('btgp,gcp->btgc', sr, c_re.astype(f32))
         - jnp.einsum('btgp,gcp->btgc', si, c_im.astype(f32)))
    y = y.reshape(B, T, S5_W) + d_skip.astype(f32) * uf.reshape(B, T, S5_W)
    y = jax.nn.gelu(y, approximate=False)
    return y * jax.nn.sigmoid(y @ w_glu.astype(f32) + b_glu.astype(f32))


def retention_mixer(q_raw, k_raw, v_raw, g_raw):
    B, T, _ = q_raw.shape
    N = T // CHUNK
    f32 = jnp.float32
    pos = jnp.arange(T, dtype=f32)
    inv_freq = 1.0 / (ROPE_BASE ** jnp.linspace(0.0, 1.0, RET_DK // 2, dtype=f32))
    ang = pos[:, None] * inv_freq[None, :]
    cos, sin = jnp.cos(ang)[:, None, :], jnp.sin(ang)[:, None, :]

    def rope(z):
        z1, z2 = jnp.split(z, 2, axis=-1)
        return jnp.concatenate([z1 * cos - z2 * sin, z2 * cos + z1 * sin], axis=-1)

    q = rope(q_raw.astype(f32).reshape(B, T, RET_HEADS, RET_DK))
    k = rope(k_raw.astype(f32).reshape(B, T, RET_HEADS, RET_DK)) * (RET_DK ** -0.5)
    v = v_raw.astype(f32).reshape(B, T, RET_HEADS, RET_DV)
    q = q.reshape(B, N, CHUNK, RET_HEADS, RET_DK)
    k = k.reshape(B, N, CHUNK, RET_HEADS, RET_DK)
    v = v.reshape(B, N, CHUNK, RET_HEADS, RET_DV)

    log_g = jnp.log(1.0 - 2.0 ** (-5.0 - jnp.arange(RET_HEADS, dtype=f32)))
    idx = jnp.arange(CHUNK, dtype=f32)
    rel = idx[:, None] - idx[None, :]
    Dmat = jnp.where(rel[None] >= 0, jnp.exp(jnp.maximum(rel, 0.0)[None] * log_g[:, None, None]), 0.0)
    scores = jnp.einsum('bnthd,bnshd->bnhts', q, k) * Dmat
    o_intra = jnp.einsum('bnhts,bnshe->bnthe', scores, v)

    q_dec = q * jnp.exp((idx + 1.0)[:, None] * log_g[None, :])[..., None]
    k_dec = k * jnp.exp((CHUNK - 1.0 - idx)[:, None] * log_g[None, :])[..., None]
    kv = jnp.einsum('bnshd,bnshe->bnhde', k_dec, v)
    chunk_decay = jnp.exp(CHUNK * log_g)[None, :, None, None]

    def step(S, kv_n):
        return chunk_decay * S + kv_n, S

    S0 = jnp.zeros((B, RET_HEADS, RET_DK, RET_DV), f32)
    _, S_prev = lax.scan(step, S0, jnp.moveaxis(kv, 1, 0))
    S_prev = jnp.moveaxis(S_prev, 0, 1)
    o_inter = jnp.einsum('bnthd,bnhde->bnthe', q_dec, S_prev)
    o = (o_intra + o_inter).reshape(B, T, RET_HEADS, RET_DV)
    gate = jax.nn.silu(g_raw.astype(f32).reshape(B, T, RET_HEADS, RET_DV))
    return (_rms(o) * gate).reshape(B, T, RET_W)


def conv_gated_ffn(h, w_up, conv_w, conv_b, w_down):
    up = h @ w_up.astype(h.dtype)
    C = up.shape[-1]
    kern = conv_w.astype(up.dtype)[:, None, :]
    up = lax.conv_general_dilated(up, kern, window_strides=(1,), padding=((CONV_W - 1, 0),),
                                  dimension_numbers=('NWC', 'WIO', 'NWC'),
                                  feature_group_count=C) + conv_b.astype(up.dtype)
    gate, val = jnp.split(up, 2, axis=-1)
    return (jax.nn.silu(gate) * val) @ w_down.astype(h.dtype)


def setup_inputs(seed: int = 0) -> dict:
    key = jax.random.key(seed)
    ks = jax.random.split(key, 24)
    f32 = jnp.float32
    nrm = lambda k, s, sc: jax.random.normal(k, s, f32) * sc
    gain = lambda k, s: 1.0 + 0.02 * jax.random.normal(k, s, f32)
    n = jnp.arange(S5_STATE, dtype=f32)
    return {
        "x": jax.random.normal(ks[0], (BATCH, SEQ, D_MODEL), f32),
        "w_in": nrm(ks[1], (DEPTH, D_MODEL, N_IN), D_MODEL ** -0.5),
        "w_out": nrm(ks[2], (DEPTH, D_MIX, D_MODEL), D_MIX ** -0.5),
        "mix_beta": gain(ks[3], (DEPTH, D_MIX)),
        "hg_lb_logits": jax.random.normal(ks[4], (DEPTH, HG_HEADS * HG_DK), f32),
        "hg_onorm": gain(ks[5], (DEPTH, HG_DV)),
        "s5_lam_re": -0.5 + 0.01 * jax.random.normal(ks[6], (DEPTH, S5_GROUPS, S5_STATE), f32),
        "s5_lam_im": math.pi * n + 0.01 * jax.random.normal(ks[7], (DEPTH, S5_GROUPS, S5_STATE), f32),
        "s5_log_dt": jax.random.uniform(ks[8], (DEPTH, S5_GROUPS), f32, math.log(1e-3), math.log(1e-1)),
        "s5_b_re": nrm(ks[9], (DEPTH, S5_GROUPS, S5_STATE, S5_GROUP_CH), (2 * S5_GROUP_CH) ** -0.5),
        "s5_b_im": nrm(ks[10], (DEPTH, S5_GROUPS, S5_STATE, S5_GROUP_CH), (2 * S5_GROUP_CH) ** -0.5),
        "s5_c_re": nrm(ks[11], (DEPTH, S5_GROUPS, S5_GROUP_CH, S5_STATE), S5_STATE ** -0.5),
        "s5_c_im": nrm(ks[12], (DEPTH, S5_GROUPS, S5_GROUP_CH, S5_STATE), S5_STATE ** -0.5),
        "s5_d": jax.random.normal(ks[13], (DEPTH, S5_W), f32),
        "s5_w_glu": nrm(ks[14], (DEPTH, S5_W, S5_W), S5_W ** -0.5),
        "s5_b_glu": nrm(ks[15], (DEPTH, S5_W), 0.01),
        "ffn_w_up": nrm(ks[16], (DEPTH, D_MODEL, 2 * D_FF), D_MODEL ** -0.5),
        "ffn_conv_w": nrm(ks[17], (DEPTH, CONV_W, 2 * D_FF), CONV_W ** -0.5),
        "ffn_conv_b": nrm(ks[18], (DEPTH, 2 * D_FF), 0.01),
        "ffn_w_down": nrm(ks[19], (DEPTH, D_FF, D_MODEL), D_FF ** -0.5),
        "g_pre_mix": gain(ks[20], (DEPTH, D_MODEL)),
        "g_post_mix": gain(ks[21], (DEPTH, D_MODEL)),
        "g_pre_ffn": gain(ks[22], (DEPTH, D_MODEL)),
        "g_post_ffn": gain(ks[23], (DEPTH, D_MODEL)),
    }


def reference(x, w_in, w_out, mix_beta, hg_lb_logits, hg_onorm, s5_lam_re, s5_lam_im, s5_log_dt,
              s5_b_re, s5_b_im, s5_c_re, s5_c_im, s5_d, s5_w_glu, s5_b_glu,
              ffn_w_up, ffn_conv_w, ffn_conv_b, ffn_w_down,
              g_pre_mix, g_post_mix, g_pre_ffn, g_post_ffn):
    p = jax.nn.softmax(hg_lb_logits.astype(jnp.float32), axis=0)
    cum = jnp.cumsum(p, axis=0)
    lower_bounds = cum - cum[0:1]
    split_at = [int(v) for v in np.cumsum(IN_SIZES)[:-1]]
    for l in range(DEPTH):
        h = rms_norm(x, g_pre_mix[l])
        z = h @ w_in[l].astype(h.dtype)
        hq, hf, hi, hg, su, rq, rk, rv, rg = jnp.split(z, split_at, axis=-1)
        o_hg = hgrn2_mixer(hq, hf, hi, hg, lower_bounds[l], hg_onorm[l])
        o_s5 = s5_mixer(su, s5_lam_re[l], s5_lam_im[l], s5_log_dt[l], s5_b_re[l], s5_b_im[l],
                        s5_c_re[l], s5_c_im[l], s5_d[l], s5_w_glu[l], s5_b_glu[l])
        o_rt = retention_mixer(rq, rk, rv, rg)
        mix = jnp.concatenate([o_hg, o_s5, o_rt], axis=-1) * mix_beta[l].astype(jnp.float32)
        y = mix.astype(x.dtype) @ w_out[l].astype(x.dtype)
        x = x + rms_norm(y, g_post_mix[l])
        h = rms_norm(x, g_pre_ffn[l])
        f = conv_gated_ffn(h, ffn_w_up[l], ffn_conv_w[l], ffn_conv_b[l], ffn_w_down[l])
        x = x + rms_norm(f, g_post_ffn[l])
    return x
```

```python
import math
from contextlib import ExitStack

import numpy as np
import concourse.bass as bass
import concourse.mybir as mybir
from concourse.bass_utils import run_bass_kernel_spmd

F32 = mybir.dt.float32
BF16 = mybir.dt.bfloat16
I32 = mybir.dt.int32
AF = mybir.ActivationFunctionType
ALU = mybir.AluOpType

D = 2048
NIN = 5888
DFF = 5632
TB = 512
EPS = 1e-6
NCST = 2304
C_M64, C_M16, C_TPOS, C_CMASK, C_RMASK, C_ROWM, C_GDEC, C_IDENT = 0, 512, 1024, 1536, 1600, 1984, 1992, 2048


class Sched:
    ENG = ("pe", "act", "dve", "pool", "sp")

    def __init__(self, nc):
        self.nc = nc
        self.e = {"pe": nc.tensor, "act": nc.scalar, "dve": nc.vector, "pool": nc.gpsimd, "sp": nc.sync}
        self.sem = {}
        self.cnt = {}
        for k in self.ENG:
            self.sem[k] = nc.alloc_semaphore("sem_" + k)
            self.cnt[k] = 0
        self.seen = {k: {} for k in self.ENG}
        self.dsem = {}
        self.st = {}
        self.ninst = 0

    def _semof(self, key):
        if isinstance(key, str):
            return self.sem[key]
        return self.dsem[key[1]][0]

    def _wait(self, eng, tok):
        if tok is None:
            return
        key, val = tok
        if self.seen[eng].get(key, 0) >= val:
            return
        if key == eng and eng == "pe":
            return
        self.e[eng].wait_ge(self._semof(key), val)
        self.seen[eng][key] = val

    def _deps(self, eng, reads, writes):
        for r in reads:
            s = self.st.get(r)
            if s is not None:
                self._wait(eng, s["w"])
                if isinstance(r, tuple) and r[0] == "pb":
                    for t in s["r"]:
                        if t[0] != eng:
                            self._wait(eng, t)
        for w in writes:
            s = self.st.get(w)
            if s is not None:
                self._wait(eng, s["w"])
                for t in s["r"]:
                    self._wait(eng, t)

    def _commit(self, tok, reads, writes):
        for r in reads:
            s = self.st.setdefault(r, {"w": None, "r": []})
            s["r"] = [t for t in s["r"] if t[0] != tok[0]] + [tok]
        for w in writes:
            self.st[w] = {"w": tok, "r": []}

    def op(self, eng, fn, reads=(), writes=()):
        self._deps(eng, reads, writes)
        ins = fn()
        self.cnt[eng] += 1
        ins.then_inc(self.sem[eng], 1)
        self._commit((eng, self.cnt[eng]), reads, writes)
        self.ninst += 1
        return ins

    def mm_group(self, fns, reads=(), writes=()):
        self._deps("pe", reads, writes)
        ins = None
        for fn in fns:
            ins = fn()
        self.cnt["pe"] += 1
        ins.then_inc(self.sem["pe"], 1)
        self._commit(("pe", self.cnt["pe"]), reads, writes)
        self.ninst += len(fns)
        return ins

    def dma(self, q, out, in_, reads=(), writes=(), semreg=None, **kw):
        reg = semreg if semreg is not None else writes[0]
        if reg not in self.dsem:
            self.dsem[reg] = [self.nc.alloc_semaphore("dsem%d" % len(self.dsem)), 0]
        self._deps(q, reads, writes)
        ins = self.e[q].dma_start(out=out, in_=in_, **kw)
        self.dsem[reg][1] += 16
        ins.then_inc(self.dsem[reg][0], 16)
        self._commit((("d", reg), self.dsem[reg][1]), reads, writes)
        self.ninst += 1
        return ins

    def dma_group(self, q, pairs, reads=(), writes=(), semreg=None, **kw):
        reg = semreg if semreg is not None else writes[0]
        if reg not in self.dsem:
            self.dsem[reg] = [self.nc.alloc_semaphore("dsem%d" % len(self.dsem)), 0]
        self._deps(q, reads, writes)
        for (out, in_) in pairs:
            ins = self.e[q].dma_start(out=out, in_=in_, **kw)
            self.dsem[reg][1] += 16
            ins.then_inc(self.dsem[reg][0], 16)
            self.ninst += 1
        self._commit((("d", reg), self.dsem[reg][1]), reads, writes)

    def barrier(self):
        for e in self.ENG:
            for f in self.ENG:
                if f != e and self.cnt[f] > 0:
                    self._wait(e, (f, self.cnt[f]))
            for reg, (sem, tot) in self.dsem.items():
                if tot > 0:
                    self._wait(e, (("d", reg), tot))
        self.st = {}

    def finish(self, regions, eng="sp"):
        for r in regions:
            s = self.st.get(r)
            if s is not None:
                self._wait(eng, s["w"])


def _build_consts():
    c = np.zeros((128, NCST), np.float32)
    t = np.arange(512)
    c[:, C_M64:C_M64 + 512] = (t % 64 != 0).astype(np.float32)[None, :]
    c[:, C_M16:C_M16 + 512] = (t % 16 != 0).astype(np.float32)[None, :]
    c[:, C_TPOS:C_TPOS + 512] = (t + 1).astype(np.float32)[None, :]
    s = np.arange(128) % 64
    tt = np.arange(64)
    causal = (s[:, None] <= tt[None, :]).astype(np.float64)
    c[:, C_CMASK:C_CMASK + 64] = causal
    for h in range(6):
        lg = np.float32(np.log(np.float32(1.0) - np.float32(2.0) ** np.float32(-5.0 - h)))
        c[:, C_RMASK + 64 * h:C_RMASK + 64 * (h + 1)] = causal * np.exp(-64.0 * float(lg))
    r = np.arange(128)
    for gl in range(8):
        c[:, C_ROWM + gl] = (r // 16 == gl)
    for j in range(3):
        for hh in range(2):
            h = 2 * j + hh
            lg = np.float32(np.log(np.float32(1.0) - np.float32(2.0) ** np.float32(-5.0 - h)))
            c[64 * hh:64 * (hh + 1), C_GDEC + j] = np.exp(64.0 * float(lg))
    c[:, C_IDENT:C_IDENT + 128] = np.eye(128)
    return c


def _build_rope(pos0, nt):
    pos = (pos0 + np.arange(nt)).astype(np.float32)
    inv_freq = (np.float32(1.0) / (np.float32(10000.0) ** np.linspace(0.0, 1.0, 32, dtype=np.float32))).astype(np.float32)
    ang = (pos[None, :] * inv_freq[:, None]).astype(np.float32).astype(np.float64)
    cos, sin = np.cos(ang), np.sin(ang)
    tc = (np.arange(nt) % 64).astype(np.float64)
    out = np.zeros((3, 4, 128, nt), np.float32)
    for j in range(3):
        for hh in range(2):
            h = 2 * j + hh
            lg = float(np.float32(np.log(np.float32(1.0) - np.float32(2.0) ** np.float32(-5.0 - h))))
            gq = np.exp((tc + 1.0) * lg)[None, :]
            gk = np.exp((63.0 - tc) * lg)[None, :] * (64.0 ** -0.5)
            r0 = 64 * hh
            cc = np.concatenate([cos, cos], 0)
            ss = np.concatenate([-sin, sin], 0)
            out[j, 0, r0:r0 + 64] = cc * gq
            out[j, 1, r0:r0 + 64] = ss * gq
            out[j, 2, r0:r0 + 64] = cc * gk
            out[j, 3, r0:r0 + 64] = ss * gk
    return out


def _inproj_chunks():
    ch = []
    for h in range(6):
        ch.append([(h * 128, 128), (768 + h * 128, 128), (2304 + h * 128, 128), (1536 + h * 128, 128)])
    for j in range(3):
        segs = []
        for base in (3584, 3968):
            segs.append((base + 128 * j, 128))
            for hh in range(2):
                h = 2 * j + hh
                segs.append((base + h * 64 + 32, 32))
                segs.append((base + h * 64, 32))
        ch.append(segs)
    for j in range(3):
        ch.append([(4352 + 256 * j, 256), (5120 + 256 * j, 256)])
    ch.append([(3072, 512)])
    return ch


PARAM_NAMES = ["w_in", "w_out", "mix_beta", "hg_lb_logits", "hg_onorm", "s5_lam_re", "s5_lam_im", "s5_log_dt",
               "s5_b_re", "s5_b_im", "s5_c_re", "s5_c_im", "s5_d", "s5_w_glu", "s5_b_glu",
               "ffn_w_up", "ffn_conv_w", "ffn_conv_b", "ffn_w_down",
               "g_pre_mix", "g_post_mix", "g_pre_ffn", "g_post_ffn"]
PARAM_SHAPES = {
    "w_in": [2, 2048, 5888], "w_out": [2, 2048, 2048], "mix_beta": [2, 2048], "hg_lb_logits": [2, 768],
    "hg_onorm": [2, 128], "s5_lam_re": [2, 32, 64], "s5_lam_im": [2, 32, 64], "s5_log_dt": [2, 32],
    "s5_b_re": [2, 32, 64, 16], "s5_b_im": [2, 32, 64, 16], "s5_c_re": [2, 32, 16, 64], "s5_c_im": [2, 32, 16, 64],
    "s5_d": [2, 512], "s5_w_glu": [2, 512, 512], "s5_b_glu": [2, 512], "ffn_w_up": [2, 2048, 11264],
    "ffn_conv_w": [2, 3, 11264], "ffn_conv_b": [2, 11264], "ffn_w_down": [2, 5632, 2048],
    "g_pre_mix": [2, 2048], "g_post_mix": [2, 2048], "g_pre_ffn": [2, 2048], "g_post_ffn": [2, 2048],
}
A_GPRE, A_GPOST, A_GPREF, A_GPOSTF, A_BETA, A_LB0, A_LB1, A_ONORM, A_S5D, A_BGLU, A_N = 0, 16, 32, 48, 64, 80, 86, 92, 93, 97, 101


class _Stop(Exception):
    pass


def build_program(NT, depth=2, dbg=None, stop=None):
    nc = bass.Bass("TRN2", target_bir_lowering=False)
    S = Sched(nc)
    NB = NT // TB
    uid = [0]

    def din(name, shape, dt=F32):
        return nc.dram_tensor(name, list(shape), dt, kind="ExternalInput").ap()

    def dscr(name, shape, dt):
        return nc.dram_tensor(name, list(shape), dt).ap()

    x_in = din("x", [NT, D])
    P = {n: din(n, PARAM_SHAPES[n]) for n in PARAM_NAMES}
    cst_d = din("cst", [128, NCST])
    rope_d = din("rope", [3, 4, 128, NT])
    out_d = nc.dram_tensor("out", [NT, D], F32, kind="ExternalOutput").ap()

    wsi = dscr("wsi", [2, 13, 128, 8192], BF16)
    wso = dscr("wso", [2, 4, 128, 8192], BF16)
    wsu = dscr("wsu", [2, 22, 128, 8192], BF16)
    wsd = dscr("wsd", [2, 16, 128, 44 * 128], BF16)
    wsg = dscr("wsg", [2, 128, 2048], BF16)
    s5tab = dscr("s5tab", [2, 3, 128, 4096], BF16)
    etab = dscr("etab", [2, 32, 2, 128, 512], F32)
    xs = [dscr("xs0", [16, 128, NT], F32), dscr("xs1", [16, 128, NT], F32)]

    dbg_out = {}
    if dbg:
        for nm in ("mix0", "xmid0", "xout0", "h0"):
            dbg_out[nm] = nc.dram_tensor("dbg_" + nm, [16, 128, NT], F32, kind="ExternalOutput").ap()

    stopped = [False]

    def stage(name):
        if stop is not None and stop == name:
            stopped[0] = True
        return stopped[0]

    def sb(es, shape, dt, name="t"):
        uid[0] += 1
        return es.enter_context(nc.sbuf_tensor("%s_%d" % (name, uid[0]), list(shape), dt)).ap()

    top = ExitStack()
    pbank = [nc.alloc_psum_tensor("pb%d" % i, [128, 512], F32).ap() for i in range(8)]

    def PB(i):
        return ("pb", i)

    C = sb(top, [128, NCST], F32, "cst")
    identf = C[:, C_IDENT:C_IDENT + 128]
    identb = sb(top, [128, 128], BF16, "identb")
    onesb = sb(top, [128, 128], BF16, "onesb")
    colsA = sb(top, [128, 2, A_N], F32, "colsA")
    colsB = sb(top, [128, 2, 352], F32, "colsB")
    lbc = sb(top, [128, 2, 6], F32, "lbc")
    omlc = sb(top, [128, 2, 6], F32, "omlc")
    gsc = sb(top, [128, 2, 16], F32, "gsc")
    S_hg = sb(top, [128, 2, 6, 128], F32, "S_hg")
    S_rt = sb(top, [128, 2, 3, 128], F32, "S_rt")
    Wst = sb(top, [128, 2, 2, 32], F32, "Wst")
    halo = sb(top, [128, 2, 88, 2], F32, "halo")
    rho = sb(top, [128, 2, 32], F32, "rho")
    hT = sb(top, [128, 16, 512], BF16, "hT")
    mixT_h = [None]
    wbuf = [sb(top, [128, 8192], BF16, "wbuf%d" % i) for i in range(2)]
    wslot = [0]

    def load_w(src2d, ncols=8192):
        s = wslot[0] % 2
        wslot[0] += 1
        S.dma("sp", wbuf[s][:, 0:ncols], src2d, reads=[], writes=[("wbuf", s)])
        return s

    def act(out, in_, func, reads, writes, scale=1.0, bias=None):
        if bias is None:
            return S.op("act", lambda: nc.scalar.activation(out=out, in_=in_, func=func, scale=scale), reads, writes)
        return S.op("act", lambda: nc.scalar.activation(out=out, in_=in_, func=func, scale=scale, bias=bias), reads, writes)

    def tt(out, in0, in1, op, reads, writes, eng="dve"):
        e = nc.vector if eng == "dve" else nc.gpsimd
        return S.op(eng, lambda: e.tensor_tensor(out=out, in0=in0, in1=in1, op=op), reads, writes)

    def ts(out, in0, s1, s2, op0, op1, reads, writes, eng="dve"):
        e = nc.vector if eng == "dve" else nc.gpsimd
        if op1 is None:
            return S.op(eng, lambda: e.tensor_scalar(out=out, in0=in0, scalar1=s1, scalar2=None, op0=op0), reads, writes)
        return S.op(eng, lambda: e.tensor_scalar(out=out, in0=in0, scalar1=s1, scalar2=s2, op0=op0, op1=op1), reads, writes)

    def stt(out, in0, scalar, in1, op0, op1, reads, writes):
        return S.op("dve", lambda: nc.vector.scalar_tensor_tensor(out=out, in0=in0, scalar=scalar, in1=in1, op0=op0, op1=op1), reads, writes)

    def cp(eng, out, in_, reads, writes):
        if eng == "act":
            return S.op("act", lambda: nc.scalar.copy(out=out, in_=in_), reads, writes)
        e = nc.vector if eng == "dve" else nc.gpsimd
        return S.op(eng, lambda: e.tensor_copy(out=out, in_=in_), reads, writes)

    def mset(eng, ap, val, writes):
        e = nc.vector if eng == "dve" else nc.gpsimd
        return S.op(eng, lambda: e.memset(ap, val), (), writes)

    def mm(out, lhsT, rhs, start=True, stop=True):
        return lambda: nc.tensor.matmul(out, lhsT=lhsT, rhs=rhs, start=start, stop=stop)

    def tr(out, in_, ident):
        return lambda: nc.tensor.transpose(out=out, in_=in_, identity=ident)

    try:
        S.dma("sp", C[:], cst_d[:, :], reads=[], writes=["C"])
        cp("dve", identb[:], identf, ["C"], ["identb"])
        mset("dve", onesb[:], 1.0, ["onesb"])
        for nm, t_ in (("S_hg", S_hg), ("S_rt", S_rt), ("Wst", Wst), ("halo", halo)):
            mset("pool", t_[:], 0.0, [nm])

        with ExitStack() as es:
            stg = sb(es, [128, 128], F32, "stg")
            for l in range(depth):
                mset("dve", stg[:], 0.0, ["stg"])
                rows = [(P["g_pre_mix"][l], 16), (P["g_post_mix"][l], 16), (P["g_pre_ffn"][l], 16), (P["g_post_ffn"][l], 16),
                        (P["mix_beta"][l], 16), (P["hg_lb_logits"][0], 6), (P["hg_lb_logits"][1], 6),
                        (P["hg_onorm"][l], 1), (P["s5_d"][l], 4), (P["s5_b_glu"][l], 4)]
                r0 = 0
                pairs = []
                for src_, n in rows:
                    pairs.append((stg[r0:r0 + n, :], src_.rearrange("(r c) -> r c", c=128)))
                    r0 += n
                S.dma_group("sp", pairs, reads=[], writes=["stg"])
                S.mm_group([tr(pbank[0][:, 0:128], stg[:, :], identf)], reads=["stg", "C"], writes=[PB(0)])
                cp("dve", colsA[:, l, :], pbank[0][:, 0:A_N], [PB(0)], ["colsA"])
                for part in range(3):
                    nrow = 128 if part < 2 else 96
                    mset("dve", stg[:], 0.0, ["stg"])
                    pairs = []
                    r = 0
                    while r < nrow:
                        gr = part * 128 + r
                        if gr < 264:
                            j, t0 = gr // 88, gr % 88
                            n = min(88 - t0, nrow - r)
                            src_ = P["ffn_conv_w"][l, j, t0 * 128:(t0 + n) * 128]
                        else:
                            t0 = gr - 264
                            n = min(88 - t0, nrow - r)
                            src_ = P["ffn_conv_b"][l, t0 * 128:(t0 + n) * 128]
                        pairs.append((stg[r:r + n, :], src_.rearrange("(r c) -> r c", c=128)))
                        r += n
                    S.dma_group("sp", pairs, reads=[], writes=["stg"])
                    S.mm_group([tr(pbank[0][:, 0:128], stg[:, :], identf)], reads=["stg", "C"], writes=[PB(0)])
                    cp("dve", colsB[:, l, part * 128:part * 128 + nrow], pbank[0][:, 0:nrow], [PB(0)], ["colsB"])
            mset("dve", lbc[:], 0.0, ["lbc"])
            if depth > 1:
                tt(lbc[:, 1, :], colsA[:, 0, A_LB1:A_LB1 + 6], colsA[:, 0, A_LB0:A_LB0 + 6], ALU.subtract, ["colsA"], ["lbc"])
                act(lbc[:, 1, :], lbc[:, 1, :], AF.Sigmoid, ["lbc"], ["lbc"])
            ts(omlc[:], lbc[:], -1.0, 1.0, ALU.mult, ALU.add, ["lbc"], ["omlc"])
            for l in range(depth):
                cp("dve", gsc[:, l, :], colsA[:, l, A_BETA:A_BETA + 16], ["colsA"], ["gsc"])
                ts(gsc[:, l, 0:6], gsc[:, l, 0:6], colsA[:, l, A_ONORM:A_ONORM + 1], None, ALU.mult, None, ["gsc", "colsA"], ["gsc"])
            S.barrier()
        stage("setup0")

        with ExitStack() as es:
            NS = 3
            st32 = [sb(es, [128, 2048], F32, "st32") for _ in range(NS)]
            st16 = [sb(es, [128, 2048], BF16, "st16") for _ in range(NS)]
            pc = [0]
            ceng = ["dve", "act", "pool"]

            def conv_piece(srcs, dst, dst_view=None):
                s = pc[0] % NS
                pc[0] += 1
                pairs = []
                for (src_, off, n) in srcs:
                    if off is None:
                        pairs.append((st32[s][:], src_))
                    else:
                        pairs.append((st32[s][:].rearrange("p (k c) -> p k c", c=512)[:, :, off:off + n], src_))
                S.dma_group("sp", pairs, reads=[], writes=[("st32", s)])
                cp(ceng[s % 3], st16[s][:], st32[s][:], [("st32", s)], [("st16", s)])
                srcv = st16[s][:] if dst_view is None else dst_view(st16[s][:])
                S.dma("pool", dst, srcv, reads=[("st16", s)], writes=[("wscr", s)])

            chunks = _inproj_chunks()
            for l in range(depth):
                for ci, segs in enumerate(chunks):
                    for kq in range(4):
                        srcs = []
                        off = 0
                        for (c0, n) in segs:
                            srcs.append((P["w_in"][l, kq * 512:(kq + 1) * 512, c0:c0 + n].rearrange("(k p) c -> p k c", p=128), off, n))
                            off += n
                        conv_piece(srcs, wsi[l, ci, :, kq * 2048:(kq + 1) * 2048])
                for ci in range(4):
                    for kq in range(4):
                        src_ = P["w_out"][l, kq * 512:(kq + 1) * 512, ci * 512:(ci + 1) * 512].rearrange("(k p) c -> p k c", p=128)
                        conv_piece([(src_, 0, 512)], wso[l, ci, :, kq * 2048:(kq + 1) * 2048])
                for ci in range(22):
                    for kq in range(4):
                        srcs = []
                        for i, c0 in enumerate((256 * ci, 5632 + 256 * ci)):
                            srcs.append((P["ffn_w_up"][l, kq * 512:(kq + 1) * 512, c0:c0 + 256].rearrange("(k p) c -> p k c", p=128), 256 * i, 256))
                        conv_piece(srcs, wsu[l, ci, :, kq * 2048:(kq + 1) * 2048])
                for kt in range(44):
                    conv_piece([(P["ffn_w_down"][l, kt * 128:(kt + 1) * 128, :], None, 0)],
                               wsd[l].rearrange("o p (k c) -> p o k c", c=128)[:, :, kt, :],
                               dst_view=lambda t: t.rearrange("p (o c) -> p o c", c=128))
                conv_piece([(P["s5_w_glu"][l].rearrange("(k p) c -> p k c", p=128), 0, 512)], wsg[l, :, :])
            S.barrier()

        stage("conv")
        for l in range(depth):
            with ExitStack() as es:
                lam = sb(es, [128, 3, 128], F32, "lam")
                mset("dve", lam[:], 0.0, ["lam"])
                S.dma_group("sp", [(lam[0:32, 0, 0:64], P["s5_lam_re"][l]), (lam[0:32, 0, 64:128], P["s5_lam_re"][l]),
                                   (lam[0:32, 1, 0:64], P["s5_lam_im"][l]), (lam[0:32, 1, 64:128], P["s5_lam_im"][l]),
                                   (lam[0:32, 2, 0:1], P["s5_log_dt"][l].rearrange("(g o) -> g o", o=1))], reads=[], writes=["lam"])
                gm = sb(es, [128, 5, 128], F32, "gm")
                act(gm[0:32, 0, 0:1], lam[0:32, 2, 0:1], AF.Exp, ["lam"], ["gm0"])
                ts(gm[0:32, 1, :], lam[0:32, 0, :], -1e-4, None, ALU.min, None, ["lam"], ["gm1"])
                ts(gm[0:32, 2, :], gm[0:32, 1, :], gm[0:32, 0, 0:1], None, ALU.mult, None, ["gm0", "gm1"], ["gm2"])
                act(gm[0:32, 3, :], gm[0:32, 2, :], AF.Exp, ["gm2"], ["gm3"])
                ts(gm[0:32, 4, :], lam[0:32, 1, :], gm[0:32, 0, 0:1], None, ALU.mult, None, ["lam", "gm0"], ["gm4"])
                cols = sb(es, [128, 4, 32], F32, "s5cols")
                for k_, src_ap in ((0, gm[0:32, 1, :]), (1, lam[0:32, 1, :]), (2, gm[0:32, 3, :]), (3, gm[0:32, 4, :])):
                    S.mm_group([tr(pbank[0][:, 0:32], src_ap, identf[0:32, 0:32])], reads=["gm1", "gm3", "gm4", "lam", "C"], writes=[PB(0)])
                    cp("dve", cols[:, k_, :], pbank[0][:, 0:32], [PB(0)], [("cols", k_)])
                cr_ = [("cols", k_) for k_ in range(4)]
                cp("dve", rho[:, l, :], cols[:, 2, :], cr_, ["rho"])
                wk = sb(es, [128, 12, 32], F32, "s5wk")
                wki = sb(es, [128, 32], I32, "s5wki")

                def sin_of(dst, src_, shift, rd, wr):
                    ts(wk[:, 10, :], src_, shift, 1.0 / (2 * math.pi), ALU.add, ALU.mult, rd, ["wk10"])
                    cp("dve", wki[:], wk[:, 10, :], ["wk10"], ["wki"])
                    cp("dve", wk[:, 11, :], wki[:], ["wki"], ["wk11"])
                    ts(wk[:, 10, :], wk[:, 11, :], -2 * math.pi, shift, ALU.mult, ALU.add, ["wk11"], ["wk10"])
                    tt(wk[:, 10, :], wk[:, 10, :], src_, ALU.add, ["wk10"] + rd, ["wk10"])
                    ts(wk[:, 10, :], wk[:, 10, :], math.pi, -math.pi, ALU.min, ALU.max, ["wk10"], ["wk10"])
                    act(dst, wk[:, 10, :], AF.Sin, ["wk10"], wr)

                sin_of(wk[:, 0, :], cols[:, 3, :], 0.0, cr_, ["wk0"])
                sin_of(wk[:, 1, :], cols[:, 3, :], math.pi / 2, cr_, ["wk1"])
                tt(wk[:, 2, :], wk[:, 1, :], cols[:, 2, :], ALU.mult, ["wk1"] + cr_, ["wk2"])
                tt(wk[:, 3, :], wk[:, 0, :], cols[:, 2, :], ALU.mult, ["wk0"] + cr_, ["wk3"])
                tt(wk[:, 4, :], cols[:, 0, :], cols[:, 0, :], ALU.mult, cr_, ["wk4"])
                tt(wk[:, 5, :], cols[:, 1, :], cols[:, 1, :], ALU.mult, cr_, ["wk5"])
                tt(wk[:, 4, :], wk[:, 4, :], wk[:, 5, :], ALU.add, ["wk4", "wk5"], ["wk4"])
                S.op("dve", lambda: nc.vector.reciprocal(out=wk[:, 4, :], in_=wk[:, 4, :]), ["wk4"], ["wk4"])
                ts(wk[:, 5, :], wk[:, 2, :], -1.0, None, ALU.add, None, ["wk2"], ["wk5"])
                tt(wk[:, 6, :], wk[:, 5, :], cols[:, 0, :], ALU.mult, ["wk5"] + cr_, ["wk6"])
                tt(wk[:, 7, :], wk[:, 3, :], cols[:, 1, :], ALU.mult, ["wk3"] + cr_, ["wk7"])
                tt(wk[:, 6, :], wk[:, 6, :], wk[:, 7, :], ALU.add, ["wk6", "wk7"], ["wk6"])
                tt(wk[:, 6, :], wk[:, 6, :], wk[:, 4, :], ALU.mult, ["wk6", "wk4"], ["wk6"])
                tt(wk[:, 7, :], wk[:, 3, :], cols[:, 0, :], ALU.mult, ["wk3"] + cr_, ["wk7"])
                tt(wk[:, 8, :], wk[:, 5, :], cols[:, 1, :], ALU.mult, ["wk5"] + cr_, ["wk8"])
                tt(wk[:, 7, :], wk[:, 7, :], wk[:, 8, :], ALU.subtract, ["wk7", "wk8"], ["wk7"])
                tt(wk[:, 7, :], wk[:, 7, :], wk[:, 4, :], ALU.mult, ["wk7", "wk4"], ["wk7"])
                bre = sb(es, [128, 32, 16], F32, "bre")
                bim = sb(es, [128, 32, 16], F32, "bim")
                S.dma_group("sp", [(bre[0:64, :, :], P["s5_b_re"][l].rearrange("g p c -> p g c")),
                                   (bim[0:64, :, :], P["s5_b_im"][l].rearrange("g p c -> p g c"))], reads=[], writes=["bre"])
                crb = wk[0:64, 6, :].unsqueeze(2).to_broadcast([64, 32, 16])
                cib = wk[0:64, 7, :].unsqueeze(2).to_broadcast([64, 32, 16])
                bb = sb(es, [128, 5, 32, 16], F32, "bb")
                tt(bb[0:64, 0], bre[0:64], crb, ALU.mult, ["bre", "wk6"], ["bb0"])
                tt(bb[0:64, 2], bim[0:64], cib, ALU.mult, ["bre", "wk7"], ["bb2"])
                tt(bb[0:64, 0], bb[0:64, 0], bb[0:64, 2], ALU.subtract, ["bb0", "bb2"], ["bb0"])
                tt(bb[0:64, 1], bim[0:64], crb, ALU.mult, ["bre", "wk6"], ["bb1"])
                tt(bb[0:64, 3], bre[0:64], cib, ALU.mult, ["bre", "wk7"], ["bb3"])
                tt(bb[0:64, 1], bb[0:64, 1], bb[0:64, 3], ALU.add, ["bb1", "bb3"], ["bb1"])
                ts(bb[0:64, 4], bb[0:64, 1], -1.0, None, ALU.mult, None, ["bb1"], ["bb4"])
                tabs = sb(es, [128, 3, 32, 128], BF16, "s5tabs")
                mset("pool", tabs[:], 0.0, ["tabs"])
                for t4 in range(4):
                    for vi, (ire, iim) in enumerate(((0, 1), (4, 0))):
                        S.mm_group([tr(pbank[1][:, 0:64], bb[0:64, ire, t4 * 8:(t4 + 1) * 8, :], identf[0:64, 0:64]),
                                    tr(pbank[1][:, 64:128], bb[0:64, iim, t4 * 8:(t4 + 1) * 8, :], identf[0:64, 0:64])],
                                   reads=["bb0", "bb1", "bb4", "C"], writes=[PB(1)])
                        for gl in range(8):
                            g = t4 * 8 + gl
                            ts(tabs[:, vi, g, :], pbank[1][:, 0:128], C[:, C_ROWM + gl:C_ROWM + gl + 1], None, ALU.mult, None, [PB(1), "C", "tabs"], [("tabs", vi, g)])
                cst_ = sb(es, [128, 4, 2, 64], F32, "cstage")
                S.dma_group("sp", [(cst_[:, :, 0, :], P["s5_c_re"][l].rearrange("(t g) c p -> (g c) t p", t=4)),
                                   (cst_[:, :, 1, :], P["s5_c_im"][l].rearrange("(t g) c p -> (g c) t p", t=4))], reads=[], writes=["cst_"])
                for t4 in range(4):
                    S.mm_group([tr(pbank[2][:, 0:128], cst_[:, t4, :, :], identf)], reads=["cst_", "C"], writes=[PB(2)])
                    for gl in range(8):
                        g = t4 * 8 + gl
                        cp("dve", tabs[0:64, 2, g, gl * 16:(gl + 1) * 16], pbank[2][0:64, gl * 16:(gl + 1) * 16], [PB(2), "tabs"], [("tabs", 2, g, 0)])
                        ts(tabs[64:128, 2, g, gl * 16:(gl + 1) * 16], pbank[2][64:128, gl * 16:(gl + 1) * 16], -1.0, None, ALU.mult, None, [PB(2), "tabs"], [("tabs", 2, g, 1)])
                S.barrier()
                S.dma("sp", s5tab[l].rearrange("v p f -> p v f"), tabs[:].rearrange("p v g c -> p v (g c)"), reads=[], writes=["s5tab"])
                ework = [sb(es, [128, 512], F32, "ew") for _ in range(3)]
                ewi = sb(es, [128, 512], I32, "ewi")
                eout = [sb(es, [128, 2, 512], F32, "eout") for _ in range(2)]
                tpos = C[:, C_TPOS:C_TPOS + 512]
                for g in range(32):
                    eo = eout[g % 2]
                    ts(ework[0][:], tpos, cols[:, 3, g:g + 1], None, ALU.mult, None, ["C"], ["ew0"])
                    for which, shift, sgn in ((0, math.pi / 2, 1.0), (1, 0.0, -1.0)):
                        ts(ework[1][:], ework[0][:], shift, 1.0 / (2 * math.pi), ALU.add, ALU.mult, ["ew0"], ["ew1"])
                        cp("dve", ewi[:], ework[1][:], ["ew1"], ["ewi"])
                        cp("dve", ework[2][:], ewi[:], ["ewi"], ["ew2"])
                        ts(ework[1][:], ework[2][:], -2 * math.pi, shift, ALU.mult, ALU.add, ["ew2"], ["ew1"])
                        tt(ework[1][:], ework[1][:], ework[0][:], ALU.add, ["ew1", "ew0"], ["ew1"])
                        ts(ework[1][:], ework[1][:], math.pi, -math.pi, ALU.min, ALU.max, ["ew1"], ["ew1"])
                        act(eo[:, which, :], ework[1][:], AF.Sin, ["ew1"], [("eo", g % 2, which)], scale=sgn)
                    S.dma("pool", etab[l, g].rearrange("w p t -> p w t"), eo[:], reads=[("eo", g % 2, 0), ("eo", g % 2, 1)], writes=[("etab", g % 2)])
                S.barrier()

        def rstd_from(pss_bank, n, out_ap, wr):
            act(out_ap, pbank[pss_bank][:], AF.Ln, [PB(pss_bank)], wr, scale=1.0 / n, bias=EPS)
            act(out_ap, out_ap, AF.Exp, wr, wr, scale=-0.5)

        def phase_p0(b):
            with ExitStack() as es:
                xin = sb(es, [128, 4, D], F32, "xin")
                xo = [sb(es, [128, 512], F32, "xo") for _ in range(4)]
                S.dma_group("sp", [(xin[:, grp, :], x_in[b * TB + grp * 128:b * TB + (grp + 1) * 128, :]) for grp in range(4)], reads=[], writes=["xin"])
                for ft in range(16):
                    bk = ft % 8
                    S.mm_group([tr(pbank[bk][:, grp * 128:(grp + 1) * 128], xin[:, grp, ft * 128:(ft + 1) * 128], identf) for grp in range(4)],
                               reads=["xin", "C"], writes=[PB(bk)])
                    cp("act" if ft % 2 else "dve", xo[ft % 4][:], pbank[bk][:], [PB(bk)], [("xo", ft % 4)])
                    S.dma("pool", xs[0][ft, :, b * TB:(b + 1) * TB], xo[ft % 4][:], reads=[("xo", ft % 4)], writes=[("st_xo", ft % 4)])
                S.barrier()

        def phase_p6(b):
            with ExitStack() as es:
                xf = sb(es, [128, 16, 512], F32, "xf")
                xt_ = [sb(es, [128, D], F32, "xtok") for _ in range(2)]
                S.dma_group("sp", [(xf[:, ft, :], xs[0][ft, :, b * TB:(b + 1) * TB]) for ft in range(16)], reads=[], writes=["xf"])
                for grp in range(4):
                    for q4 in range(4):
                        bk = (grp % 2) * 4 + q4
                        S.mm_group([tr(pbank[bk][:, i * 128:(i + 1) * 128], xf[:, q4 * 4 + i, grp * 128:(grp + 1) * 128], identf) for i in range(4)],
                                   reads=["xf", "C"], writes=[PB(bk)])
                        cp("act" if q4 % 2 else "dve", xt_[grp % 2][:, q4 * 512:(q4 + 1) * 512], pbank[bk][:], [PB(bk)], [("xtok", grp % 2, q4)])
                    S.dma("pool", out_d[b * TB + grp * 128:b * TB + (grp + 1) * 128, :], xt_[grp % 2][:],
                          reads=[("xtok", grp % 2, q4) for q4 in range(4)], writes=[("st_out", grp % 2)])
                S.barrier()

        def norm_to_hT(l, b, src, gcol0):
            with ExitStack() as es:
                xT = sb(es, [128, 16, 512], F32, "xT")
                sq = [sb(es, [128, 512], BF16, "sq") for _ in range(2)]
                rstd = sb(es, [128, 512], F32, "rstd")
                S.dma_group("sp", [(xT[:, ft, :], src[ft, :, b * TB:(b + 1) * TB]) for ft in range(16)], reads=[], writes=["xT"])
                for ft in range(16):
                    act(sq[ft % 2][:], xT[:, ft, :], AF.Square, ["xT"], [("sq", ft % 2)])
                    S.mm_group([mm(pbank[7][:], onesb[:], sq[ft % 2][:], ft == 0, ft == 15)], reads=[("sq", ft % 2), "onesb"], writes=[PB(7)])
                rstd_from(7, D, rstd[:], ["rstd"])
                for ft in range(16):
                    stt(hT[:, ft, :], xT[:, ft, :], colsA[:, l, gcol0 + ft:gcol0 + ft + 1], rstd[:], ALU.mult, ALU.mult,
                        ["xT", "rstd", "colsA"], [("hT", ft)])
                S.barrier()

        HT_ALL = [("hT", ft) for ft in range(16)]

        def proj_fm(bank, ws, col0, ncols=128):
            wv = wbuf[ws][:].rearrange("p (k c) -> p k c", c=512)
            S.mm_group([mm(pbank[bank][0:ncols, :], wv[:, kt, col0:col0 + ncols], hT[:, kt, :], kt == 0, kt == 15) for kt in range(16)],
                       reads=HT_ALL + [("wbuf", ws)], writes=[PB(bank)])

        def proj_tm(bank, ws, col0, ncols, width):
            wv = wbuf[ws][:].rearrange("p (k c) -> p k c", c=512)
            pv = pbank[bank][:].rearrange("p (g c) -> p g c", c=width)
            fns = []
            for grp in range(4):
                for kt in range(16):
                    fns.append(mm(pv[:, grp, 0:ncols], hT[:, kt, grp * 128:(grp + 1) * 128], wv[:, kt, col0:col0 + ncols], kt == 0, kt == 15))
            S.mm_group(fns, reads=HT_ALL + [("wbuf", ws)], writes=[PB(bank)])

        def head_norm_out(l, po_bank, pss_bank, gate, mt, es_tiles, rd_gate):
            sqb, rs, tmp = es_tiles
            act(sqb[:], pbank[po_bank][:], AF.Square, [PB(po_bank)], ["hn_sq"])
            S.mm_group([mm(pbank[pss_bank][:], onesb[:], sqb[:])], reads=["hn_sq", "onesb"], writes=[PB(pss_bank)])
            rstd_from(pss_bank, 128, rs[:], ["hn_rs"])
            tt(tmp[:], pbank[po_bank][:], rs[:], ALU.mult, [PB(po_bank), "hn_rs"], ["hn_tmp"])
            stt(mixT_h[0][:, mt, :], tmp[:], gsc[:, l, mt:mt + 1], gate, ALU.mult, ALU.mult, ["hn_tmp", "gsc"] + rd_gate, [("mixT", mt)])

        def phase_hgrn(l, b):
            with ExitStack() as es:
                F = lambda nm: sb(es, [128, 512], F32, nm)
                Bt = lambda nm: sb(es, [128, 512], BF16, nm)
                qf, t1, lf, kk, gate, b64, b16, e16, e64, tmpd = [F(n) for n in ("qf", "t1", "lf", "kk", "gate", "b64", "b16", "e16", "e64", "tmpd")]
                tk = sb(es, [128, 4, 512], F32, "tk")
                qh, qt, kdec, vt, kdt, sqb = [Bt(n) for n in ("qh", "qt", "kdec", "vt", "kdt", "sqb")]
                khat = sb(es, [128, 8, 4, 64], BF16, "khat")
                AT = sb(es, [128, 4, 128], BF16, "AT")
                Sbf = sb(es, [128, 8, 128], BF16, "Sbf")
                dec = sb(es, [128, 8], F32, "dec")
                rs, tmpo = F("rs"), F("tmpo")
                mset("pool", khat[:], 0.0, ["khat"])
                mset("pool", AT[:], 0.0, ["AT"])
                m64 = C[:, C_M64:C_M64 + 512]
                m16 = C[:, C_M16:C_M16 + 512]
                cmask = C[:, C_CMASK:C_CMASK + 64]
                b64v = b64[:].rearrange("p (c j) -> p c j", j=64)
                kkv = kk[:].rearrange("p (c j) -> p c j", j=64)
                ws_next = load_w(wsi[l, 0, :, :])
                for h in range(6):
                    ws = ws_next
                    if h < 5:
                        ws_next = load_w(wsi[l, h + 1, :, :])
                    proj_fm(0, ws, 0)
                    proj_fm(1, ws, 128)
                    proj_fm(2, ws, 256)
                    proj_tm(3, ws, 384, 128, 128)
                    act(qf[:], pbank[0][:], AF.Silu, [PB(0)], ["qf"])
                    act(t1[:], pbank[1][:], AF.Sigmoid, [PB(1)], ["t1"])
                    act(kk[:], pbank[1][:], AF.Sigmoid, [PB(1)], ["kk"], scale=-1.0)
                    act(gate[:], pbank[2][:], AF.Silu, [PB(2)], ["gate"])
                    cp("act", vt[:], pbank[3][:], [PB(3)], ["vt"])
                    if stage("hg1"):
                        S.barrier()
                        return
                    ts(t1[:], t1[:], omlc[:, l, h:h + 1], lbc[:, l, h:h + 1], ALU.mult, ALU.add, ["t1", "omlc", "lbc"], ["t1"])
                    act(lf[:], t1[:], AF.Ln, ["t1"], ["lf"])
                    ts(kk[:], kk[:], omlc[:, l, h:h + 1], None, ALU.mult, None, ["kk", "omlc"], ["kk"])
                    S.op("dve", lambda: nc.vector.tensor_tensor_scan(out=b64[:], data0=m64, data1=lf[:], initial=0.0, op0=ALU.mult, op1=ALU.add), ["lf", "C"], ["b64"])
                    S.op("dve", lambda: nc.vector.tensor_tensor_scan(out=b16[:], data0=m16, data1=lf[:], initial=0.0, op0=ALU.mult, op1=ALU.add), ["lf", "C"], ["b16"])
                    act(e16[:], b16[:], AF.Exp, ["b16"], ["e16"])
                    act(e64[:], b64[:], AF.Exp, ["b64"], ["e64"])
                    tt(qh[:], qf[:], e16[:], ALU.mult, ["qf", "e16"], ["qh"])
                    tt(qt[:], qf[:], e64[:], ALU.mult, ["qf", "e64"], ["qt"])
                    tt(tmpd[:].rearrange("p (c j) -> p c j", j=64), b64v[:, :, 63:64].to_broadcast([128, 8, 64]), b64v, ALU.subtract, ["b64"], ["tmpd"])
                    act(tmpd[:], tmpd[:], AF.Exp, ["tmpd"], ["tmpd"])
                    tt(kdec[:], kk[:], tmpd[:], ALU.mult, ["kk", "tmpd"], ["kdec"])
                    act(dec[:], b64v[:, :, 63], AF.Exp, ["b64"], ["dec"])
                    if stage("hg2"):
                        S.barrier()
                        return
                    for i in range(4):
                        n = 16 * (i + 1)
                        tkv = tk[:, i, :].rearrange("p (c j) -> p c j", j=64)
                        if i == 0:
                            act(tkv[:, :, 0:n], b64v[:, :, 0:n], AF.Exp, ["b64"], [("tk", i)], scale=-1.0)
                        else:
                            tt(tkv[:, :, 0:n], b64v[:, :, 16 * i - 1:16 * i].to_broadcast([128, 8, n]), b64v[:, :, 0:n], ALU.subtract, ["b64"], [("tk", i)])
                            act(tkv[:, :, 0:n], tkv[:, :, 0:n], AF.Exp, [("tk", i)], [("tk", i)])
                        tt(khat[:, :, i, 0:n], kkv[:, :, 0:n], tkv[:, :, 0:n], ALU.mult, ["kk", ("tk", i)], ["khat"])
                    if stage("hg3"):
                        S.barrier()
                        return
                    fns = []
                    for c in range(8):
                        j2, par = c // 2, c % 2
                        for i in range(4):
                            fns.append(mm(pbank[4][par * 64:(par + 1) * 64, j2 * 64 + 16 * i:j2 * 64 + 16 * i + 16], khat[:, c, i, :],
                                          qh[:, c * 64 + 16 * i:c * 64 + 16 * i + 16]))
                    S.mm_group(fns, reads=["khat", "qh"], writes=[PB(4)])
                    pAv = pbank[4][:, 0:256].rearrange("p (j t) -> p j t", t=64)
                    for par in range(2):
                        sl = slice(par * 64, (par + 1) * 64)
                        tt(AT[sl, :, par * 64:(par + 1) * 64], pAv[sl, :, :], cmask[sl, :].unsqueeze(1).to_broadcast([64, 4, 64]), ALU.mult,
                           [PB(4), "C"], ["AT"])
                    if stage("hg4"):
                        S.barrier()
                        return
                    p7b = pbank[7][:].bitcast(BF16)
                    S.mm_group([tr(p7b[:, grp * 128:(grp + 1) * 128], kdec[:, grp * 128:(grp + 1) * 128], identb[:]) for grp in range(4)],
                               reads=["kdec", "identb"], writes=[PB(7)])
                    cp("act", kdt[:], p7b[:, 0:512], [PB(7)], ["kdt"])
                    if stage("hg5"):
                        S.barrier()
                        return
                    for par in range(2):
                        fns = []
                        sl = slice(par * 64, (par + 1) * 64)
                        for j2 in range(4):
                            fns.append(mm(pbank[5 + par][:, j2 * 128:(j2 + 1) * 128], kdt[sl, j2 * 128:(j2 + 1) * 128], vt[sl, j2 * 128:(j2 + 1) * 128]))
                        S.mm_group(fns, reads=["kdt", "vt"], writes=[PB(5 + par)])
                    if stage("hg6"):
                        S.barrier()
                        return
                    Sreg = ("S_hg", l, h)
                    for c in range(8):
                        cp("act", Sbf[:, c, :], S_hg[:, l, h, :], [Sreg, "S_hg"], [("Sbf", c)])
                        stt(S_hg[:, l, h, :], S_hg[:, l, h, :], dec[:, c:c + 1], pbank[5 + c % 2][:, (c // 2) * 128:(c // 2 + 1) * 128], ALU.mult, ALU.add,
                            [Sreg, "S_hg", "dec", PB(5 + c % 2)], [Sreg])
                    if stage("hg7"):
                        S.barrier()
                        return
                    for j2 in range(4):
                        S.mm_group([mm(pbank[0][:, j2 * 128:(j2 + 1) * 128], vt[:, j2 * 128:(j2 + 1) * 128], AT[:, j2, :], True, False),
                                    mm(pbank[0][:, (2 * j2) * 64:(2 * j2 + 1) * 64], Sbf[:, 2 * j2, :], qt[:, (2 * j2) * 64:(2 * j2 + 1) * 64], False, False),
                                    mm(pbank[0][:, (2 * j2 + 1) * 64:(2 * j2 + 2) * 64], Sbf[:, 2 * j2 + 1, :], qt[:, (2 * j2 + 1) * 64:(2 * j2 + 2) * 64], False, True)],
                                   reads=["vt", "AT", ("Sbf", 2 * j2), ("Sbf", 2 * j2 + 1), "qt"], writes=[PB(0)])
                    if stage("hg8"):
                        S.barrier()
                        return
                    head_norm_out(l, 0, 1, gate[:], h, (sqb, rs, tmpo), ["gate"])
                    if stage("hg9"):
                        S.barrier()
                        return
                S.barrier()

        def phase_ret(l, b):
            with ExitStack() as es:
                F = lambda nm: sb(es, [128, 512], F32, nm)
                Bt = lambda nm: sb(es, [128, 512], BF16, nm)
                tab = sb(es, [128, 4, 512], F32, "rtab")
                ta, tb_ = F("ta"), F("tb")
                qd, kd, kdt, sqb = [Bt(n) for n in ("qd", "kd", "kdt", "sqb")]
                qdz = [Bt("qdz0"), Bt("qdz1")]
                vt = sb(es, [128, 4, 256], BF16, "vtr")
                gates = [F("g0"), F("g1")]
                PT = [sb(es, [128, 4, 128], BF16, "PT%d" % i) for i in range(2)]
                Sbf = sb(es, [128, 8, 128], BF16, "Sbfr")
                rs, tmpo = F("rs"), F("tmpo")
                for t_ in (qdz[0], qdz[1], PT[0], PT[1]):
                    mset("pool", t_[:], 0.0, ["rz"])
                for j in range(3):
                    wa = load_w(wsi[l, 6 + j, :, :])
                    wb = load_w(wsi[l, 9 + j, :, :])
                    S.dma_group("sp", [(tab[:, w, :], rope_d[j, w, :, b * TB:(b + 1) * TB]) for w in range(4)], reads=[], writes=["rtab"])
                    for w in range(4):
                        proj_fm(w, wa, 128 * w)
                    for (z, zp, cq, sq_, dst, nm) in ((0, 1, 0, 1, qd, "qd"), (2, 3, 2, 3, kd, "kd")):
                        tt(ta[:], pbank[z][:], tab[:, cq, :], ALU.mult, [PB(z), "rtab"], ["ta"])
                        tt(tb_[:], pbank[zp][:], tab[:, sq_, :], ALU.mult, [PB(zp), "rtab"], ["tb"])
                        tt(dst[:], ta[:], tb_[:], ALU.add, ["ta", "tb"], [nm])
                        if nm == "qd":
                            for hh in range(2):
                                sl = slice(64 * hh, 64 * hh + 64)
                                tt(qdz[hh][sl, :], ta[sl, :], tb_[sl, :], ALU.add, ["ta", "tb", "rz"], [("qdz", hh)])
                    wv = wbuf[wb][:].rearrange("p (k c) -> p k c", c=512)
                    fns = []
                    for grp in range(4):
                        pvv = pbank[4 + grp // 2][:, (grp % 2) * 256:(grp % 2 + 1) * 256]
                        for kt in range(16):
                            fns.append(mm(pvv, hT[:, kt, grp * 128:(grp + 1) * 128], wv[:, kt, 0:256], kt == 0, kt == 15))
                    S.mm_group(fns, reads=HT_ALL + [("wbuf", wb)], writes=[PB(4), PB(5)])
                    for half in range(2):
                        cp("act", vt[:, 2 * half:2 * half + 2, :], pbank[4 + half][:].rearrange("p (g c) -> p g c", c=256), [PB(4), PB(5)], [("vtr", half)])
                    for hh in range(2):
                        proj_fm(6 + hh, wb, 256 + 128 * hh)
                        act(gates[hh][:], pbank[6 + hh][:], AF.Silu, [PB(6 + hh)], [("gate", hh)])
                    VT = [("vtr", 0), ("vtr", 1)]
                    p3b = pbank[3][:].bitcast(BF16)
                    S.mm_group([tr(p3b[:, grp * 128:(grp + 1) * 128], kd[:, grp * 128:(grp + 1) * 128], identb[:]) for grp in range(4)],
                               reads=["kd", "identb"], writes=[PB(3)])
                    cp("act", kdt[:], p3b[:, 0:512], [PB(3)], ["kdt"])
                    for par in range(2):
                        fns = []
                        sl = slice(par * 64, (par + 1) * 64)
                        for j2 in range(4):
                            for hh in range(2):
                                fns.append(mm(pbank[1 + par][64 * hh:64 * hh + 64, j2 * 128:(j2 + 1) * 128],
                                              kdt[sl, j2 * 128 + 64 * hh:j2 * 128 + 64 * hh + 64], vt[sl, j2, 128 * hh:128 * hh + 128]))
                        S.mm_group(fns, reads=["kdt"] + VT, writes=[PB(1 + par)])
                    Sreg = ("S_rt", l, j)
                    for c in range(8):
                        cp("act", Sbf[:, c, :], S_rt[:, l, j, :], [Sreg, "S_rt"], [("Sbf", c)])
                        stt(S_rt[:, l, j, :], S_rt[:, l, j, :], C[:, C_GDEC + j:C_GDEC + j + 1], pbank[1 + c % 2][:, (c // 2) * 128:(c // 2 + 1) * 128],
                            ALU.mult, ALU.add, [Sreg, "S_rt", "C", PB(1 + c % 2)], [Sreg])
                    for hh in range(2):
                        h = 2 * j + hh
                        fns = []
                        for c in range(8):
                            j2, par = c // 2, c % 2
                            fns.append(mm(pbank[0][par * 64:(par + 1) * 64, j2 * 64:(j2 + 1) * 64], kd[:, c * 64:(c + 1) * 64], qdz[hh][:, c * 64:(c + 1) * 64]))
                        S.mm_group(fns, reads=["kd", ("qdz", hh), "rz"], writes=[PB(0)])
                        pSv = pbank[0][:, 0:256].rearrange("p (j t) -> p j t", t=64)
                        rm = C[:, C_RMASK + 64 * h:C_RMASK + 64 * (h + 1)]
                        for par in range(2):
                            sl = slice(par * 64, (par + 1) * 64)
                            tt(PT[hh][sl, :, par * 64:(par + 1) * 64], pSv[sl, :, :], rm[sl, :].unsqueeze(1).to_broadcast([64, 4, 64]), ALU.mult,
                               [PB(0), "C", "rz"], [("PT", hh)])
                        ob = 6 + hh
                        for j2 in range(4):
                            S.mm_group([mm(pbank[ob][:, j2 * 128:(j2 + 1) * 128], vt[:, j2, 128 * hh:128 * hh + 128], PT[hh][:, j2, :], True, False),
                                        mm(pbank[ob][:, (2 * j2) * 64:(2 * j2 + 1) * 64], Sbf[:, 2 * j2, :], qdz[hh][:, (2 * j2) * 64:(2 * j2 + 1) * 64], False, False),
                                        mm(pbank[ob][:, (2 * j2 + 1) * 64:(2 * j2 + 2) * 64], Sbf[:, 2 * j2 + 1, :], qdz[hh][:, (2 * j2 + 1) * 64:(2 * j2 + 2) * 64], False, True)],
                                       reads=VT + [("PT", hh), ("Sbf", 2 * j2), ("Sbf", 2 * j2 + 1), ("qdz", hh), ("gate", hh)], writes=[PB(ob)])
                        head_norm_out(l, ob, 4 + hh, gates[hh][:], 10 + h, (sqb, rs, tmpo), [("gate", hh)])
                S.barrier()

        def phase_s5(l, b):
            with ExitStack() as es:
                F = lambda nm: sb(es, [128, 512], F32, nm)
                tabs = sb(es, [128, 3, 32, 128], BF16, "s5t")
                wg = sb(es, [128, 4, 512], BF16, "wglu")
                uf = sb(es, [128, 4, 512], F32, "uf")
                ub = sb(es, [128, 4, 512], BF16, "ub")
                et = [sb(es, [128, 2, 512], F32, "et") for _ in range(2)]
                t1, t2, t3, t4, xa, xb_, Wa, Wb = [F(n) for n in ("t1", "t2", "t3", "t4", "xa", "xb", "Wa", "Wb")]
                Sg = [sb(es, [128, 512], BF16, "Sg") for _ in range(2)]
                ya = F("ya")
                yg = sb(es, [128, 4, 512], F32, "yg")
                ygb = sb(es, [128, 4, 512], BF16, "ygb")
                sg_ = F("sg")
                S.dma("sp", tabs[:].rearrange("p v g c -> p v (g c)"), s5tab[l].rearrange("v p f -> p v f"), reads=["s5tab"], writes=["s5t"])
                S.dma("sp", wg[:].rearrange("p k c -> p (k c)"), wsg[l, :, :], reads=[], writes=["wglu"])
                ws = load_w(wsi[l, 12, :, :])
                for t in range(4):
                    proj_fm(t, ws, 128 * t)
                    cp("act", uf[:, t, :], pbank[t][:], [PB(t)], [("uf", t)])
                    cp("dve", ub[:, t, :], uf[:, t, :], [("uf", t)], [("ub", t)])
                for g in range(32):
                    t, gl = g // 8, g % 8
                    e = et[g % 2]
                    S.dma("sp", e[:], etab[l, g].rearrange("w p t -> p w t"), reads=[], writes=[("et", g % 2)])
                    pX, pXp = 4 + 2 * (g % 2), 5 + 2 * (g % 2)
                    S.mm_group([mm(pbank[pX][:], tabs[:, 0, g, :], ub[:, t, :])], reads=["s5t", ("ub", t)], writes=[PB(pX)])
                    S.mm_group([mm(pbank[pXp][:], tabs[:, 1, g, :], ub[:, t, :])], reads=["s5t", ("ub", t)], writes=[PB(pXp)])
                    ER = [("et", g % 2)]
                    Ec, Es = e[:, 0, :], e[:, 1, :]
                    tt(t1[:], pbank[pX][:], Ec, ALU.mult, [PB(pX)] + ER, ["t1"])
                    tt(t2[:], pbank[pXp][:], Es, ALU.mult, [PB(pXp)] + ER, ["t2"])
                    tt(xa[:], t1[:], t2[:], ALU.add, ["t1", "t2"], ["xa"])
                    tt(t3[:], pbank[pXp][:], Ec, ALU.mult, [PB(pXp)] + ER, ["t3"])
                    tt(t4[:], pbank[pX][:], Es, ALU.mult, [PB(pX)] + ER, ["t4"])
                    tt(xb_[:], t3[:], t4[:], ALU.subtract, ["t3", "t4"], ["xb"])
                    rh = rho[:, l, g:g + 1].to_broadcast([128, 512])
                    S.op("dve", lambda: nc.vector.tensor_tensor_scan(out=Wa[:], data0=rh, data1=xa[:], initial=Wst[:, l, 0, g:g + 1], op0=ALU.mult, op1=ALU.add),
                         ["xa", "rho", ("Wst", l, g)], ["Wa"])
                    S.op("dve", lambda: nc.vector.tensor_tensor_scan(out=Wb[:], data0=rh, data1=xb_[:], initial=Wst[:, l, 1, g:g + 1], op0=ALU.mult, op1=ALU.add),
                         ["xb", "rho", ("Wst", l, g)], ["Wb"])
                    tt(t1[:], Wa[:], Ec, ALU.mult, ["Wa"] + ER, ["t1"])
                    tt(t2[:], Wb[:], Es, ALU.mult, ["Wb"] + ER, ["t2"])
                    tt(Sg[g % 2][:], t1[:], t2[:], ALU.subtract, ["t1", "t2"], [("Sg", g % 2)])
                    tt(Wst[:, l, 0, g:g + 1], t1[:, 511:512], t2[:, 511:512], ALU.subtract, ["t1", "t2"], [("Wst", l, g)])
                    tt(t3[:, 0:1], Wb[:, 511:512], e[:, 0, 511:512], ALU.mult, ["Wb"] + ER, ["t3"])
                    tt(t4[:, 0:1], Wa[:, 511:512], e[:, 1, 511:512], ALU.mult, ["Wa"] + ER, ["t4"])
                    tt(Wst[:, l, 1, g:g + 1], t3[:, 0:1], t4[:, 0:1], ALU.add, ["t3", "t4", ("Wst", l, g)], [("Wst", l, g)])
                    S.mm_group([mm(pbank[t][:], tabs[:, 2, g, :], Sg[g % 2][:], gl == 0, gl == 7)], reads=["s5t", ("Sg", g % 2)], writes=[PB(t)])
                    if gl == 7:
                        stt(ya[:], uf[:, t, :], colsA[:, l, A_S5D + t:A_S5D + t + 1], pbank[t][:], ALU.mult, ALU.add, [("uf", t), "colsA", PB(t)], ["ya"])
                        act(yg[:, t, :], ya[:], AF.Gelu, ["ya"], [("yg", t)])
                        cp("dve", ygb[:, t, :], yg[:, t, :], [("yg", t)], [("ygb", t)])
                for ot in range(4):
                    bk = 4 + ot % 2
                    S.mm_group([mm(pbank[bk][:], wg[:, kt, ot * 128:(ot + 1) * 128], ygb[:, kt, :], kt == 0, kt == 3) for kt in range(4)],
                               reads=["wglu"] + [("ygb", k_) for k_ in range(4)], writes=[PB(bk)])
                    act(sg_[:], pbank[bk][:], AF.Sigmoid, [PB(bk), "colsA"], ["sg"], bias=colsA[:, l, A_BGLU + ot:A_BGLU + ot + 1])
                    stt(mixT_h[0][:, 6 + ot, :], yg[:, ot, :], gsc[:, l, 6 + ot:7 + ot], sg_[:], ALU.mult, ALU.mult, [("yg", ot), "gsc", "sg"], [("mixT", 6 + ot)])
                S.barrier()

        def post_norm_residual(l, b, gcol0, xsrc, srcid, dstfn, es, yb):
            rstd = sb(es, [128, 512], F32, "rstd2")
            xr = [sb(es, [128, 512], F32, "xr") for _ in range(2)]
            tmp = [sb(es, [128, 512], F32, "pn_tmp") for _ in range(2)]
            rstd_from(7, D, rstd[:], ["rstd2"])
            for ot in range(16):
                s = ot % 2
                S.dma("sp", xr[s][:], xsrc[ot, :, b * TB:(b + 1) * TB], reads=[], writes=[("xr", s)])
                stt(tmp[s][:], yb[:, ot, :], colsA[:, l, gcol0 + ot:gcol0 + ot + 1], rstd[:], ALU.mult, ALU.mult, [("yb", ot), "colsA", "rstd2"], [("pn_tmp", s)])
                tt(tmp[s][:], tmp[s][:], xr[s][:], ALU.add, [("pn_tmp", s), ("xr", s)], [("pn_tmp", s)], eng="pool")
                dstfn(ot, tmp[s][:], [("pn_tmp", s)], s)

        def evac_y(bank, ot, yb, sq):
            cp("act", yb[:, ot, :], pbank[bank][:], [PB(bank)], [("yb", ot)])
            act(sq[ot % 2][:], pbank[bank][:], AF.Square, [PB(bank)], [("sq2", ot % 2)])
            S.mm_group([mm(pbank[7][:], onesb[:], sq[ot % 2][:], ot == 0, ot == 15)], reads=[("sq2", ot % 2), "onesb"], writes=[PB(7)])

        def phase_outproj(l, b, xsrc, srcid, xdst, dstid):
            with ExitStack() as es:
                yb = sb(es, [128, 16, 512], F32, "yb")
                sq = [sb(es, [128, 512], BF16, "sq2") for _ in range(2)]
                MIX_ALL = [("mixT", k_) for k_ in range(16)]
                ws_next = load_w(wso[l, 0, :, :])
                for oc in range(4):
                    ws = ws_next
                    if oc < 3:
                        ws_next = load_w(wso[l, oc + 1, :, :])
                    wv = wbuf[ws][:].rearrange("p (k c) -> p k c", c=512)
                    for oi in range(4):
                        ot = oc * 4 + oi
                        bk = ot % 4
                        S.mm_group([mm(pbank[bk][:], wv[:, kt, oi * 128:(oi + 1) * 128], mixT_h[0][:, kt, :], kt == 0, kt == 15) for kt in range(16)],
                                   reads=MIX_ALL + [("wbuf", ws)], writes=[PB(bk)])
                        evac_y(bk, ot, yb, sq)

                def dst(ot, ap, rd, s):
                    S.dma("pool", xdst[ot, :, b * TB:(b + 1) * TB], ap, reads=rd, writes=[("st_pn", s)])
                post_norm_residual(l, b, A_GPOST, xsrc, srcid, dst, es, yb)
                S.barrier()

        def phase_ffn(l, b, xsrc, srcid, xdst, dstid):
            norm_to_hT(l, b, xsrc, A_GPREF)
            with ExitStack() as es:
                actT = sb(es, [128, 44, 512], BF16, "actT")
                es_up = ExitStack()
                U = [sb(es_up, [128, 514], F32, "U") for _ in range(4)]
                acc = [sb(es_up, [128, 512], F32, "acc") for _ in range(4)]
                sgt = [sb(es_up, [128, 512], F32, "sgt") for _ in range(2)]
                cw = colsB[:, l, :]
                ws_next = load_w(wsu[l, 0, :, :])
                uc = 0
                for cu in range(22):
                    ws = ws_next
                    if cu < 21:
                        ws_next = load_w(wsu[l, cu + 1, :, :])
                    for i in range(2):
                        ffi = 2 * cu + i
                        accs = []
                        for kind in range(2):
                            tg = ffi + 44 * kind
                            bk = (2 * ffi + kind) % 6
                            proj_fm(bk, ws, 256 * kind + 128 * i)
                            u = U[uc % 4]
                            a = acc[uc % 4]
                            ur, ar_ = ("U", uc % 4), ("acc", uc % 4)
                            uc += 1
                            cp("act", u[:, 2:514], pbank[bk][:], [PB(bk)], [ur])
                            cp("pool", u[:, 0:2], halo[:, l, tg, :], [("halo", l, tg), "halo"], [(ur, "h")])
                            act(a[:], pbank[bk][:], AF.Identity, [PB(bk), "colsB"], [ar_], scale=cw[:, 2 * 88 + tg:2 * 88 + tg + 1], bias=cw[:, 264 + tg:264 + tg + 1])
                            stt(a[:], u[:, 1:513], cw[:, 88 + tg:88 + tg + 1], a[:], ALU.mult, ALU.add, [ur, (ur, "h"), ar_, "colsB"], [ar_])
                            stt(a[:], u[:, 0:512], cw[:, tg:tg + 1], a[:], ALU.mult, ALU.add, [ur, (ur, "h"), ar_, "colsB"], [ar_])
                            cp("pool", halo[:, l, tg, :], u[:, 512:514], [ur, "halo"], [("halo", l, tg)])
                            accs.append((a, ar_))
                        sg = sgt[ffi % 2]
                        act(sg[:], accs[0][0][:], AF.Silu, [accs[0][1]], [("sgt", ffi % 2)])
                        tt(actT[:, ffi, :], sg[:], accs[1][0][:], ALU.mult, [("sgt", ffi % 2), accs[1][1]], [("actT", ffi)], eng="pool")
                S.barrier()
                es_up.close()
                yb = sb(es, [128, 16, 512], F32, "ybf")
                sq = [sb(es, [128, 512], BF16, "sq2f") for _ in range(2)]
                ACT_ALL = [("actT", k_) for k_ in range(44)]
                ws_next = load_w(wsd[l, 0, :, :], 44 * 128)
                for ot in range(16):
                    ws = ws_next
                    if ot < 15:
                        ws_next = load_w(wsd[l, ot + 1, :, :], 44 * 128)
                    bk = ot % 4
                    S.mm_group([mm(pbank[bk][:], wbuf[ws][:, kt * 128:(kt + 1) * 128], actT[:, kt, :], kt == 0, kt == 43) for kt in range(44)],
                               reads=ACT_ALL + [("wbuf", ws)], writes=[PB(bk)])
                    evac_y(bk, ot, yb, sq)

                def dst(ot, ap, rd, s):
                    S.dma("pool", xdst[ot, :, b * TB:(b + 1) * TB], ap, reads=rd, writes=[("st_pn", s)])
                post_norm_residual(l, b, A_GPOSTF, xsrc, srcid, dst, es, yb)
                S.barrier()

        def xs_key(t):
            return id(t)

        def main_schedule():
            if stage("s5tab"):
                return
            for b in range(NB):
                phase_p0(b)
            if stage("p0"):
                return
            for l in range(depth):
                for b in range(NB):
                    with ExitStack() as mes:
                        mixT_h[0] = sb(mes, [128, 16, 512], BF16, "mixT")
                        norm_to_hT(l, b, xs[0], A_GPRE)
                        if stage("norm"):
                            return
                        phase_hgrn(l, b)
                        if stage("hgrn"):
                            return
                        phase_ret(l, b)
                        if stage("ret"):
                            return
                        phase_s5(l, b)
                        if stage("s5"):
                            return
                        if dbg and l == 0:
                            S.dma("pool", dbg_out["mix0"][:, :, b * TB:(b + 1) * TB].rearrange("k p t -> p k t"), mixT_h[0][:], reads=[], writes=["dbgmix"])
                            S.dma("pool", dbg_out["h0"][:, :, b * TB:(b + 1) * TB].rearrange("k p t -> p k t"), hT[:], reads=[], writes=["dbgh"])
                            S.barrier()
                        phase_outproj(l, b, xs[0], 0, xs[1], 1)
                        if dbg and l == 0:
                            S.dma("sp", dbg_out["xmid0"][:, :, b * TB:(b + 1) * TB], xs[1][:, :, b * TB:(b + 1) * TB], reads=[], writes=["dbgxm"])
                            S.barrier()
                    if stage("outproj"):
                        return
                    phase_ffn(l, b, xs[1], 1, xs[0], 0)
                    if stage("ffn"):
                        return
                    if dbg and l == 0:
                        S.dma("sp", dbg_out["xout0"][:, :, b * TB:(b + 1) * TB], xs[0][:, :, b * TB:(b + 1) * TB], reads=[], writes=["dbgxo"])
                        S.barrier()
            for b in range(NB):
                phase_p6(b)

        if not stopped[0]:
            main_schedule()
    except _Stop:
        pass
    S.barrier()
    top.close()
    return nc, S


_CACHE = {}


def kernel(**inputs):
    x = np.ascontiguousarray(inputs["x"], dtype=np.float32)
    B, T, _ = x.shape
    n_cores = B
    key = (T,)
    if key not in _CACHE:
        _CACHE[key] = build_program(T)
    nc, _ = _CACHE[key]
    cst = _build_consts()
    rope = _build_rope(0, T)
    in_maps = []
    for c in range(n_cores):
        m = {"x": x[c], "cst": cst, "rope": rope}
        for n in PARAM_NAMES:
            m[n] = np.ascontiguousarray(inputs[n], dtype=np.float32)
        in_maps.append(m)
    res = run_bass_kernel_spmd(nc, in_maps, core_ids=list(range(n_cores)))
    out = np.stack([np.asarray(res.results[c]["out"]) for c in range(n_cores)], axis=0)
    return out.astype(np.float32)
```

```python
import math
from contextlib import ExitStack

import numpy as np
import concourse.bass as bass
import concourse.mybir as mybir
from concourse.bass_utils import run_bass_kernel_spmd

F32 = mybir.dt.float32
BF16 = mybir.dt.bfloat16
I32 = mybir.dt.int32
AF = mybir.ActivationFunctionType
ALU = mybir.AluOpType

D = 2048
NIN = 5888
DFF = 5632
TB = 512
EPS = 1e-6
NCST = 2304
C_M64, C_M16, C_TPOS, C_CMASK, C_RMASK, C_ROWM, C_GDEC, C_IDENT = 0, 512, 1024, 1536, 1600, 1984, 1992, 2048


class Sched:
    ENG = ("pe", "act", "dve", "pool", "sp")

    def __init__(self, nc):
        self.nc = nc
        self.e = {"pe": nc.tensor, "act": nc.scalar, "dve": nc.vector, "pool": nc.gpsimd, "sp": nc.sync}
        self.sem = {}
        self.cnt = {}
        for k in self.ENG:
            self.sem[k] = nc.alloc_semaphore("sem_" + k)
            self.cnt[k] = 0
        self.seen = {k: {} for k in self.ENG}
        self.dsem = {}
        self.st = {}
        self.ninst = 0

    def _semof(self, key):
        if isinstance(key, str):
            return self.sem[key]
        return self.dsem[key[1]][0]

    def _wait(self, eng, tok):
        if tok is None:
            return
        key, val = tok
        if self.seen[eng].get(key, 0) >= val:
            return
        if key == eng and eng == "pe":
            return
        self.e[eng].wait_ge(self._semof(key), val)
        self.seen[eng][key] = val

    def _deps(self, eng, reads, writes):
        for r in reads:
            s = self.st.get(r)
            if s is not None:
                self._wait(eng, s["w"])
                if isinstance(r, tuple) and r[0] == "pb":
                    for t in s["r"]:
                        if t[0] != eng:
                            self._wait(eng, t)
        for w in writes:
            s = self.st.get(w)
            if s is not None:
                self._wait(eng, s["w"])
                for t in s["r"]:
                    self._wait(eng, t)

    def _commit(self, tok, reads, writes):
        for r in reads:
            s = self.st.setdefault(r, {"w": None, "r": []})
            s["r"] = [t for t in s["r"] if t[0] != tok[0]] + [tok]
        for w in writes:
            self.st[w] = {"w": tok, "r": []}

    def op(self, eng, fn, reads=(), writes=()):
        self._deps(eng, reads, writes)
        ins = fn()
        self.cnt[eng] += 1
        ins.then_inc(self.sem[eng], 1)
        self._commit((eng, self.cnt[eng]), reads, writes)
        self.ninst += 1
        return ins

    def mm_group(self, fns, reads=(), writes=()):
        self._deps("pe", reads, writes)
        ins = None
        for fn in fns:
            ins = fn()
        self.cnt["pe"] += 1
        ins.then_inc(self.sem["pe"], 1)
        self._commit(("pe", self.cnt["pe"]), reads, writes)
        self.ninst += len(fns)
        return ins

    def dma(self, q, out, in_, reads=(), writes=(), semreg=None, **kw):
        reg = semreg if semreg is not None else writes[0]
        if reg not in self.dsem:
            self.dsem[reg] = [self.nc.alloc_semaphore("dsem%d" % len(self.dsem)), 0]
        self._deps(q, reads, writes)
        ins = self.e[q].dma_start(out=out, in_=in_, **kw)
        self.dsem[reg][1] += 16
        ins.then_inc(self.dsem[reg][0], 16)
        self._commit((("d", reg), self.dsem[reg][1]), reads, writes)
        self.ninst += 1
        return ins

    def dma_group(self, q, pairs, reads=(), writes=(), semreg=None, **kw):
        reg = semreg if semreg is not None else writes[0]
        if reg not in self.dsem:
            self.dsem[reg] = [self.nc.alloc_semaphore("dsem%d" % len(self.dsem)), 0]
        self._deps(q, reads, writes)
        for (out, in_) in pairs:
            ins = self.e[q].dma_start(out=out, in_=in_, **kw)
            self.dsem[reg][1] += 16
            ins.then_inc(self.dsem[reg][0], 16)
            self.ninst += 1
        self._commit((("d", reg), self.dsem[reg][1]), reads, writes)

    def barrier(self):
        for e in self.ENG:
            for f in self.ENG:
                if f != e and self.cnt[f] > 0:
                    self._wait(e, (f, self.cnt[f]))
            for reg, (sem, tot) in self.dsem.items():
                if tot > 0:
                    self._wait(e, (("d", reg), tot))
        self.st = {}

    def finish(self, regions, eng="sp"):
        for r in regions:
            s = self.st.get(r)
            if s is not None:
                self._wait(eng, s["w"])


def _build_consts():
    c = np.zeros((128, NCST), np.float32)
    t = np.arange(512)
    c[:, C_M64:C_M64 + 512] = (t % 64 != 0).astype(np.float32)[None, :]
    c[:, C_M16:C_M16 + 512] = (t % 16 != 0).astype(np.float32)[None, :]
    c[:, C_TPOS:C_TPOS + 512] = (t + 1).astype(np.float32)[None, :]
    s = np.arange(128) % 64
    tt = np.arange(64)
    causal = (s[:, None] <= tt[None, :]).astype(np.float64)
    c[:, C_CMASK:C_CMASK + 64] = causal
    for h in range(6):
        lg = np.float32(np.log(np.float32(1.0) - np.float32(2.0) ** np.float32(-5.0 - h)))
        c[:, C_RMASK + 64 * h:C_RMASK + 64 * (h + 1)] = causal * np.exp(-64.0 * float(lg))
    r = np.arange(128)
    for gl in range(8):
        c[:, C_ROWM + gl] = (r // 16 == gl)
    for j in range(3):
        for hh in range(2):
            h = 2 * j + hh
            lg = np.float32(np.log(np.float32(1.0) - np.float32(2.0) ** np.float32(-5.0 - h)))
            c[64 * hh:64 * (hh + 1), C_GDEC + j] = np.exp(64.0 * float(lg))
    c[:, C_IDENT:C_IDENT + 128] = np.eye(128)
    return c


def _build_rope(pos0, nt):
    pos = (pos0 + np.arange(nt)).astype(np.float32)
    inv_freq = (np.float32(1.0) / (np.float32(10000.0) ** np.linspace(0.0, 1.0, 32, dtype=np.float32))).astype(np.float32)
    ang = (pos[None, :] * inv_freq[:, None]).astype(np.float32).astype(np.float64)
    cos, sin = np.cos(ang), np.sin(ang)
    tc = (np.arange(nt) % 64).astype(np.float64)
    out = np.zeros((3, 4, 128, nt), np.float32)
    for j in range(3):
        for hh in range(2):
            h = 2 * j + hh
            lg = float(np.float32(np.log(np.float32(1.0) - np.float32(2.0) ** np.float32(-5.0 - h))))
            gq = np.exp((tc + 1.0) * lg)[None, :]
            gk = np.exp((63.0 - tc) * lg)[None, :] * (64.0 ** -0.5)
            r0 = 64 * hh
            cc = np.concatenate([cos, cos], 0)
            ss = np.concatenate([-sin, sin], 0)
            out[j, 0, r0:r0 + 64] = cc * gq
            out[j, 1, r0:r0 + 64] = ss * gq
            out[j, 2, r0:r0 + 64] = cc * gk
            out[j, 3, r0:r0 + 64] = ss * gk
    return out


def _inproj_chunks():
    ch = []
    for h in range(6):
        ch.append([(h * 128, 128), (768 + h * 128, 128), (2304 + h * 128, 128), (1536 + h * 128, 128)])
    for j in range(3):
        segs = []
        for base in (3584, 3968):
            segs.append((base + 128 * j, 128))
            for hh in range(2):
                h = 2 * j + hh
                segs.append((base + h * 64 + 32, 32))
                segs.append((base + h * 64, 32))
        ch.append(segs)
    for j in range(3):
        ch.append([(4352 + 256 * j, 256), (5120 + 256 * j, 256)])
    ch.append([(3072, 512)])
    return ch


PARAM_NAMES = ["w_in", "w_out", "mix_beta", "hg_lb_logits", "hg_onorm", "s5_lam_re", "s5_lam_im", "s5_log_dt",
               "s5_b_re", "s5_b_im", "s5_c_re", "s5_c_im", "s5_d", "s5_w_glu", "s5_b_glu",
               "ffn_w_up", "ffn_conv_w", "ffn_conv_b", "ffn_w_down",
               "g_pre_mix", "g_post_mix", "g_pre_ffn", "g_post_ffn"]
PARAM_SHAPES = {
    "w_in": [2, 2048, 5888], "w_out": [2, 2048, 2048], "mix_beta": [2, 2048], "hg_lb_logits": [2, 768],
    "hg_onorm": [2, 128], "s5_lam_re": [2, 32, 64], "s5_lam_im": [2, 32, 64], "s5_log_dt": [2, 32],
    "s5_b_re": [2, 32, 64, 16], "s5_b_im": [2, 32, 64, 16], "s5_c_re": [2, 32, 16, 64], "s5_c_im": [2, 32, 16, 64],
    "s5_d": [2, 512], "s5_w_glu": [2, 512, 512], "s5_b_glu": [2, 512], "ffn_w_up": [2, 2048, 11264],
    "ffn_conv_w": [2, 3, 11264], "ffn_conv_b": [2, 11264], "ffn_w_down": [2, 5632, 2048],
    "g_pre_mix": [2, 2048], "g_post_mix": [2, 2048], "g_pre_ffn": [2, 2048], "g_post_ffn": [2, 2048],
}
A_GPRE, A_GPOST, A_GPREF, A_GPOSTF, A_BETA, A_LB0, A_LB1, A_ONORM, A_S5D, A_BGLU, A_N = 0, 16, 32, 48, 64, 80, 86, 92, 93, 97, 101


class _Stop(Exception):
    pass


def build_program(NT, depth=2, dbg=None, stop=None):
    nc = bass.Bass("TRN2", target_bir_lowering=False)
    S = Sched(nc)
    NB = NT // TB
    uid = [0]

    def din(name, shape, dt=F32):
        return nc.dram_tensor(name, list(shape), dt, kind="ExternalInput").ap()

    def dscr(name, shape, dt):
        return nc.dram_tensor(name, list(shape), dt).ap()

    x_in = din("x", [NT, D])
    P = {n: din(n, PARAM_SHAPES[n]) for n in PARAM_NAMES}
    cst_d = din("cst", [128, NCST])
    rope_d = din("rope", [3, 4, 128, NT])
    out_d = nc.dram_tensor("out", [NT, D], F32, kind="ExternalOutput").ap()

    wsi = dscr("wsi", [2, 13, 128, 8192], BF16)
    wso = dscr("wso", [2, 4, 128, 8192], BF16)
    wsu = dscr("wsu", [2, 22, 128, 8192], BF16)
    wsd = dscr("wsd", [2, 16, 128, 44 * 128], BF16)
    wsg = dscr("wsg", [2, 128, 2048], BF16)
    s5tab = dscr("s5tab", [2, 3, 128, 4096], BF16)
    etab = dscr("etab", [2, 32, 2, 128, 512], F32)
    xs = [dscr("xs0", [16, 128, NT], F32), dscr("xs1", [16, 128, NT], F32)]

    dbg_out = {}
    if dbg:
        for nm in ("mix0", "xmid0", "xout0", "h0"):
            dbg_out[nm] = nc.dram_tensor("dbg_" + nm, [16, 128, NT], F32, kind="ExternalOutput").ap()

    stopped = [False]

    def stage(name):
        if stop is not None and stop == name:
            stopped[0] = True
        return stopped[0]

    def sb(es, shape, dt, name="t"):
        uid[0] += 1
        return es.enter_context(nc.sbuf_tensor("%s_%d" % (name, uid[0]), list(shape), dt)).ap()

    top = ExitStack()
    pbank = [nc.alloc_psum_tensor("pb%d" % i, [128, 512], F32).ap() for i in range(8)]

    def PB(i):
        return ("pb", i)

    C = sb(top, [128, NCST], F32, "cst")
    identf = C[:, C_IDENT:C_IDENT + 128]
    identb = sb(top, [128, 128], BF16, "identb")
    onesb = sb(top, [128, 128], BF16, "onesb")
    colsA = sb(top, [128, 2, A_N], F32, "colsA")
    colsB = sb(top, [128, 2, 352], F32, "colsB")
    lbc = sb(top, [128, 2, 6], F32, "lbc")
    omlc = sb(top, [128, 2, 6], F32, "omlc")
    gsc = sb(top, [128, 2, 16], F32, "gsc")
    S_hg = sb(top, [128, 2, 6, 128], F32, "S_hg")
    S_rt = sb(top, [128, 2, 3, 128], F32, "S_rt")
    Wst = sb(top, [128, 2, 2, 32], F32, "Wst")
    halo = sb(top, [128, 2, 88, 2], F32, "halo")
    rho = sb(top, [128, 2, 32], F32, "rho")
    hT = sb(top, [128, 16, 512], BF16, "hT")
    mixT_h = [None]
    wbuf = [sb(top, [128, 8192], BF16, "wbuf%d" % i) for i in range(2)]
    wslot = [0]

    def load_w(src2d, ncols=8192):
        s = wslot[0] % 2
        wslot[0] += 1
        S.dma("sp", wbuf[s][:, 0:ncols], src2d, reads=[], writes=[("wbuf", s)])
        return s

    def act(out, in_, func, reads, writes, scale=1.0, bias=None):
        if bias is None:
            return S.op("act", lambda: nc.scalar.activation(out=out, in_=in_, func=func, scale=scale), reads, writes)
        return S.op("act", lambda: nc.scalar.activation(out=out, in_=in_, func=func, scale=scale, bias=bias), reads, writes)

    def tt(out, in0, in1, op, reads, writes, eng="dve"):
        e = nc.vector if eng == "dve" else nc.gpsimd
        return S.op(eng, lambda: e.tensor_tensor(out=out, in0=in0, in1=in1, op=op), reads, writes)

    def ts(out, in0, s1, s2, op0, op1, reads, writes, eng="dve"):
        e = nc.vector if eng == "dve" else nc.gpsimd
        if op1 is None:
            return S.op(eng, lambda: e.tensor_scalar(out=out, in0=in0, scalar1=s1, scalar2=None, op0=op0), reads, writes)
        return S.op(eng, lambda: e.tensor_scalar(out=out, in0=in0, scalar1=s1, scalar2=s2, op0=op0, op1=op1), reads, writes)

    def stt(out, in0, scalar, in1, op0, op1, reads, writes):
        return S.op("dve", lambda: nc.vector.scalar_tensor_tensor(out=out, in0=in0, scalar=scalar, in1=in1, op0=op0, op1=op1), reads, writes)

    def cp(eng, out, in_, reads, writes):
        if eng == "act":
            return S.op("act", lambda: nc.scalar.copy(out=out, in_=in_), reads, writes)
        e = nc.vector if eng == "dve" else nc.gpsimd
        return S.op(eng, lambda: e.tensor_copy(out=out, in_=in_), reads, writes)

    def mset(eng, ap, val, writes):
        e = nc.vector if eng == "dve" else nc.gpsimd
        return S.op(eng, lambda: e.memset(ap, val), (), writes)

    def mm(out, lhsT, rhs, start=True, stop=True):
        return lambda: nc.tensor.matmul(out, lhsT=lhsT, rhs=rhs, start=start, stop=stop)

    def tr(out, in_, ident):
        return lambda: nc.tensor.transpose(out=out, in_=in_, identity=ident)

    try:
        S.dma("sp", C[:], cst_d[:, :], reads=[], writes=["C"])
        cp("dve", identb[:], identf, ["C"], ["identb"])
        mset("dve", onesb[:], 1.0, ["onesb"])
        for nm, t_ in (("S_hg", S_hg), ("S_rt", S_rt), ("Wst", Wst), ("halo", halo)):
            mset("pool", t_[:], 0.0, [nm])

        with ExitStack() as es:
            stg = sb(es, [128, 128], F32, "stg")
            for l in range(depth):
                mset("dve", stg[:], 0.0, ["stg"])
                rows = [(P["g_pre_mix"][l], 16), (P["g_post_mix"][l], 16), (P["g_pre_ffn"][l], 16), (P["g_post_ffn"][l], 16),
                        (P["mix_beta"][l], 16), (P["hg_lb_logits"][0], 6), (P["hg_lb_logits"][1], 6),
                        (P["hg_onorm"][l], 1), (P["s5_d"][l], 4), (P["s5_b_glu"][l], 4)]
                r0 = 0
                pairs = []
                for src_, n in rows:
                    pairs.append((stg[r0:r0 + n, :], src_.rearrange("(r c) -> r c", c=128)))
                    r0 += n
                S.dma_group("sp", pairs, reads=[], writes=["stg"])
                S.mm_group([tr(pbank[0][:, 0:128], stg[:, :], identf)], reads=["stg", "C"], writes=[PB(0)])
                cp("dve", colsA[:, l, :], pbank[0][:, 0:A_N], [PB(0)], ["colsA"])
                for part in range(3):
                    nrow = 128 if part < 2 else 96
                    mset("dve", stg[:], 0.0, ["stg"])
                    pairs = []
                    r = 0
                    while r < nrow:
                        gr = part * 128 + r
                        if gr < 264:
                            j, t0 = gr // 88, gr % 88
                            n = min(88 - t0, nrow - r)
                            src_ = P["ffn_conv_w"][l, j, t0 * 128:(t0 + n) * 128]
                        else:
                            t0 = gr - 264
                            n = min(88 - t0, nrow - r)
                            src_ = P["ffn_conv_b"][l, t0 * 128:(t0 + n) * 128]
                        pairs.append((stg[r:r + n, :], src_.rearrange("(r c) -> r c", c=128)))
                        r += n
                    S.dma_group("sp", pairs, reads=[], writes=["stg"])
                    S.mm_group([tr(pbank[0][:, 0:128], stg[:, :], identf)], reads=["stg", "C"], writes=[PB(0)])
                    cp("dve", colsB[:, l, part * 128:part * 128 + nrow], pbank[0][:, 0:nrow], [PB(0)], ["colsB"])
            mset("dve", lbc[:], 0.0, ["lbc"])
            if depth > 1:
                tt(lbc[:, 1, :], colsA[:, 0, A_LB1:A_LB1 + 6], colsA[:, 0, A_LB0:A_LB0 + 6], ALU.subtract, ["colsA"], ["lbc"])
                act(lbc[:, 1, :], lbc[:, 1, :], AF.Sigmoid, ["lbc"], ["lbc"])
            ts(omlc[:], lbc[:], -1.0, 1.0, ALU.mult, ALU.add, ["lbc"], ["omlc"])
            for l in range(depth):
                cp("dve", gsc[:, l, :], colsA[:, l, A_BETA:A_BETA + 16], ["colsA"], ["gsc"])
                ts(gsc[:, l, 0:6], gsc[:, l, 0:6], colsA[:, l, A_ONORM:A_ONORM + 1], None, ALU.mult, None, ["gsc", "colsA"], ["gsc"])
            S.barrier()
        stage("setup0")

        with ExitStack() as es:
            NS = 6
            st32 = [sb(es, [128, 2048], F32, "st32") for _ in range(NS)]
            st16 = [sb(es, [128, 2048], BF16, "st16") for _ in range(NS)]
            pc = [0]
            ceng = ["dve", "act", "pool"]

            def conv_piece(srcs, dst, dst_view=None):
                s = pc[0] % NS
                pc[0] += 1
                pairs = []
                for (src_, off, n) in srcs:
                    if off is None:
                        pairs.append((st32[s][:], src_))
                    else:
                        pairs.append((st32[s][:].rearrange("p (k c) -> p k c", c=512)[:, :, off:off + n], src_))
                S.dma_group("sp", pairs, reads=[], writes=[("st32", s)])
                cp(ceng[s % 3], st16[s][:], st32[s][:], [("st32", s)], [("st16", s)])
                srcv = st16[s][:] if dst_view is None else dst_view(st16[s][:])
                S.dma("act", dst, srcv, reads=[("st16", s)], writes=[("wscr", s)])

            chunks = _inproj_chunks()
            for l in range(depth):
                for ci, segs in enumerate(chunks):
                    for kq in range(4):
                        srcs = []
                        off = 0
                        for (c0, n) in segs:
                            srcs.append((P["w_in"][l, kq * 512:(kq + 1) * 512, c0:c0 + n].rearrange("(k p) c -> p k c", p=128), off, n))
                            off += n
                        conv_piece(srcs, wsi[l, ci, :, kq * 2048:(kq + 1) * 2048])
                for ci in range(4):
                    for kq in range(4):
                        src_ = P["w_out"][l, kq * 512:(kq + 1) * 512, ci * 512:(ci + 1) * 512].rearrange("(k p) c -> p k c", p=128)
                        conv_piece([(src_, 0, 512)], wso[l, ci, :, kq * 2048:(kq + 1) * 2048])
                for ci in range(22):
                    for kq in range(4):
                        srcs = []
                        for i, c0 in enumerate((256 * ci, 5632 + 256 * ci)):
                            srcs.append((P["ffn_w_up"][l, kq * 512:(kq + 1) * 512, c0:c0 + 256].rearrange("(k p) c -> p k c", p=128), 256 * i, 256))
                        conv_piece(srcs, wsu[l, ci, :, kq * 2048:(kq + 1) * 2048])
                for kt in range(44):
                    conv_piece([(P["ffn_w_down"][l, kt * 128:(kt + 1) * 128, :], None, 0)],
                               wsd[l].rearrange("o p (k c) -> p o k c", c=128)[:, :, kt, :],
                               dst_view=lambda t: t.rearrange("p (o c) -> p o c", c=128))
                conv_piece([(P["s5_w_glu"][l].rearrange("(k p) c -> p k c", p=128), 0, 512)], wsg[l, :, :])
            S.barrier()

        stage("conv")
        for l in range(depth):
            with ExitStack() as es:
                lam = sb(es, [128, 3, 128], F32, "lam")
                mset("dve", lam[:], 0.0, ["lam"])
                S.dma_group("sp", [(lam[0:32, 0, 0:64], P["s5_lam_re"][l]), (lam[0:32, 0, 64:128], P["s5_lam_re"][l]),
                                   (lam[0:32, 1, 0:64], P["s5_lam_im"][l]), (lam[0:32, 1, 64:128], P["s5_lam_im"][l]),
                                   (lam[0:32, 2, 0:1], P["s5_log_dt"][l].rearrange("(g o) -> g o", o=1))], reads=[], writes=["lam"])
                gm = sb(es, [128, 5, 128], F32, "gm")
                act(gm[0:32, 0, 0:1], lam[0:32, 2, 0:1], AF.Exp, ["lam"], ["gm0"])
                ts(gm[0:32, 1, :], lam[0:32, 0, :], -1e-4, None, ALU.min, None, ["lam"], ["gm1"])
                ts(gm[0:32, 2, :], gm[0:32, 1, :], gm[0:32, 0, 0:1], None, ALU.mult, None, ["gm0", "gm1"], ["gm2"])
                act(gm[0:32, 3, :], gm[0:32, 2, :], AF.Exp, ["gm2"], ["gm3"])
                ts(gm[0:32, 4, :], lam[0:32, 1, :], gm[0:32, 0, 0:1], None, ALU.mult, None, ["lam", "gm0"], ["gm4"])
                cols = sb(es, [128, 4, 32], F32, "s5cols")
                for k_, src_ap in ((0, gm[0:32, 1, :]), (1, lam[0:32, 1, :]), (2, gm[0:32, 3, :]), (3, gm[0:32, 4, :])):
                    S.mm_group([tr(pbank[0][:, 0:32], src_ap, identf[0:32, 0:32])], reads=["gm1", "gm3", "gm4", "lam", "C"], writes=[PB(0)])
                    cp("dve", cols[:, k_, :], pbank[0][:, 0:32], [PB(0)], [("cols", k_)])
                cr_ = [("cols", k_) for k_ in range(4)]
                cp("dve", rho[:, l, :], cols[:, 2, :], cr_, ["rho"])
                wk = sb(es, [128, 12, 32], F32, "s5wk")
                wki = sb(es, [128, 32], I32, "s5wki")

                def sin_of(dst, src_, shift, rd, wr):
                    ts(wk[:, 10, :], src_, shift, 1.0 / (2 * math.pi), ALU.add, ALU.mult, rd, ["wk10"])
                    cp("dve", wki[:], wk[:, 10, :], ["wk10"], ["wki"])
                    cp("dve", wk[:, 11, :], wki[:], ["wki"], ["wk11"])
                    ts(wk[:, 10, :], wk[:, 11, :], -2 * math.pi, shift, ALU.mult, ALU.add, ["wk11"], ["wk10"])
                    tt(wk[:, 10, :], wk[:, 10, :], src_, ALU.add, ["wk10"] + rd, ["wk10"])
                    ts(wk[:, 10, :], wk[:, 10, :], math.pi, -math.pi, ALU.min, ALU.max, ["wk10"], ["wk10"])
                    act(dst, wk[:, 10, :], AF.Sin, ["wk10"], wr)

                sin_of(wk[:, 0, :], cols[:, 3, :], 0.0, cr_, ["wk0"])
                sin_of(wk[:, 1, :], cols[:, 3, :], math.pi / 2, cr_, ["wk1"])
                tt(wk[:, 2, :], wk[:, 1, :], cols[:, 2, :], ALU.mult, ["wk1"] + cr_, ["wk2"])
                tt(wk[:, 3, :], wk[:, 0, :], cols[:, 2, :], ALU.mult, ["wk0"] + cr_, ["wk3"])
                tt(wk[:, 4, :], cols[:, 0, :], cols[:, 0, :], ALU.mult, cr_, ["wk4"])
                tt(wk[:, 5, :], cols[:, 1, :], cols[:, 1, :], ALU.mult, cr_, ["wk5"])
                tt(wk[:, 4, :], wk[:, 4, :], wk[:, 5, :], ALU.add, ["wk4", "wk5"], ["wk4"])
                S.op("dve", lambda: nc.vector.reciprocal(out=wk[:, 4, :], in_=wk[:, 4, :]), ["wk4"], ["wk4"])
                ts(wk[:, 5, :], wk[:, 2, :], -1.0, None, ALU.add, None, ["wk2"], ["wk5"])
                tt(wk[:, 6, :], wk[:, 5, :], cols[:, 0, :], ALU.mult, ["wk5"] + cr_, ["wk6"])
                tt(wk[:, 7, :], wk[:, 3, :], cols[:, 1, :], ALU.mult, ["wk3"] + cr_, ["wk7"])
                tt(wk[:, 6, :], wk[:, 6, :], wk[:, 7, :], ALU.add, ["wk6", "wk7"], ["wk6"])
                tt(wk[:, 6, :], wk[:, 6, :], wk[:, 4, :], ALU.mult, ["wk6", "wk4"], ["wk6"])
                tt(wk[:, 7, :], wk[:, 3, :], cols[:, 0, :], ALU.mult, ["wk3"] + cr_, ["wk7"])
                tt(wk[:, 8, :], wk[:, 5, :], cols[:, 1, :], ALU.mult, ["wk5"] + cr_, ["wk8"])
                tt(wk[:, 7, :], wk[:, 7, :], wk[:, 8, :], ALU.subtract, ["wk7", "wk8"], ["wk7"])
                tt(wk[:, 7, :], wk[:, 7, :], wk[:, 4, :], ALU.mult, ["wk7", "wk4"], ["wk7"])
                bre = sb(es, [128, 32, 16], F32, "bre")
                bim = sb(es, [128, 32, 16], F32, "bim")
                S.dma_group("sp", [(bre[0:64, :, :], P["s5_b_re"][l].rearrange("g p c -> p g c")),
                                   (bim[0:64, :, :], P["s5_b_im"][l].rearrange("g p c -> p g c"))], reads=[], writes=["bre"])
                crb = wk[0:64, 6, :].unsqueeze(2).to_broadcast([64, 32, 16])
                cib = wk[0:64, 7, :].unsqueeze(2).to_broadcast([64, 32, 16])
                bb = sb(es, [128, 5, 32, 16], F32, "bb")
                tt(bb[0:64, 0], bre[0:64], crb, ALU.mult, ["bre", "wk6"], ["bb0"])
                tt(bb[0:64, 2], bim[0:64], cib, ALU.mult, ["bre", "wk7"], ["bb2"])
                tt(bb[0:64, 0], bb[0:64, 0], bb[0:64, 2], ALU.subtract, ["bb0", "bb2"], ["bb0"])
                tt(bb[0:64, 1], bim[0:64], crb, ALU.mult, ["bre", "wk6"], ["bb1"])
                tt(bb[0:64, 3], bre[0:64], cib, ALU.mult, ["bre", "wk7"], ["bb3"])
                tt(bb[0:64, 1], bb[0:64, 1], bb[0:64, 3], ALU.add, ["bb1", "bb3"], ["bb1"])
                ts(bb[0:64, 4], bb[0:64, 1], -1.0, None, ALU.mult, None, ["bb1"], ["bb4"])
                tabs = sb(es, [128, 3, 32, 128], BF16, "s5tabs")
                mset("pool", tabs[:], 0.0, ["tabs"])
                for t4 in range(4):
                    for vi, (ire, iim) in enumerate(((0, 1), (4, 0))):
                        S.mm_group([tr(pbank[1][:, 0:64], bb[0:64, ire, t4 * 8:(t4 + 1) * 8, :], identf[0:64, 0:64]),
                                    tr(pbank[1][:, 64:128], bb[0:64, iim, t4 * 8:(t4 + 1) * 8, :], identf[0:64, 0:64])],
                                   reads=["bb0", "bb1", "bb4", "C"], writes=[PB(1)])
                        for gl in range(8):
                            g = t4 * 8 + gl
                            ts(tabs[:, vi, g, :], pbank[1][:, 0:128], C[:, C_ROWM + gl:C_ROWM + gl + 1], None, ALU.mult, None, [PB(1), "C", "tabs"], [("tabs", vi, g)])
                cst_ = sb(es, [128, 4, 2, 64], F32, "cstage")
                S.dma_group("sp", [(cst_[:, :, 0, :], P["s5_c_re"][l].rearrange("(t g) c p -> (g c) t p", t=4)),
                                   (cst_[:, :, 1, :], P["s5_c_im"][l].rearrange("(t g) c p -> (g c) t p", t=4))], reads=[], writes=["cst_"])
                for t4 in range(4):
                    S.mm_group([tr(pbank[2][:, 0:128], cst_[:, t4, :, :], identf)], reads=["cst_", "C"], writes=[PB(2)])
                    for gl in range(8):
                        g = t4 * 8 + gl
                        cp("dve", tabs[0:64, 2, g, gl * 16:(gl + 1) * 16], pbank[2][0:64, gl * 16:(gl + 1) * 16], [PB(2), "tabs"], [("tabs", 2, g, 0)])
                        ts(tabs[64:128, 2, g, gl * 16:(gl + 1) * 16], pbank[2][64:128, gl * 16:(gl + 1) * 16], -1.0, None, ALU.mult, None, [PB(2), "tabs"], [("tabs", 2, g, 1)])
                S.barrier()
                S.dma("sp", s5tab[l].rearrange("v p f -> p v f"), tabs[:].rearrange("p v g c -> p v (g c)"), reads=[], writes=["s5tab"])
                ework = [sb(es, [128, 512], F32, "ew") for _ in range(3)]
                ewi = sb(es, [128, 512], I32, "ewi")
                eout = [sb(es, [128, 2, 512], F32, "eout") for _ in range(2)]
                tpos = C[:, C_TPOS:C_TPOS + 512]
                for g in range(32):
                    eo = eout[g % 2]
                    ts(ework[0][:], tpos, cols[:, 3, g:g + 1], None, ALU.mult, None, ["C"], ["ew0"])
                    for which, shift, sgn in ((0, math.pi / 2, 1.0), (1, 0.0, -1.0)):
                        ts(ework[1][:], ework[0][:], shift, 1.0 / (2 * math.pi), ALU.add, ALU.mult, ["ew0"], ["ew1"])
                        cp("dve", ewi[:], ework[1][:], ["ew1"], ["ewi"])
                        cp("dve", ework[2][:], ewi[:], ["ewi"], ["ew2"])
                        ts(ework[1][:], ework[2][:], -2 * math.pi, shift, ALU.mult, ALU.add, ["ew2"], ["ew1"])
                        tt(ework[1][:], ework[1][:], ework[0][:], ALU.add, ["ew1", "ew0"], ["ew1"])
                        ts(ework[1][:], ework[1][:], math.pi, -math.pi, ALU.min, ALU.max, ["ew1"], ["ew1"])
                        act(eo[:, which, :], ework[1][:], AF.Sin, ["ew1"], [("eo", g % 2, which)], scale=sgn)
                    S.dma("pool", etab[l, g].rearrange("w p t -> p w t"), eo[:], reads=[("eo", g % 2, 0), ("eo", g % 2, 1)], writes=[("etab", g % 2)])
                S.barrier()

        def rstd_from(pss_bank, n, out_ap, wr):
            act(out_ap, pbank[pss_bank][:], AF.Ln, [PB(pss_bank)], wr, scale=1.0 / n, bias=EPS)
            act(out_ap, out_ap, AF.Exp, wr, wr, scale=-0.5)

        def phase_p0(b):
            with ExitStack() as es:
                xin = sb(es, [128, 4, D], F32, "xin")
                xo = [sb(es, [128, 512], F32, "xo") for _ in range(4)]
                S.dma_group("sp", [(xin[:, grp, :], x_in[b * TB + grp * 128:b * TB + (grp + 1) * 128, :]) for grp in range(4)], reads=[], writes=["xin"])
                for ft in range(16):
                    bk = ft % 8
                    S.mm_group([tr(pbank[bk][:, grp * 128:(grp + 1) * 128], xin[:, grp, ft * 128:(ft + 1) * 128], identf) for grp in range(4)],
                               reads=["xin", "C"], writes=[PB(bk)])
                    cp("act" if ft % 2 else "dve", xo[ft % 4][:], pbank[bk][:], [PB(bk)], [("xo", ft % 4)])
                    S.dma("pool", xs[0][ft, :, b * TB:(b + 1) * TB], xo[ft % 4][:], reads=[("xo", ft % 4)], writes=[("st_xo", ft % 4)])
                S.barrier()

        def phase_p6(b):
            with ExitStack() as es:
                xf = sb(es, [128, 16, 512], F32, "xf")
                xt_ = [sb(es, [128, D], F32, "xtok") for _ in range(2)]
                S.dma_group("sp", [(xf[:, ft, :], xs[0][ft, :, b * TB:(b + 1) * TB]) for ft in range(16)], reads=[], writes=["xf"])
                for grp in range(4):
                    for q4 in range(4):
                        bk = (grp % 2) * 4 + q4
                        S.mm_group([tr(pbank[bk][:, i * 128:(i + 1) * 128], xf[:, q4 * 4 + i, grp * 128:(grp + 1) * 128], identf) for i in range(4)],
                                   reads=["xf", "C"], writes=[PB(bk)])
                        cp("act" if q4 % 2 else "dve", xt_[grp % 2][:, q4 * 512:(q4 + 1) * 512], pbank[bk][:], [PB(bk)], [("xtok", grp % 2, q4)])
                    S.dma("pool", out_d[b * TB + grp * 128:b * TB + (grp + 1) * 128, :], xt_[grp % 2][:],
                          reads=[("xtok", grp % 2, q4) for q4 in range(4)], writes=[("st_out", grp % 2)])
                S.barrier()

        def norm_to_hT(l, b, src, gcol0):
            with ExitStack() as es:
                xT = sb(es, [128, 16, 512], F32, "xT")
                sq = [sb(es, [128, 512], BF16, "sq") for _ in range(2)]
                rstd = sb(es, [128, 512], F32, "rstd")
                S.dma_group("sp", [(xT[:, ft, :], src[ft, :, b * TB:(b + 1) * TB]) for ft in range(16)], reads=[], writes=["xT"])
                for ft in range(16):
                    act(sq[ft % 2][:], xT[:, ft, :], AF.Square, ["xT"], [("sq", ft % 2)])
                    S.mm_group([mm(pbank[7][:], onesb[:], sq[ft % 2][:], ft == 0, ft == 15)], reads=[("sq", ft % 2), "onesb"], writes=[PB(7)])
                rstd_from(7, D, rstd[:], ["rstd"])
                for ft in range(16):
                    stt(hT[:, ft, :], xT[:, ft, :], colsA[:, l, gcol0 + ft:gcol0 + ft + 1], rstd[:], ALU.mult, ALU.mult,
                        ["xT", "rstd", "colsA"], [("hT", ft)])
                S.barrier()

        HT_ALL = [("hT", ft) for ft in range(16)]

        def proj_fm(bank, ws, col0, ncols=128):
            wv = wbuf[ws][:].rearrange("p (k c) -> p k c", c=512)
            S.mm_group([mm(pbank[bank][0:ncols, :], wv[:, kt, col0:col0 + ncols], hT[:, kt, :], kt == 0, kt == 15) for kt in range(16)],
                       reads=HT_ALL + [("wbuf", ws)], writes=[PB(bank)])

        def proj_tm(bank, ws, col0, ncols, width):
            wv = wbuf[ws][:].rearrange("p (k c) -> p k c", c=512)
            pv = pbank[bank][:].rearrange("p (g c) -> p g c", c=width)
            fns = []
            for grp in range(4):
                for kt in range(16):
                    fns.append(mm(pv[:, grp, 0:ncols], hT[:, kt, grp * 128:(grp + 1) * 128], wv[:, kt, col0:col0 + ncols], kt == 0, kt == 15))
            S.mm_group(fns, reads=HT_ALL + [("wbuf", ws)], writes=[PB(bank)])

        def head_norm_out(l, po_bank, pss_bank, gate, mt, es_tiles, rd_gate):
            sqb, rs, tmp = es_tiles
            act(sqb[:], pbank[po_bank][:], AF.Square, [PB(po_bank)], ["hn_sq"])
            S.mm_group([mm(pbank[pss_bank][:], onesb[:], sqb[:])], reads=["hn_sq", "onesb"], writes=[PB(pss_bank)])
            rstd_from(pss_bank, 128, rs[:], ["hn_rs"])
            tt(tmp[:], pbank[po_bank][:], rs[:], ALU.mult, [PB(po_bank), "hn_rs"], ["hn_tmp"])
            stt(mixT_h[0][:, mt, :], tmp[:], gsc[:, l, mt:mt + 1], gate, ALU.mult, ALU.mult, ["hn_tmp", "gsc"] + rd_gate, [("mixT", mt)])

        def phase_hgrn(l, b, run_s5):
            with ExitStack() as es:
                F = lambda nm: sb(es, [128, 512], F32, nm)
                Bt = lambda nm: sb(es, [128, 512], BF16, nm)
                qf, t1, lf, kk, gate, b64, b16, e16, e64, tmpd = [F(n) for n in ("qf", "t1", "lf", "kk", "gate", "b64", "b16", "e16", "e64", "tmpd")]
                tk = sb(es, [128, 4, 512], F32, "tk")
                qh, qt, kdec, vt, kdt, sqb = [Bt(n) for n in ("qh", "qt", "kdec", "vt", "kdt", "sqb")]
                khat = sb(es, [128, 8, 4, 64], BF16, "khat")
                AT = sb(es, [128, 4, 128], BF16, "AT")
                Sbf = sb(es, [128, 8, 128], BF16, "Sbf")
                dec = sb(es, [128, 8], F32, "dec")
                rs, tmpo = F("rs"), F("tmpo")
                mset("pool", khat[:], 0.0, ["khat"])
                mset("pool", AT[:], 0.0, ["AT"])
                m64 = C[:, C_M64:C_M64 + 512]
                m16 = C[:, C_M16:C_M16 + 512]
                cmask = C[:, C_CMASK:C_CMASK + 64]
                b64v = b64[:].rearrange("p (c j) -> p c j", j=64)
                kkv = kk[:].rearrange("p (c j) -> p c j", j=64)
                ws_next = load_w(wsi[l, 0, :, :])
                for h in range(6):
                    ws = ws_next
                    if h < 5:
                        ws_next = load_w(wsi[l, h + 1, :, :])
                    proj_fm(0, ws, 0)
                    proj_fm(1, ws, 128)
                    proj_fm(2, ws, 256)
                    proj_tm(3, ws, 384, 128, 128)
                    run_s5(1)
                    act(qf[:], pbank[0][:], AF.Silu, [PB(0)], ["qf"])
                    act(t1[:], pbank[1][:], AF.Sigmoid, [PB(1)], ["t1"])
                    act(kk[:], pbank[1][:], AF.Sigmoid, [PB(1)], ["kk"], scale=-1.0)
                    act(gate[:], pbank[2][:], AF.Silu, [PB(2)], ["gate"])
                    cp("act", vt[:], pbank[3][:], [PB(3)], ["vt"])
                    if stage("hg1"):
                        S.barrier()
                        return
                    ts(t1[:], t1[:], omlc[:, l, h:h + 1], lbc[:, l, h:h + 1], ALU.mult, ALU.add, ["t1", "omlc", "lbc"], ["t1"])
                    act(lf[:], t1[:], AF.Ln, ["t1"], ["lf"])
                    ts(kk[:], kk[:], omlc[:, l, h:h + 1], None, ALU.mult, None, ["kk", "omlc"], ["kk"])
                    S.op("dve", lambda: nc.vector.tensor_tensor_scan(out=b64[:], data0=m64, data1=lf[:], initial=0.0, op0=ALU.mult, op1=ALU.add), ["lf", "C"], ["b64"])
                    S.op("dve", lambda: nc.vector.tensor_tensor_scan(out=b16[:], data0=m16, data1=lf[:], initial=0.0, op0=ALU.mult, op1=ALU.add), ["lf", "C"], ["b16"])
                    act(e16[:], b16[:], AF.Exp, ["b16"], ["e16"])
                    act(e64[:], b64[:], AF.Exp, ["b64"], ["e64"])
                    tt(qh[:], qf[:], e16[:], ALU.mult, ["qf", "e16"], ["qh"])
                    tt(qt[:], qf[:], e64[:], ALU.mult, ["qf", "e64"], ["qt"])
                    tt(tmpd[:].rearrange("p (c j) -> p c j", j=64), b64v[:, :, 63:64].to_broadcast([128, 8, 64]), b64v, ALU.subtract, ["b64"], ["tmpd"])
                    act(tmpd[:], tmpd[:], AF.Exp, ["tmpd"], ["tmpd"])
                    tt(kdec[:], kk[:], tmpd[:], ALU.mult, ["kk", "tmpd"], ["kdec"])
                    act(dec[:], b64v[:, :, 63], AF.Exp, ["b64"], ["dec"])
                    if stage("hg2"):
                        S.barrier()
                        return
                    for i in range(4):
                        n = 16 * (i + 1)
                        tkv = tk[:, i, :].rearrange("p (c j) -> p c j", j=64)
                        if i == 0:
                            act(tkv[:, :, 0:n], b64v[:, :, 0:n], AF.Exp, ["b64"], [("tk", i)], scale=-1.0)
                        else:
                            tt(tkv[:, :, 0:n], b64v[:, :, 16 * i - 1:16 * i].to_broadcast([128, 8, n]), b64v[:, :, 0:n], ALU.subtract, ["b64"], [("tk", i)])
                            act(tkv[:, :, 0:n], tkv[:, :, 0:n], AF.Exp, [("tk", i)], [("tk", i)])
                        tt(khat[:, :, i, 0:n], kkv[:, :, 0:n], tkv[:, :, 0:n], ALU.mult, ["kk", ("tk", i)], ["khat"])
                    if stage("hg3"):
                        S.barrier()
                        return
                    fns = []
                    for c in range(8):
                        j2, par = c // 2, c % 2
                        for i in range(4):
                            fns.append(mm(pbank[4][par * 64:(par + 1) * 64, j2 * 64 + 16 * i:j2 * 64 + 16 * i + 16], khat[:, c, i, :],
                                          qh[:, c * 64 + 16 * i:c * 64 + 16 * i + 16]))
                    S.mm_group(fns, reads=["khat", "qh"], writes=[PB(4)])
                    run_s5(1)
                    pAv = pbank[4][:, 0:256].rearrange("p (j t) -> p j t", t=64)
                    for par in range(2):
                        sl = slice(par * 64, (par + 1) * 64)
                        tt(AT[sl, :, par * 64:(par + 1) * 64], pAv[sl, :, :], cmask[sl, :].unsqueeze(1).to_broadcast([64, 4, 64]), ALU.mult,
                           [PB(4), "C"], ["AT"])
                    if stage("hg4"):
                        S.barrier()
                        return
                    p7b = pbank[4][:].bitcast(BF16)
                    S.mm_group([tr(p7b[:, grp * 128:(grp + 1) * 128], kdec[:, grp * 128:(grp + 1) * 128], identb[:]) for grp in range(4)],
                               reads=["kdec", "identb"], writes=[PB(4)])
                    cp("act", kdt[:], p7b[:, 0:512], [PB(4)], ["kdt"])
                    if stage("hg5"):
                        S.barrier()
                        return
                    for par in range(2):
                        fns = []
                        sl = slice(par * 64, (par + 1) * 64)
                        for j2 in range(4):
                            fns.append(mm(pbank[2 + par][:, j2 * 128:(j2 + 1) * 128], kdt[sl, j2 * 128:(j2 + 1) * 128], vt[sl, j2 * 128:(j2 + 1) * 128]))
                        S.mm_group(fns, reads=["kdt", "vt"], writes=[PB(2 + par)])
                    if stage("hg6"):
                        S.barrier()
                        return
                    Sreg = ("S_hg", l, h)
                    for c in range(8):
                        cp("act", Sbf[:, c, :], S_hg[:, l, h, :], [Sreg, "S_hg"], [("Sbf", c)])
                        stt(S_hg[:, l, h, :], S_hg[:, l, h, :], dec[:, c:c + 1], pbank[2 + c % 2][:, (c // 2) * 128:(c // 2 + 1) * 128], ALU.mult, ALU.add,
                            [Sreg, "S_hg", "dec", PB(2 + c % 2)], [Sreg])
                    if stage("hg7"):
                        S.barrier()
                        return
                    for j2 in range(4):
                        S.mm_group([mm(pbank[0][:, j2 * 128:(j2 + 1) * 128], vt[:, j2 * 128:(j2 + 1) * 128], AT[:, j2, :], True, False),
                                    mm(pbank[0][:, (2 * j2) * 64:(2 * j2 + 1) * 64], Sbf[:, 2 * j2, :], qt[:, (2 * j2) * 64:(2 * j2 + 1) * 64], False, False),
                                    mm(pbank[0][:, (2 * j2 + 1) * 64:(2 * j2 + 2) * 64], Sbf[:, 2 * j2 + 1, :], qt[:, (2 * j2 + 1) * 64:(2 * j2 + 2) * 64], False, True)],
                                   reads=["vt", "AT", ("Sbf", 2 * j2), ("Sbf", 2 * j2 + 1), "qt"], writes=[PB(0)])
                    if stage("hg8"):
                        S.barrier()
                        return
                    head_norm_out(l, 0, 1, gate[:], h, (sqb, rs, tmpo), ["gate"])
                    run_s5(1)
                    if stage("hg9"):
                        S.barrier()
                        return
                S.barrier()

        def phase_ret(l, b, run_s5):
            with ExitStack() as es:
                F = lambda nm: sb(es, [128, 512], F32, nm)
                Bt = lambda nm: sb(es, [128, 512], BF16, nm)
                tab = sb(es, [128, 4, 512], F32, "rtab")
                ta, tb_ = F("ta"), F("tb")
                qd, kd, kdt, sqb = [Bt(n) for n in ("qd", "kd", "kdt", "sqb")]
                qdz = [Bt("qdz0"), Bt("qdz1")]
                vt = sb(es, [128, 4, 256], BF16, "vtr")
                gates = [F("g0"), F("g1")]
                PT = [sb(es, [128, 4, 128], BF16, "PT%d" % i) for i in range(2)]
                Sbf = sb(es, [128, 8, 128], BF16, "Sbfr")
                rs, tmpo = F("rs"), F("tmpo")
                for t_ in (qdz[0], qdz[1], PT[0], PT[1]):
                    mset("pool", t_[:], 0.0, ["rz"])
                for j in range(3):
                    wa = load_w(wsi[l, 6 + j, :, :])
                    wb = load_w(wsi[l, 9 + j, :, :])
                    S.dma_group("sp", [(tab[:, w, :], rope_d[j, w, :, b * TB:(b + 1) * TB]) for w in range(4)], reads=[], writes=["rtab"])
                    for w in range(4):
                        proj_fm(w, wa, 128 * w)
                    for (z, zp, cq, sq_, dst, nm) in ((0, 1, 0, 1, qd, "qd"), (2, 3, 2, 3, kd, "kd")):
                        tt(ta[:], pbank[z][:], tab[:, cq, :], ALU.mult, [PB(z), "rtab"], ["ta"])
                        tt(tb_[:], pbank[zp][:], tab[:, sq_, :], ALU.mult, [PB(zp), "rtab"], ["tb"])
                        tt(dst[:], ta[:], tb_[:], ALU.add, ["ta", "tb"], [nm])
                        if nm == "qd":
                            for hh in range(2):
                                sl = slice(64 * hh, 64 * hh + 64)
                                tt(qdz[hh][sl, :], ta[sl, :], tb_[sl, :], ALU.add, ["ta", "tb", "rz"], [("qdz", hh)])
                    wv = wbuf[wb][:].rearrange("p (k c) -> p k c", c=512)
                    fns = []
                    for grp in range(4):
                        pvv = pbank[4 - grp // 2][:, (grp % 2) * 256:(grp % 2 + 1) * 256]
                        for kt in range(16):
                            fns.append(mm(pvv, hT[:, kt, grp * 128:(grp + 1) * 128], wv[:, kt, 0:256], kt == 0, kt == 15))
                    S.mm_group(fns, reads=HT_ALL + [("wbuf", wb)], writes=[PB(4), PB(3)])
                    for half in range(2):
                        cp("act", vt[:, 2 * half:2 * half + 2, :], pbank[4 - half][:].rearrange("p (g c) -> p g c", c=256), [PB(4), PB(3)], [("vtr", half)])
                    for hh in range(2):
                        proj_fm(hh, wb, 256 + 128 * hh)
                        act(gates[hh][:], pbank[hh][:], AF.Silu, [PB(hh)], [("gate", hh)])
                    run_s5(1)
                    VT = [("vtr", 0), ("vtr", 1)]
                    p3b = pbank[2][:].bitcast(BF16)
                    S.mm_group([tr(p3b[:, grp * 128:(grp + 1) * 128], kd[:, grp * 128:(grp + 1) * 128], identb[:]) for grp in range(4)],
                               reads=["kd", "identb"], writes=[PB(2)])
                    cp("act", kdt[:], p3b[:, 0:512], [PB(2)], ["kdt"])
                    for par in range(2):
                        fns = []
                        sl = slice(par * 64, (par + 1) * 64)
                        for j2 in range(4):
                            for hh in range(2):
                                fns.append(mm(pbank[3 - par][64 * hh:64 * hh + 64, j2 * 128:(j2 + 1) * 128],
                                              kdt[sl, j2 * 128 + 64 * hh:j2 * 128 + 64 * hh + 64], vt[sl, j2, 128 * hh:128 * hh + 128]))
                        S.mm_group(fns, reads=["kdt"] + VT, writes=[PB(3 - par)])
                    run_s5(1)
                    Sreg = ("S_rt", l, j)
                    for c in range(8):
                        cp("act", Sbf[:, c, :], S_rt[:, l, j, :], [Sreg, "S_rt"], [("Sbf", c)])
                        stt(S_rt[:, l, j, :], S_rt[:, l, j, :], C[:, C_GDEC + j:C_GDEC + j + 1], pbank[3 - c % 2][:, (c // 2) * 128:(c // 2 + 1) * 128],
                            ALU.mult, ALU.add, [Sreg, "S_rt", "C", PB(3 - c % 2)], [Sreg])
                    for hh in range(2):
                        h = 2 * j + hh
                        fns = []
                        for c in range(8):
                            j2, par = c // 2, c % 2
                            fns.append(mm(pbank[4][par * 64:(par + 1) * 64, j2 * 64:(j2 + 1) * 64], kd[:, c * 64:(c + 1) * 64], qdz[hh][:, c * 64:(c + 1) * 64]))
                        S.mm_group(fns, reads=["kd", ("qdz", hh), "rz"], writes=[PB(4)])
                        pSv = pbank[4][:, 0:256].rearrange("p (j t) -> p j t", t=64)
                        rm = C[:, C_RMASK + 64 * h:C_RMASK + 64 * (h + 1)]
                        for par in range(2):
                            sl = slice(par * 64, (par + 1) * 64)
                            tt(PT[hh][sl, :, par * 64:(par + 1) * 64], pSv[sl, :, :], rm[sl, :].unsqueeze(1).to_broadcast([64, 4, 64]), ALU.mult,
                               [PB(4), "C", "rz"], [("PT", hh)])
                        ob = hh
                        for j2 in range(4):
                            S.mm_group([mm(pbank[ob][:, j2 * 128:(j2 + 1) * 128], vt[:, j2, 128 * hh:128 * hh + 128], PT[hh][:, j2, :], True, False),
                                        mm(pbank[ob][:, (2 * j2) * 64:(2 * j2 + 1) * 64], Sbf[:, 2 * j2, :], qdz[hh][:, (2 * j2) * 64:(2 * j2 + 1) * 64], False, False),
                                        mm(pbank[ob][:, (2 * j2 + 1) * 64:(2 * j2 + 2) * 64], Sbf[:, 2 * j2 + 1, :], qdz[hh][:, (2 * j2 + 1) * 64:(2 * j2 + 2) * 64], False, True)],
                                       reads=VT + [("PT", hh), ("Sbf", 2 * j2), ("Sbf", 2 * j2 + 1), ("qdz", hh), ("gate", hh)], writes=[PB(ob)])
                        head_norm_out(l, ob, 4, gates[hh][:], 10 + h, (sqb, rs, tmpo), [("gate", hh)])
                        run_s5(1)
                S.barrier()

        def s5_make(l, b, es):
            F = lambda nm: sb(es, [128, 512], F32, nm)
            tabs2 = [sb(es, [128, 3, 8, 128], BF16, "s5t") for _ in range(2)]
            r1, r2_ = F("r1"), F("r2")
            r3 = sb(es, [128, 2], F32, "r3")
            wg = sb(es, [128, 4, 512], BF16, "wglu")
            uf = sb(es, [128, 4, 512], F32, "uf")
            ub = sb(es, [128, 4, 512], BF16, "ub")
            et = [sb(es, [128, 2, 512], F32, "et") for _ in range(2)]
            t1, t2, t3, t4, xa, xb_, Wa, Wb = [F(n) for n in ("t1", "t2", "t3", "t4", "xa", "xb", "Wa", "Wb")]
            Sg = [sb(es, [128, 512], BF16, "Sg") for _ in range(2)]
            ya = F("ya")
            yg = sb(es, [128, 4, 512], F32, "yg")
            ygb = sb(es, [128, 4, 512], BF16, "ygb")
            sg_ = F("sg")
            S.dma("sp", wg[:].rearrange("p k c -> p (k c)"), wsg[l, :, :], reads=[], writes=["wglu"])
            ws = load_w(wsi[l, 12, :, :])
            for t in range(4):
                proj_fm(t, ws, 128 * t)
                cp("act", uf[:, t, :], pbank[t][:], [PB(t)], [("uf", t)])
                cp("dve", ub[:, t, :], uf[:, t, :], [("uf", t)], [("ub", t)])

            def y_accum(g):
                t, gl = g // 8, g % 8
                S.mm_group([mm(pbank[7][:], tabs_of(g)[:, 2, gl, :], Sg[g % 2][:], gl == 0, gl == 7)], reads=[("s5t", t % 2), ("Sg", g % 2)], writes=[PB(7)])
                if gl == 7:
                    stt(ya[:], uf[:, t, :], colsA[:, l, A_S5D + t:A_S5D + t + 1], pbank[7][:], ALU.mult, ALU.add, [("uf", t), "colsA", PB(7)], ["ya"])
                    act(yg[:, t, :], ya[:], AF.Gelu, ["ya"], [("yg", t)])
                    cp("dve", ygb[:, t, :], yg[:, t, :], [("yg", t)], [("ygb", t)])

            def tabs_of(g):
                return tabs2[(g // 8) % 2]

            def gen():
                for g in range(32):
                    t, gl = g // 8, g % 8
                    tb_ = tabs_of(g)
                    if gl == 0:
                        S.dma("sp", tb_[:].rearrange("p v g c -> p v (g c)"), s5tab[l, :, :, t * 1024:(t + 1) * 1024].rearrange("v p f -> p v f"),
                              reads=[], writes=[("s5t", t % 2)])
                    e = et[g % 2]
                    S.dma("sp", e[:], etab[l, g].rearrange("w p t -> p w t"), reads=[], writes=[("et", g % 2)])
                    ER = [("et", g % 2)]
                    Ec, Es = e[:, 0, :], e[:, 1, :]
                    S.mm_group([mm(pbank[5][:], tb_[:, 0, gl, :], ub[:, t, :])], reads=[("s5t", t % 2), ("ub", t)], writes=[PB(5)])
                    S.mm_group([mm(pbank[6][:], tb_[:, 1, gl, :], ub[:, t, :])], reads=[("s5t", t % 2), ("ub", t)], writes=[PB(6)])
                    if g >= 1:
                        y_accum(g - 1)
                    tt(t1[:], pbank[5][:], Ec, ALU.mult, [PB(5)] + ER, ["t1"])
                    tt(t4[:], pbank[5][:], Es, ALU.mult, [PB(5)] + ER, ["t4"])
                    tt(t2[:], pbank[6][:], Es, ALU.mult, [PB(6)] + ER, ["t2"])
                    tt(t3[:], pbank[6][:], Ec, ALU.mult, [PB(6)] + ER, ["t3"])
                    tt(xa[:], t1[:], t2[:], ALU.add, ["t1", "t2"], ["xa"], eng="pool")
                    tt(xb_[:], t3[:], t4[:], ALU.subtract, ["t3", "t4"], ["xb"], eng="pool")
                    rh = rho[:, l, g:g + 1].to_broadcast([128, 512])
                    S.op("dve", lambda: nc.vector.tensor_tensor_scan(out=Wa[:], data0=rh, data1=xa[:], initial=Wst[:, l, 0, g:g + 1], op0=ALU.mult, op1=ALU.add),
                         ["xa", "rho", ("Wst", l, g)], ["Wa"])
                    S.op("dve", lambda: nc.vector.tensor_tensor_scan(out=Wb[:], data0=rh, data1=xb_[:], initial=Wst[:, l, 1, g:g + 1], op0=ALU.mult, op1=ALU.add),
                         ["xb", "rho", ("Wst", l, g)], ["Wb"])
                    tt(r1[:], Wa[:], Ec, ALU.mult, ["Wa"] + ER, ["r1"], eng="pool")
                    tt(r2_[:], Wb[:], Es, ALU.mult, ["Wb"] + ER, ["r2"], eng="pool")
                    tt(Sg[g % 2][:], r1[:], r2_[:], ALU.subtract, ["r1", "r2"], [("Sg", g % 2)])
                    tt(Wst[:, l, 0, g:g + 1], r1[:, 511:512], r2_[:, 511:512], ALU.subtract, ["r1", "r2"], [("Wst", l, g)], eng="pool")
                    tt(r3[:, 0:1], Wb[:, 511:512], e[:, 0, 511:512], ALU.mult, ["Wb"] + ER, ["r3"], eng="pool")
                    tt(r3[:, 1:2], Wa[:, 511:512], e[:, 1, 511:512], ALU.mult, ["Wa"] + ER, ["r3"], eng="pool")
                    tt(Wst[:, l, 1, g:g + 1], r3[:, 0:1], r3[:, 1:2], ALU.add, ["r3", ("Wst", l, g)], [("Wst", l, g)], eng="pool")
                    yield
                y_accum(31)

            def end():
                for ot in range(4):
                    bk = 5 + ot % 2
                    S.mm_group([mm(pbank[bk][:], wg[:, kt, ot * 128:(ot + 1) * 128], ygb[:, kt, :], kt == 0, kt == 3) for kt in range(4)],
                               reads=["wglu"] + [("ygb", k_) for k_ in range(4)], writes=[PB(bk)])
                    act(sg_[:], pbank[bk][:], AF.Sigmoid, [PB(bk), "colsA"], ["sg"], bias=colsA[:, l, A_BGLU + ot:A_BGLU + ot + 1])
                    stt(mixT_h[0][:, 6 + ot, :], yg[:, ot, :], gsc[:, l, 6 + ot:7 + ot], sg_[:], ALU.mult, ALU.mult, [("yg", ot), "gsc", "sg"], [("mixT", 6 + ot)])

            return gen(), end

        def post_norm_residual(l, b, gcol0, xsrc, srcid, dstfn, es, yb):
            rstd = sb(es, [128, 512], F32, "rstd2")
            xr = [sb(es, [128, 512], F32, "xr") for _ in range(2)]
            tmp = [sb(es, [128, 512], F32, "pn_tmp") for _ in range(2)]
            rstd_from(7, D, rstd[:], ["rstd2"])
            for ot in range(16):
                s = ot % 2
                S.dma("sp", xr[s][:], xsrc[ot, :, b * TB:(b + 1) * TB], reads=[], writes=[("xr", s)])
                stt(tmp[s][:], yb[:, ot, :], colsA[:, l, gcol0 + ot:gcol0 + ot + 1], rstd[:], ALU.mult, ALU.mult, [("yb", ot), "colsA", "rstd2"], [("pn_tmp", s)])
                tt(tmp[s][:], tmp[s][:], xr[s][:], ALU.add, [("pn_tmp", s), ("xr", s)], [("pn_tmp", s)], eng="pool")
                dstfn(ot, tmp[s][:], [("pn_tmp", s)], s)

        def evac_y(bank, ot, yb, sq):
            cp("act", yb[:, ot, :], pbank[bank][:], [PB(bank)], [("yb", ot)])
            act(sq[ot % 3][:], pbank[bank][:], AF.Square, [PB(bank)], [("sq2", ot % 3)])
            if ot >= 1:
                o1 = ot - 1
                S.mm_group([mm(pbank[7][:], onesb[:], sq[o1 % 3][:], o1 == 0, False)], reads=[("sq2", o1 % 3), "onesb"], writes=[PB(7)])
            if ot == 15:
                S.mm_group([mm(pbank[7][:], onesb[:], sq[ot % 3][:], False, True)], reads=[("sq2", ot % 3), "onesb"], writes=[PB(7)])

        def phase_outproj(l, b, xsrc, srcid, xdst, dstid):
            with ExitStack() as es:
                yb = sb(es, [128, 16, 512], F32, "yb")
                sq = [sb(es, [128, 512], BF16, "sq2") for _ in range(3)]
                MIX_ALL = [("mixT", k_) for k_ in range(16)]
                ws_next = load_w(wso[l, 0, :, :])
                for oc in range(4):
                    ws = ws_next
                    if oc < 3:
                        ws_next = load_w(wso[l, oc + 1, :, :])
                    wv = wbuf[ws][:].rearrange("p (k c) -> p k c", c=512)
                    for oi in range(4):
                        ot = oc * 4 + oi
                        bk = ot % 4
                        S.mm_group([mm(pbank[bk][:], wv[:, kt, oi * 128:(oi + 1) * 128], mixT_h[0][:, kt, :], kt == 0, kt == 15) for kt in range(16)],
                                   reads=MIX_ALL + [("wbuf", ws)], writes=[PB(bk)])
                        evac_y(bk, ot, yb, sq)

                def dst(ot, ap, rd, s):
                    S.dma("pool", xdst[ot, :, b * TB:(b + 1) * TB], ap, reads=rd, writes=[("st_pn", s)])
                post_norm_residual(l, b, A_GPOST, xsrc, srcid, dst, es, yb)
                S.barrier()

        def phase_ffn(l, b, xsrc, srcid, xdst, dstid):
            norm_to_hT(l, b, xsrc, A_GPREF)
            with ExitStack() as es:
                actT = sb(es, [128, 44, 512], BF16, "actT")
                es_up = ExitStack()
                U = [sb(es_up, [128, 514], F32, "U") for _ in range(4)]
                acc = [sb(es_up, [128, 512], F32, "acc") for _ in range(4)]
                sgt = [sb(es_up, [128, 512], F32, "sgt") for _ in range(2)]
                cw = colsB[:, l, :]
                ws_next = load_w(wsu[l, 0, :, :])
                uc = 0
                for cu in range(22):
                    ws = ws_next
                    if cu < 21:
                        ws_next = load_w(wsu[l, cu + 1, :, :])
                    for i in range(2):
                        ffi = 2 * cu + i
                        accs = []
                        for kind in range(2):
                            tg = ffi + 44 * kind
                            bk = (2 * ffi + kind) % 6
                            proj_fm(bk, ws, 256 * kind + 128 * i)
                            u = U[uc % 4]
                            a = acc[uc % 4]
                            ur, ar_ = ("U", uc % 4), ("acc", uc % 4)
                            uc += 1
                            cp("act", u[:, 2:514], pbank[bk][:], [PB(bk)], [ur])
                            cp("pool", u[:, 0:2], halo[:, l, tg, :], [("halo", l, tg), "halo"], [(ur, "h")])
                            act(a[:], pbank[bk][:], AF.Identity, [PB(bk), "colsB"], [ar_], scale=cw[:, 2 * 88 + tg:2 * 88 + tg + 1], bias=cw[:, 264 + tg:264 + tg + 1])
                            stt(a[:], u[:, 1:513], cw[:, 88 + tg:88 + tg + 1], a[:], ALU.mult, ALU.add, [ur, (ur, "h"), ar_, "colsB"], [ar_])
                            stt(a[:], u[:, 0:512], cw[:, tg:tg + 1], a[:], ALU.mult, ALU.add, [ur, (ur, "h"), ar_, "colsB"], [ar_])
                            cp("pool", halo[:, l, tg, :], u[:, 512:514], [ur, "halo"], [("halo", l, tg)])
                            accs.append((a, ar_))
                        sg = sgt[ffi % 2]
                        act(sg[:], accs[0][0][:], AF.Silu, [accs[0][1]], [("sgt", ffi % 2)])
                        tt(actT[:, ffi, :], sg[:], accs[1][0][:], ALU.mult, [("sgt", ffi % 2), accs[1][1]], [("actT", ffi)], eng="pool")
                S.barrier()
                es_up.close()
                yb = sb(es, [128, 16, 512], F32, "ybf")
                sq = [sb(es, [128, 512], BF16, "sq2f") for _ in range(3)]
                ACT_ALL = [("actT", k_) for k_ in range(44)]
                ws_next = load_w(wsd[l, 0, :, :], 44 * 128)
                for ot in range(16):
                    ws = ws_next
                    if ot < 15:
                        ws_next = load_w(wsd[l, ot + 1, :, :], 44 * 128)
                    bk = ot % 4
                    S.mm_group([mm(pbank[bk][:], wbuf[ws][:, kt * 128:(kt + 1) * 128], actT[:, kt, :], kt == 0, kt == 43) for kt in range(44)],
                               reads=ACT_ALL + [("wbuf", ws)], writes=[PB(bk)])
                    evac_y(bk, ot, yb, sq)

                def dst(ot, ap, rd, s):
                    S.dma("pool", xdst[ot, :, b * TB:(b + 1) * TB], ap, reads=rd, writes=[("st_pn", s)])
                post_norm_residual(l, b, A_GPOSTF, xsrc, srcid, dst, es, yb)
                S.barrier()

        def xs_key(t):
            return id(t)

        def main_schedule():
            if stage("s5tab"):
                return
            for b in range(NB):
                phase_p0(b)
            if stage("p0"):
                return
            for l in range(depth):
                for b in range(NB):
                    with ExitStack() as mes:
                        mixT_h[0] = sb(mes, [128, 16, 512], BF16, "mixT")
                        norm_to_hT(l, b, xs[0], A_GPRE)
                        if stage("norm"):
                            return
                        s5gen, s5end = s5_make(l, b, mes)

                        def run_s5(n):
                            for _ in range(n):
                                try:
                                    next(s5gen)
                                except StopIteration:
                                    return
                        phase_hgrn(l, b, run_s5)
                        if stage("hgrn"):
                            S.barrier()
                            return
                        phase_ret(l, b, run_s5)
                        run_s5(32)
                        s5end()
                        S.barrier()
                        if stage("s5"):
                            return
                        if dbg and l == 0:
                            S.dma("pool", dbg_out["mix0"][:, :, b * TB:(b + 1) * TB].rearrange("k p t -> p k t"), mixT_h[0][:], reads=[], writes=["dbgmix"])
                            S.dma("pool", dbg_out["h0"][:, :, b * TB:(b + 1) * TB].rearrange("k p t -> p k t"), hT[:], reads=[], writes=["dbgh"])
                            S.barrier()
                        phase_outproj(l, b, xs[0], 0, xs[1], 1)
                        if dbg and l == 0:
                            S.dma("sp", dbg_out["xmid0"][:, :, b * TB:(b + 1) * TB], xs[1][:, :, b * TB:(b + 1) * TB], reads=[], writes=["dbgxm"])
                            S.barrier()
                    if stage("outproj"):
                        return
                    phase_ffn(l, b, xs[1], 1, xs[0], 0)
                    if stage("ffn"):
                        return
                    if dbg and l == 0:
                        S.dma("sp", dbg_out["xout0"][:, :, b * TB:(b + 1) * TB], xs[0][:, :, b * TB:(b + 1) * TB], reads=[], writes=["dbgxo"])
                        S.barrier()
            for b in range(NB):
                phase_p6(b)

        if not stopped[0]:
            main_schedule()
    except _Stop:
        pass
    S.barrier()
    top.close()
    return nc, S


_CACHE = {}


def kernel(**inputs):
    x = np.ascontiguousarray(inputs["x"], dtype=np.float32)
    B, T, _ = x.shape
    n_cores = B
    key = (T,)
    if key not in _CACHE:
        _CACHE[key] = build_program(T)
    nc, _ = _CACHE[key]
    cst = _build_consts()
    rope = _build_rope(0, T)
    in_maps = []
    for c in range(n_cores):
        m = {"x": x[c], "cst": cst, "rope": rope}
        for n in PARAM_NAMES:
            m[n] = np.ascontiguousarray(inputs[n], dtype=np.float32)
        in_maps.append(m)
    res = run_bass_kernel_spmd(nc, in_maps, core_ids=list(range(n_cores)))
    out = np.stack([np.asarray(res.results[c]["out"]) for c in range(n_cores)], axis=0)
    return out.astype(np.float32)
```

```python
import math
from contextlib import ExitStack

import numpy as np
import concourse.bass as bass
import concourse.mybir as mybir
from concourse.bass_utils import run_bass_kernel_spmd

F32 = mybir.dt.float32
BF16 = mybir.dt.bfloat16
I32 = mybir.dt.int32
AF = mybir.ActivationFunctionType
ALU = mybir.AluOpType

D = 2048
NIN = 5888
DFF = 5632
TB = 512
EPS = 1e-6
NCST = 2304
C_M64, C_M16, C_TPOS, C_CMASK, C_RMASK, C_ROWM, C_GDEC, C_IDENT = 0, 512, 1024, 1536, 1600, 1984, 1992, 2048


class Sched:
    ENG = ("pe", "act", "dve", "pool", "sp")

    def __init__(self, nc):
        self.nc = nc
        self.e = {"pe": nc.tensor, "act": nc.scalar, "dve": nc.vector, "pool": nc.gpsimd, "sp": nc.sync}
        self.sem = {}
        self.cnt = {}
        for k in self.ENG:
            self.sem[k] = nc.alloc_semaphore("sem_" + k)
            self.cnt[k] = 0
        self.seen = {k: {} for k in self.ENG}
        self.dsem = {}
        self.st = {}
        self.ninst = 0

    def _semof(self, key):
        if isinstance(key, str):
            return self.sem[key]
        return self.dsem[key[1]][0]

    def _wait(self, eng, tok):
        if tok is None:
            return
        key, val = tok
        if self.seen[eng].get(key, 0) >= val:
            return
        if key == eng and eng == "pe":
            return
        self.e[eng].wait_ge(self._semof(key), val)
        self.seen[eng][key] = val

    def _deps(self, eng, reads, writes):
        for r in reads:
            s = self.st.get(r)
            if s is not None:
                self._wait(eng, s["w"])
                if isinstance(r, tuple) and r[0] == "pb":
                    for t in s["r"]:
                        if t[0] != eng:
                            self._wait(eng, t)
        for w in writes:
            s = self.st.get(w)
            if s is not None:
                self._wait(eng, s["w"])
                for t in s["r"]:
                    self._wait(eng, t)

    def _commit(self, tok, reads, writes):
        for r in reads:
            s = self.st.setdefault(r, {"w": None, "r": []})
            s["r"] = [t for t in s["r"] if t[0] != tok[0]] + [tok]
        for w in writes:
            self.st[w] = {"w": tok, "r": []}

    def op(self, eng, fn, reads=(), writes=()):
        self._deps(eng, reads, writes)
        ins = fn()
        self.cnt[eng] += 1
        ins.then_inc(self.sem[eng], 1)
        self._commit((eng, self.cnt[eng]), reads, writes)
        self.ninst += 1
        return ins

    def mm_group(self, fns, reads=(), writes=()):
        self._deps("pe", reads, writes)
        ins = None
        for fn in fns:
            ins = fn()
        self.cnt["pe"] += 1
        ins.then_inc(self.sem["pe"], 1)
        self._commit(("pe", self.cnt["pe"]), reads, writes)
        self.ninst += len(fns)
        return ins

    def dma(self, q, out, in_, reads=(), writes=(), semreg=None, **kw):
        reg = semreg if semreg is not None else writes[0]
        if reg not in self.dsem:
            self.dsem[reg] = [self.nc.alloc_semaphore("dsem%d" % len(self.dsem)), 0]
        self._deps(q, reads, writes)
        ins = self.e[q].dma_start(out=out, in_=in_, **kw)
        self.dsem[reg][1] += 16
        ins.then_inc(self.dsem[reg][0], 16)
        self._commit((("d", reg), self.dsem[reg][1]), reads, writes)
        self.ninst += 1
        return ins

    def dma_group(self, q, pairs, reads=(), writes=(), semreg=None, **kw):
        reg = semreg if semreg is not None else writes[0]
        if reg not in self.dsem:
            self.dsem[reg] = [self.nc.alloc_semaphore("dsem%d" % len(self.dsem)), 0]
        self._deps(q, reads, writes)
        for (out, in_) in pairs:
            ins = self.e[q].dma_start(out=out, in_=in_, **kw)
            self.dsem[reg][1] += 16
            ins.then_inc(self.dsem[reg][0], 16)
            self.ninst += 1
        self._commit((("d", reg), self.dsem[reg][1]), reads, writes)

    def barrier(self):
        for e in self.ENG:
            for f in self.ENG:
                if f != e and self.cnt[f] > 0:
                    self._wait(e, (f, self.cnt[f]))
            for reg, (sem, tot) in self.dsem.items():
                if tot > 0:
                    self._wait(e, (("d", reg), tot))
        self.st = {}

    def finish(self, regions, eng="sp"):
        for r in regions:
            s = self.st.get(r)
            if s is not None:
                self._wait(eng, s["w"])


def _build_consts():
    c = np.zeros((128, NCST), np.float32)
    t = np.arange(512)
    c[:, C_M64:C_M64 + 512] = (t % 64 != 0).astype(np.float32)[None, :]
    c[:, C_M16:C_M16 + 512] = (t % 16 != 0).astype(np.float32)[None, :]
    c[:, C_TPOS:C_TPOS + 512] = (t + 1).astype(np.float32)[None, :]
    s = np.arange(128) % 64
    tt = np.arange(64)
    causal = (s[:, None] <= tt[None, :]).astype(np.float64)
    c[:, C_CMASK:C_CMASK + 64] = causal
    for h in range(6):
        lg = np.float32(np.log(np.float32(1.0) - np.float32(2.0) ** np.float32(-5.0 - h)))
        c[:, C_RMASK + 64 * h:C_RMASK + 64 * (h + 1)] = causal * np.exp(-64.0 * float(lg))
    r = np.arange(128)
    for gl in range(8):
        c[:, C_ROWM + gl] = (r // 16 == gl)
    for j in range(3):
        for hh in range(2):
            h = 2 * j + hh
            lg = np.float32(np.log(np.float32(1.0) - np.float32(2.0) ** np.float32(-5.0 - h)))
            c[64 * hh:64 * (hh + 1), C_GDEC + j] = np.exp(64.0 * float(lg))
    c[:, C_IDENT:C_IDENT + 128] = np.eye(128)
    return c


def _build_rope(pos0, nt):
    pos = (pos0 + np.arange(nt)).astype(np.float32)
    inv_freq = (np.float32(1.0) / (np.float32(10000.0) ** np.linspace(0.0, 1.0, 32, dtype=np.float32))).astype(np.float32)
    ang = (pos[None, :] * inv_freq[:, None]).astype(np.float32).astype(np.float64)
    cos, sin = np.cos(ang), np.sin(ang)
    tc = (np.arange(nt) % 64).astype(np.float64)
    out = np.zeros((3, 4, 128, nt), np.float32)
    for j in range(3):
        for hh in range(2):
            h = 2 * j + hh
            lg = float(np.float32(np.log(np.float32(1.0) - np.float32(2.0) ** np.float32(-5.0 - h))))
            gq = np.exp((tc + 1.0) * lg)[None, :]
            gk = np.exp((63.0 - tc) * lg)[None, :] * (64.0 ** -0.5)
            r0 = 64 * hh
            cc = np.concatenate([cos, cos], 0)
            ss = np.concatenate([-sin, sin], 0)
            out[j, 0, r0:r0 + 64] = cc * gq
            out[j, 1, r0:r0 + 64] = ss * gq
            out[j, 2, r0:r0 + 64] = cc * gk
            out[j, 3, r0:r0 + 64] = ss * gk
    return out


def _inproj_chunks():
    ch = []
    for h in range(6):
        ch.append([(h * 128, 128), (768 + h * 128, 128), (2304 + h * 128, 128), (1536 + h * 128, 128)])
    for j in range(3):
        segs = []
        for base in (3584, 3968):
            segs.append((base + 128 * j, 128))
            for hh in range(2):
                h = 2 * j + hh
                segs.append((base + h * 64 + 32, 32))
                segs.append((base + h * 64, 32))
        ch.append(segs)
    for j in range(3):
        ch.append([(4352 + 256 * j, 256), (5120 + 256 * j, 256)])
    ch.append([(3072, 512)])
    return ch


PARAM_NAMES = ["w_in", "w_out", "mix_beta", "hg_lb_logits", "hg_onorm", "s5_lam_re", "s5_lam_im", "s5_log_dt",
               "s5_b_re", "s5_b_im", "s5_c_re", "s5_c_im", "s5_d", "s5_w_glu", "s5_b_glu",
               "ffn_w_up", "ffn_conv_w", "ffn_conv_b", "ffn_w_down",
               "g_pre_mix", "g_post_mix", "g_pre_ffn", "g_post_ffn"]
PARAM_SHAPES = {
    "w_in": [2, 2048, 5888], "w_out": [2, 2048, 2048], "mix_beta": [2, 2048], "hg_lb_logits": [2, 768],
    "hg_onorm": [2, 128], "s5_lam_re": [2, 32, 64], "s5_lam_im": [2, 32, 64], "s5_log_dt": [2, 32],
    "s5_b_re": [2, 32, 64, 16], "s5_b_im": [2, 32, 64, 16], "s5_c_re": [2, 32, 16, 64], "s5_c_im": [2, 32, 16, 64],
    "s5_d": [2, 512], "s5_w_glu": [2, 512, 512], "s5_b_glu": [2, 512], "ffn_w_up": [2, 2048, 11264],
    "ffn_conv_w": [2, 3, 11264], "ffn_conv_b": [2, 11264], "ffn_w_down": [2, 5632, 2048],
    "g_pre_mix": [2, 2048], "g_post_mix": [2, 2048], "g_pre_ffn": [2, 2048], "g_post_ffn": [2, 2048],
}
A_GPRE, A_GPOST, A_GPREF, A_GPOSTF, A_BETA, A_LB0, A_LB1, A_ONORM, A_S5D, A_BGLU, A_N = 0, 16, 32, 48, 64, 80, 86, 92, 93, 97, 101


class _Stop(Exception):
    pass


def build_program(NT, depth=2, dbg=None, stop=None):
    nc = bass.Bass("TRN2", target_bir_lowering=False)
    S = Sched(nc)
    NB = NT // TB
    uid = [0]

    def din(name, shape, dt=F32):
        return nc.dram_tensor(name, list(shape), dt, kind="ExternalInput").ap()

    def dscr(name, shape, dt):
        return nc.dram_tensor(name, list(shape), dt).ap()

    x_in = din("x", [NT, D])
    P = {n: din(n, PARAM_SHAPES[n]) for n in PARAM_NAMES}
    cst_d = din("cst", [128, NCST])
    rope_d = din("rope", [3, 4, 128, NT])
    out_d = nc.dram_tensor("out", [NT, D], F32, kind="ExternalOutput").ap()

    wsi = dscr("wsi", [2, 13, 128, 8192], BF16)
    wso = dscr("wso", [2, 4, 128, 8192], BF16)
    wsu = dscr("wsu", [2, 22, 128, 8192], BF16)
    wsd = dscr("wsd", [2, 16, 128, 44 * 128], BF16)
    wsg = dscr("wsg", [2, 128, 2048], BF16)
    s5tab = dscr("s5tab", [2, 3, 128, 4096], BF16)
    etab = dscr("etab", [2, 32, 2, 128, 512], F32)
    xs = [dscr("xs0", [16, 128, NT], F32), dscr("xs1", [16, 128, NT], F32)]

    dbg_out = {}
    if dbg:
        for nm in ("mix0", "xmid0", "xout0", "h0"):
            dbg_out[nm] = nc.dram_tensor("dbg_" + nm, [16, 128, NT], F32, kind="ExternalOutput").ap()

    stopped = [False]

    def stage(name):
        if stop is not None and stop == name:
            stopped[0] = True
        return stopped[0]

    def sb(es, shape, dt, name="t"):
        uid[0] += 1
        return es.enter_context(nc.sbuf_tensor("%s_%d" % (name, uid[0]), list(shape), dt)).ap()

    top = ExitStack()
    pbank = [nc.alloc_psum_tensor("pb%d" % i, [128, 512], F32).ap() for i in range(8)]

    def PB(i):
        return ("pb", i)

    C = sb(top, [128, NCST], F32, "cst")
    identf = C[:, C_IDENT:C_IDENT + 128]
    identb = sb(top, [128, 128], BF16, "identb")
    onesb = sb(top, [128, 128], BF16, "onesb")
    colsA = sb(top, [128, 2, A_N], F32, "colsA")
    colsB = sb(top, [128, 2, 352], F32, "colsB")
    lbc = sb(top, [128, 2, 6], F32, "lbc")
    omlc = sb(top, [128, 2, 6], F32, "omlc")
    gsc = sb(top, [128, 2, 16], F32, "gsc")
    S_hg = sb(top, [128, 2, 6, 128], F32, "S_hg")
    S_rt = sb(top, [128, 2, 3, 128], F32, "S_rt")
    Wst = sb(top, [128, 2, 2, 32], F32, "Wst")
    halo = sb(top, [128, 2, 88, 2], F32, "halo")
    rho = sb(top, [128, 2, 32], F32, "rho")
    hT = sb(top, [128, 16, 512], BF16, "hT")
    mixT_h = [None]
    wbuf = [sb(top, [128, 8192], BF16, "wbuf%d" % i) for i in range(2)]
    wslot = [0]

    def load_w(src2d, ncols=8192):
        s = wslot[0] % 2
        wslot[0] += 1
        S.dma("sp", wbuf[s][:, 0:ncols], src2d, reads=[], writes=[("wbuf", s)])
        return s

    def act(out, in_, func, reads, writes, scale=1.0, bias=None):
        if bias is None:
            return S.op("act", lambda: nc.scalar.activation(out=out, in_=in_, func=func, scale=scale), reads, writes)
        return S.op("act", lambda: nc.scalar.activation(out=out, in_=in_, func=func, scale=scale, bias=bias), reads, writes)

    def tt(out, in0, in1, op, reads, writes, eng="dve"):
        e = nc.vector if eng == "dve" else nc.gpsimd
        return S.op(eng, lambda: e.tensor_tensor(out=out, in0=in0, in1=in1, op=op), reads, writes)

    def ts(out, in0, s1, s2, op0, op1, reads, writes, eng="dve"):
        e = nc.vector if eng == "dve" else nc.gpsimd
        if op1 is None:
            return S.op(eng, lambda: e.tensor_scalar(out=out, in0=in0, scalar1=s1, scalar2=None, op0=op0), reads, writes)
        return S.op(eng, lambda: e.tensor_scalar(out=out, in0=in0, scalar1=s1, scalar2=s2, op0=op0, op1=op1), reads, writes)

    def stt(out, in0, scalar, in1, op0, op1, reads, writes):
        return S.op("dve", lambda: nc.vector.scalar_tensor_tensor(out=out, in0=in0, scalar=scalar, in1=in1, op0=op0, op1=op1), reads, writes)

    def cp(eng, out, in_, reads, writes):
        if eng == "act":
            return S.op("act", lambda: nc.scalar.copy(out=out, in_=in_), reads, writes)
        e = nc.vector if eng == "dve" else nc.gpsimd
        return S.op(eng, lambda: e.tensor_copy(out=out, in_=in_), reads, writes)

    def mset(eng, ap, val, writes):
        e = nc.vector if eng == "dve" else nc.gpsimd
        return S.op(eng, lambda: e.memset(ap, val), (), writes)

    def mm(out, lhsT, rhs, start=True, stop=True):
        return lambda: nc.tensor.matmul(out, lhsT=lhsT, rhs=rhs, start=start, stop=stop)

    def tr(out, in_, ident):
        return lambda: nc.tensor.transpose(out=out, in_=in_, identity=ident)

    try:
        S.dma("sp", C[:], cst_d[:, :], reads=[], writes=["C"])
        cp("dve", identb[:], identf, ["C"], ["identb"])
        mset("dve", onesb[:], 1.0, ["onesb"])
        for nm, t_ in (("S_hg", S_hg), ("S_rt", S_rt), ("Wst", Wst), ("halo", halo)):
            mset("pool", t_[:], 0.0, [nm])

        with ExitStack() as es:
            stg = sb(es, [128, 128], F32, "stg")
            for l in range(depth):
                mset("dve", stg[:], 0.0, ["stg"])
                rows = [(P["g_pre_mix"][l], 16), (P["g_post_mix"][l], 16), (P["g_pre_ffn"][l], 16), (P["g_post_ffn"][l], 16),
                        (P["mix_beta"][l], 16), (P["hg_lb_logits"][0], 6), (P["hg_lb_logits"][1], 6),
                        (P["hg_onorm"][l], 1), (P["s5_d"][l], 4), (P["s5_b_glu"][l], 4)]
                r0 = 0
                pairs = []
                for src_, n in rows:
                    pairs.append((stg[r0:r0 + n, :], src_.rearrange("(r c) -> r c", c=128)))
                    r0 += n
                S.dma_group("sp", pairs, reads=[], writes=["stg"])
                S.mm_group([tr(pbank[0][:, 0:128], stg[:, :], identf)], reads=["stg", "C"], writes=[PB(0)])
                cp("dve", colsA[:, l, :], pbank[0][:, 0:A_N], [PB(0)], ["colsA"])
                for part in range(3):
                    nrow = 128 if part < 2 else 96
                    mset("dve", stg[:], 0.0, ["stg"])
                    pairs = []
                    r = 0
                    while r < nrow:
                        gr = part * 128 + r
                        if gr < 264:
                            j, t0 = gr // 88, gr % 88
                            n = min(88 - t0, nrow - r)
                            src_ = P["ffn_conv_w"][l, j, t0 * 128:(t0 + n) * 128]
                        else:
                            t0 = gr - 264
                            n = min(88 - t0, nrow - r)
                            src_ = P["ffn_conv_b"][l, t0 * 128:(t0 + n) * 128]
                        pairs.append((stg[r:r + n, :], src_.rearrange("(r c) -> r c", c=128)))
                        r += n
                    S.dma_group("sp", pairs, reads=[], writes=["stg"])
                    S.mm_group([tr(pbank[0][:, 0:128], stg[:, :], identf)], reads=["stg", "C"], writes=[PB(0)])
                    cp("dve", colsB[:, l, part * 128:part * 128 + nrow], pbank[0][:, 0:nrow], [PB(0)], ["colsB"])
            mset("dve", lbc[:], 0.0, ["lbc"])
            if depth > 1:
                tt(lbc[:, 1, :], colsA[:, 0, A_LB1:A_LB1 + 6], colsA[:, 0, A_LB0:A_LB0 + 6], ALU.subtract, ["colsA"], ["lbc"])
                act(lbc[:, 1, :], lbc[:, 1, :], AF.Sigmoid, ["lbc"], ["lbc"])
            ts(omlc[:], lbc[:], -1.0, 1.0, ALU.mult, ALU.add, ["lbc"], ["omlc"])
            for l in range(depth):
                cp("dve", gsc[:, l, :], colsA[:, l, A_BETA:A_BETA + 16], ["colsA"], ["gsc"])
                ts(gsc[:, l, 0:6], gsc[:, l, 0:6], colsA[:, l, A_ONORM:A_ONORM + 1], None, ALU.mult, None, ["gsc", "colsA"], ["gsc"])
            S.barrier()
        stage("setup0")

        with ExitStack() as es:
            NS = 6
            st32 = [sb(es, [128, 2048], F32, "st32") for _ in range(NS)]
            st16 = [sb(es, [128, 2048], BF16, "st16") for _ in range(NS)]
            pc = [0]
            ceng = ["dve", "act", "pool"]

            def conv_piece(srcs, dst, dst_view=None):
                s = pc[0] % NS
                pc[0] += 1
                pairs = []
                for (src_, off, n) in srcs:
                    if off is None:
                        pairs.append((st32[s][:], src_))
                    else:
                        pairs.append((st32[s][:].rearrange("p (k c) -> p k c", c=512)[:, :, off:off + n], src_))
                S.dma_group("sp", pairs, reads=[], writes=[("st32", s)])
                cp(ceng[s % 3], st16[s][:], st32[s][:], [("st32", s)], [("st16", s)])
                srcv = st16[s][:] if dst_view is None else dst_view(st16[s][:])
                S.dma("act", dst, srcv, reads=[("st16", s)], writes=[("wscr", s)])

            chunks = _inproj_chunks()
            for l in range(depth):
                for ci, segs in enumerate(chunks):
                    for kq in range(4):
                        srcs = []
                        off = 0
                        for (c0, n) in segs:
                            srcs.append((P["w_in"][l, kq * 512:(kq + 1) * 512, c0:c0 + n].rearrange("(k p) c -> p k c", p=128), off, n))
                            off += n
                        conv_piece(srcs, wsi[l, ci, :, kq * 2048:(kq + 1) * 2048])
                for ci in range(4):
                    for kq in range(4):
                        src_ = P["w_out"][l, kq * 512:(kq + 1) * 512, ci * 512:(ci + 1) * 512].rearrange("(k p) c -> p k c", p=128)
                        conv_piece([(src_, 0, 512)], wso[l, ci, :, kq * 2048:(kq + 1) * 2048])
                for ci in range(22):
                    for kq in range(4):
                        srcs = []
                        for i, c0 in enumerate((256 * ci, 5632 + 256 * ci)):
                            srcs.append((P["ffn_w_up"][l, kq * 512:(kq + 1) * 512, c0:c0 + 256].rearrange("(k p) c -> p k c", p=128), 256 * i, 256))
                        conv_piece(srcs, wsu[l, ci, :, kq * 2048:(kq + 1) * 2048])
                for kt in range(44):
                    conv_piece([(P["ffn_w_down"][l, kt * 128:(kt + 1) * 128, :], None, 0)],
                               wsd[l].rearrange("o p (k c) -> p o k c", c=128)[:, :, kt, :],
                               dst_view=lambda t: t.rearrange("p (o c) -> p o c", c=128))
                conv_piece([(P["s5_w_glu"][l].rearrange("(k p) c -> p k c", p=128), 0, 512)], wsg[l, :, :])
            S.barrier()

        stage("conv")
        for l in range(depth):
            with ExitStack() as es:
                lam = sb(es, [128, 3, 128], F32, "lam")
                mset("dve", lam[:], 0.0, ["lam"])
                S.dma_group("sp", [(lam[0:32, 0, 0:64], P["s5_lam_re"][l]), (lam[0:32, 0, 64:128], P["s5_lam_re"][l]),
                                   (lam[0:32, 1, 0:64], P["s5_lam_im"][l]), (lam[0:32, 1, 64:128], P["s5_lam_im"][l]),
                                   (lam[0:32, 2, 0:1], P["s5_log_dt"][l].rearrange("(g o) -> g o", o=1))], reads=[], writes=["lam"])
                gm = sb(es, [128, 5, 128], F32, "gm")
                act(gm[0:32, 0, 0:1], lam[0:32, 2, 0:1], AF.Exp, ["lam"], ["gm0"])
                ts(gm[0:32, 1, :], lam[0:32, 0, :], -1e-4, None, ALU.min, None, ["lam"], ["gm1"])
                ts(gm[0:32, 2, :], gm[0:32, 1, :], gm[0:32, 0, 0:1], None, ALU.mult, None, ["gm0", "gm1"], ["gm2"])
                act(gm[0:32, 3, :], gm[0:32, 2, :], AF.Exp, ["gm2"], ["gm3"])
                ts(gm[0:32, 4, :], lam[0:32, 1, :], gm[0:32, 0, 0:1], None, ALU.mult, None, ["lam", "gm0"], ["gm4"])
                cols = sb(es, [128, 4, 32], F32, "s5cols")
                for k_, src_ap in ((0, gm[0:32, 1, :]), (1, lam[0:32, 1, :]), (2, gm[0:32, 3, :]), (3, gm[0:32, 4, :])):
                    S.mm_group([tr(pbank[0][:, 0:32], src_ap, identf[0:32, 0:32])], reads=["gm1", "gm3", "gm4", "lam", "C"], writes=[PB(0)])
                    cp("dve", cols[:, k_, :], pbank[0][:, 0:32], [PB(0)], [("cols", k_)])
                cr_ = [("cols", k_) for k_ in range(4)]
                cp("dve", rho[:, l, :], cols[:, 2, :], cr_, ["rho"])
                wk = sb(es, [128, 12, 32], F32, "s5wk")
                wki = sb(es, [128, 32], I32, "s5wki")

                def sin_of(dst, src_, shift, rd, wr):
                    ts(wk[:, 10, :], src_, shift, 1.0 / (2 * math.pi), ALU.add, ALU.mult, rd, ["wk10"])
                    cp("dve", wki[:], wk[:, 10, :], ["wk10"], ["wki"])
                    cp("dve", wk[:, 11, :], wki[:], ["wki"], ["wk11"])
                    ts(wk[:, 10, :], wk[:, 11, :], -2 * math.pi, shift, ALU.mult, ALU.add, ["wk11"], ["wk10"])
                    tt(wk[:, 10, :], wk[:, 10, :], src_, ALU.add, ["wk10"] + rd, ["wk10"])
                    ts(wk[:, 10, :], wk[:, 10, :], math.pi, -math.pi, ALU.min, ALU.max, ["wk10"], ["wk10"])
                    act(dst, wk[:, 10, :], AF.Sin, ["wk10"], wr)

                sin_of(wk[:, 0, :], cols[:, 3, :], 0.0, cr_, ["wk0"])
                sin_of(wk[:, 1, :], cols[:, 3, :], math.pi / 2, cr_, ["wk1"])
                tt(wk[:, 2, :], wk[:, 1, :], cols[:, 2, :], ALU.mult, ["wk1"] + cr_, ["wk2"])
                tt(wk[:, 3, :], wk[:, 0, :], cols[:, 2, :], ALU.mult, ["wk0"] + cr_, ["wk3"])
                tt(wk[:, 4, :], cols[:, 0, :], cols[:, 0, :], ALU.mult, cr_, ["wk4"])
                tt(wk[:, 5, :], cols[:, 1, :], cols[:, 1, :], ALU.mult, cr_, ["wk5"])
                tt(wk[:, 4, :], wk[:, 4, :], wk[:, 5, :], ALU.add, ["wk4", "wk5"], ["wk4"])
                S.op("dve", lambda: nc.vector.reciprocal(out=wk[:, 4, :], in_=wk[:, 4, :]), ["wk4"], ["wk4"])
                ts(wk[:, 5, :], wk[:, 2, :], -1.0, None, ALU.add, None, ["wk2"], ["wk5"])
                tt(wk[:, 6, :], wk[:, 5, :], cols[:, 0, :], ALU.mult, ["wk5"] + cr_, ["wk6"])
                tt(wk[:, 7, :], wk[:, 3, :], cols[:, 1, :], ALU.mult, ["wk3"] + cr_, ["wk7"])
                tt(wk[:, 6, :], wk[:, 6, :], wk[:, 7, :], ALU.add, ["wk6", "wk7"], ["wk6"])
                tt(wk[:, 6, :], wk[:, 6, :], wk[:, 4, :], ALU.mult, ["wk6", "wk4"], ["wk6"])
                tt(wk[:, 7, :], wk[:, 3, :], cols[:, 0, :], ALU.mult, ["wk3"] + cr_, ["wk7"])
                tt(wk[:, 8, :], wk[:, 5, :], cols[:, 1, :], ALU.mult, ["wk5"] + cr_, ["wk8"])
                tt(wk[:, 7, :], wk[:, 7, :], wk[:, 8, :], ALU.subtract, ["wk7", "wk8"], ["wk7"])
                tt(wk[:, 7, :], wk[:, 7, :], wk[:, 4, :], ALU.mult, ["wk7", "wk4"], ["wk7"])
                bre = sb(es, [128, 32, 16], F32, "bre")
                bim = sb(es, [128, 32, 16], F32, "bim")
                S.dma_group("sp", [(bre[0:64, :, :], P["s5_b_re"][l].rearrange("g p c -> p g c")),
                                   (bim[0:64, :, :], P["s5_b_im"][l].rearrange("g p c -> p g c"))], reads=[], writes=["bre"])
                crb = wk[0:64, 6, :].unsqueeze(2).to_broadcast([64, 32, 16])
                cib = wk[0:64, 7, :].unsqueeze(2).to_broadcast([64, 32, 16])
                bb = sb(es, [128, 5, 32, 16], F32, "bb")
                tt(bb[0:64, 0], bre[0:64], crb, ALU.mult, ["bre", "wk6"], ["bb0"])
                tt(bb[0:64, 2], bim[0:64], cib, ALU.mult, ["bre", "wk7"], ["bb2"])
                tt(bb[0:64, 0], bb[0:64, 0], bb[0:64, 2], ALU.subtract, ["bb0", "bb2"], ["bb0"])
                tt(bb[0:64, 1], bim[0:64], crb, ALU.mult, ["bre", "wk6"], ["bb1"])
                tt(bb[0:64, 3], bre[0:64], cib, ALU.mult, ["bre", "wk7"], ["bb3"])
                tt(bb[0:64, 1], bb[0:64, 1], bb[0:64, 3], ALU.add, ["bb1", "bb3"], ["bb1"])
                ts(bb[0:64, 4], bb[0:64, 1], -1.0, None, ALU.mult, None, ["bb1"], ["bb4"])
                tabs = sb(es, [128, 3, 32, 128], BF16, "s5tabs")
                mset("pool", tabs[:], 0.0, ["tabs"])
                for t4 in range(4):
                    for vi, (ire, iim) in enumerate(((0, 1), (4, 0))):
                        S.mm_group([tr(pbank[1][:, 0:64], bb[0:64, ire, t4 * 8:(t4 + 1) * 8, :], identf[0:64, 0:64]),
                                    tr(pbank[1][:, 64:128], bb[0:64, iim, t4 * 8:(t4 + 1) * 8, :], identf[0:64, 0:64])],
                                   reads=["bb0", "bb1", "bb4", "C"], writes=[PB(1)])
                        for gl in range(8):
                            g = t4 * 8 + gl
                            ts(tabs[:, vi, g, :], pbank[1][:, 0:128], C[:, C_ROWM + gl:C_ROWM + gl + 1], None, ALU.mult, None, [PB(1), "C", "tabs"], [("tabs", vi, g)])
                cst_ = sb(es, [128, 4, 2, 64], F32, "cstage")
                S.dma_group("sp", [(cst_[:, :, 0, :], P["s5_c_re"][l].rearrange("(t g) c p -> (g c) t p", t=4)),
                                   (cst_[:, :, 1, :], P["s5_c_im"][l].rearrange("(t g) c p -> (g c) t p", t=4))], reads=[], writes=["cst_"])
                for t4 in range(4):
                    S.mm_group([tr(pbank[2][:, 0:128], cst_[:, t4, :, :], identf)], reads=["cst_", "C"], writes=[PB(2)])
                    for gl in range(8):
                        g = t4 * 8 + gl
                        cp("dve", tabs[0:64, 2, g, gl * 16:(gl + 1) * 16], pbank[2][0:64, gl * 16:(gl + 1) * 16], [PB(2), "tabs"], [("tabs", 2, g, 0)])
                        ts(tabs[64:128, 2, g, gl * 16:(gl + 1) * 16], pbank[2][64:128, gl * 16:(gl + 1) * 16], -1.0, None, ALU.mult, None, [PB(2), "tabs"], [("tabs", 2, g, 1)])
                S.barrier()
                S.dma("sp", s5tab[l].rearrange("v p f -> p v f"), tabs[:].rearrange("p v g c -> p v (g c)"), reads=[], writes=["s5tab"])
                ework = [sb(es, [128, 512], F32, "ew") for _ in range(3)]
                ewi = sb(es, [128, 512], I32, "ewi")
                eout = [sb(es, [128, 2, 512], F32, "eout") for _ in range(2)]
                tpos = C[:, C_TPOS:C_TPOS + 512]
                for g in range(32):
                    eo = eout[g % 2]
                    ts(ework[0][:], tpos, cols[:, 3, g:g + 1], None, ALU.mult, None, ["C"], ["ew0"])
                    for which, shift, sgn in ((0, math.pi / 2, 1.0), (1, 0.0, -1.0)):
                        ts(ework[1][:], ework[0][:], shift, 1.0 / (2 * math.pi), ALU.add, ALU.mult, ["ew0"], ["ew1"])
                        cp("dve", ewi[:], ework[1][:], ["ew1"], ["ewi"])
                        cp("dve", ework[2][:], ewi[:], ["ewi"], ["ew2"])
                        ts(ework[1][:], ework[2][:], -2 * math.pi, shift, ALU.mult, ALU.add, ["ew2"], ["ew1"])
                        tt(ework[1][:], ework[1][:], ework[0][:], ALU.add, ["ew1", "ew0"], ["ew1"])
                        ts(ework[1][:], ework[1][:], math.pi, -math.pi, ALU.min, ALU.max, ["ew1"], ["ew1"])
                        act(eo[:, which, :], ework[1][:], AF.Sin, ["ew1"], [("eo", g % 2, which)], scale=sgn)
                    S.dma("pool", etab[l, g].rearrange("w p t -> p w t"), eo[:], reads=[("eo", g % 2, 0), ("eo", g % 2, 1)], writes=[("etab", g % 2)])
                S.barrier()

        def rstd_from(pss_bank, n, out_ap, wr):
            act(out_ap, pbank[pss_bank][:], AF.Ln, [PB(pss_bank)], wr, scale=1.0 / n, bias=EPS)
            act(out_ap, out_ap, AF.Exp, wr, wr, scale=-0.5)

        def phase_p0(b):
            with ExitStack() as es:
                xin = sb(es, [128, 4, D], F32, "xin")
                xo = [sb(es, [128, 512], F32, "xo") for _ in range(4)]
                S.dma_group("sp", [(xin[:, grp, :], x_in[b * TB + grp * 128:b * TB + (grp + 1) * 128, :]) for grp in range(4)], reads=[], writes=["xin"])
                for ft in range(16):
                    bk = ft % 8
                    S.mm_group([tr(pbank[bk][:, grp * 128:(grp + 1) * 128], xin[:, grp, ft * 128:(ft + 1) * 128], identf) for grp in range(4)],
                               reads=["xin", "C"], writes=[PB(bk)])
                    cp("act" if ft % 2 else "dve", xo[ft % 4][:], pbank[bk][:], [PB(bk)], [("xo", ft % 4)])
                    S.dma("pool", xs[0][ft, :, b * TB:(b + 1) * TB], xo[ft % 4][:], reads=[("xo", ft % 4)], writes=[("st_xo", ft % 4)])
                S.barrier()

        def phase_p6(b):
            with ExitStack() as es:
                xf = sb(es, [128, 16, 512], F32, "xf")
                xt_ = [sb(es, [128, D], F32, "xtok") for _ in range(2)]
                S.dma_group("sp", [(xf[:, ft, :], xs[0][ft, :, b * TB:(b + 1) * TB]) for ft in range(16)], reads=[], writes=["xf"])
                for grp in range(4):
                    for q4 in range(4):
                        bk = (grp % 2) * 4 + q4
                        S.mm_group([tr(pbank[bk][:, i * 128:(i + 1) * 128], xf[:, q4 * 4 + i, grp * 128:(grp + 1) * 128], identf) for i in range(4)],
                                   reads=["xf", "C"], writes=[PB(bk)])
                        cp("act" if q4 % 2 else "dve", xt_[grp % 2][:, q4 * 512:(q4 + 1) * 512], pbank[bk][:], [PB(bk)], [("xtok", grp % 2, q4)])
                    S.dma("pool", out_d[b * TB + grp * 128:b * TB + (grp + 1) * 128, :], xt_[grp % 2][:],
                          reads=[("xtok", grp % 2, q4) for q4 in range(4)], writes=[("st_out", grp % 2)])
                S.barrier()

        def norm_to_hT(l, b, src, gcol0):
            with ExitStack() as es:
                xT = sb(es, [128, 16, 512], F32, "xT")
                sq = [sb(es, [128, 512], BF16, "sq") for _ in range(2)]
                rstd = sb(es, [128, 512], F32, "rstd")
                S.dma_group("sp", [(xT[:, ft, :], src[ft, :, b * TB:(b + 1) * TB]) for ft in range(16)], reads=[], writes=["xT"])
                for ft in range(16):
                    act(sq[ft % 2][:], xT[:, ft, :], AF.Square, ["xT"], [("sq", ft % 2)])
                    S.mm_group([mm(pbank[7][:], onesb[:], sq[ft % 2][:], ft == 0, ft == 15)], reads=[("sq", ft % 2), "onesb"], writes=[PB(7)])
                rstd_from(7, D, rstd[:], ["rstd"])
                for ft in range(16):
                    stt(hT[:, ft, :], xT[:, ft, :], colsA[:, l, gcol0 + ft:gcol0 + ft + 1], rstd[:], ALU.mult, ALU.mult,
                        ["xT", "rstd", "colsA"], [("hT", ft)])
                S.barrier()

        HT_ALL = [("hT", ft) for ft in range(16)]

        def proj_fm(bank, ws, col0, ncols=128):
            wv = wbuf[ws][:].rearrange("p (k c) -> p k c", c=512)
            S.mm_group([mm(pbank[bank][0:ncols, :], wv[:, kt, col0:col0 + ncols], hT[:, kt, :], kt == 0, kt == 15) for kt in range(16)],
                       reads=HT_ALL + [("wbuf", ws)], writes=[PB(bank)])

        def proj_tm(bank, ws, col0, ncols, width):
            wv = wbuf[ws][:].rearrange("p (k c) -> p k c", c=512)
            pv = pbank[bank][:].rearrange("p (g c) -> p g c", c=width)
            fns = []
            for grp in range(4):
                for kt in range(16):
                    fns.append(mm(pv[:, grp, 0:ncols], hT[:, kt, grp * 128:(grp + 1) * 128], wv[:, kt, col0:col0 + ncols], kt == 0, kt == 15))
            S.mm_group(fns, reads=HT_ALL + [("wbuf", ws)], writes=[PB(bank)])

        def head_norm_out(l, po_bank, pss_bank, gate, mt, es_tiles, rd_gate):
            sqb, rs, tmp = es_tiles
            act(sqb[:], pbank[po_bank][:], AF.Square, [PB(po_bank)], ["hn_sq"])
            S.mm_group([mm(pbank[pss_bank][:], onesb[:], sqb[:])], reads=["hn_sq", "onesb"], writes=[PB(pss_bank)])
            rstd_from(pss_bank, 128, rs[:], ["hn_rs"])
            tt(tmp[:], pbank[po_bank][:], rs[:], ALU.mult, [PB(po_bank), "hn_rs"], ["hn_tmp"])
            stt(mixT_h[0][:, mt, :], tmp[:], gsc[:, l, mt:mt + 1], gate, ALU.mult, ALU.mult, ["hn_tmp", "gsc"] + rd_gate, [("mixT", mt)])

        def phase_hgrn(l, b):
            with ExitStack() as es:
                F = lambda nm: sb(es, [128, 512], F32, nm)
                Bt = lambda nm: sb(es, [128, 512], BF16, nm)
                qf, t1, lf, kk, b64, b16, e16, e64, tmpd = [F(n) for n in ("qf", "t1", "lf", "kk", "b64", "b16", "e16", "e64", "tmpd")]
                tk = sb(es, [128, 4, 512], F32, "tk")
                gate = [F("gate0"), F("gate1")]
                qh = [Bt("qh0"), Bt("qh1")]
                qt = [Bt("qt0"), Bt("qt1")]
                kdec = [Bt("kdec0"), Bt("kdec1")]
                vt = [Bt("vt0"), Bt("vt1")]
                khat = [sb(es, [128, 8, 4, 64], BF16, "khat%d" % i) for i in range(2)]
                dec = [sb(es, [128, 8], F32, "dec%d" % i) for i in range(2)]
                kdt, sqb = Bt("kdt"), Bt("sqb")
                AT = sb(es, [128, 4, 128], BF16, "AT")
                Sbf = sb(es, [128, 8, 128], BF16, "Sbf")
                rs, tmpo = F("rs"), F("tmpo")
                for i in range(2):
                    mset("pool", khat[i][:], 0.0, [("khat", i)])
                mset("pool", AT[:], 0.0, ["AT"])
                m64 = C[:, C_M64:C_M64 + 512]
                m16 = C[:, C_M16:C_M16 + 512]
                cmask = C[:, C_CMASK:C_CMASK + 64]
                b64v = b64[:].rearrange("p (c j) -> p c j", j=64)
                kkv = kk[:].rearrange("p (c j) -> p c j", j=64)
                wsl = {}
                wsl[0] = load_w(wsi[l, 0, :, :])

                def front(h):
                    s = h % 2
                    ws = wsl[h]
                    if h < 5:
                        wsl[h + 1] = load_w(wsi[l, h + 1, :, :])
                    proj_fm(0, ws, 0)
                    proj_fm(1, ws, 128)
                    yield
                    proj_fm(2, ws, 256)
                    proj_tm(3, ws, 384, 128, 128)
                    yield
                    act(qf[:], pbank[0][:], AF.Silu, [PB(0)], ["qf"])
                    act(t1[:], pbank[1][:], AF.Sigmoid, [PB(1)], ["t1"])
                    yield
                    act(kk[:], pbank[1][:], AF.Sigmoid, [PB(1)], ["kk"], scale=-1.0)
                    act(gate[s][:], pbank[2][:], AF.Silu, [PB(2)], [("gate", s)])
                    yield
                    cp("act", vt[s][:], pbank[3][:], [PB(3)], [("vt", s)])
                    ts(t1[:], t1[:], omlc[:, l, h:h + 1], lbc[:, l, h:h + 1], ALU.mult, ALU.add, ["t1", "omlc", "lbc"], ["t1"])
                    yield
                    act(lf[:], t1[:], AF.Ln, ["t1"], ["lf"])
                    ts(kk[:], kk[:], omlc[:, l, h:h + 1], None, ALU.mult, None, ["kk", "omlc"], ["kk"])
                    yield
                    S.op("dve", lambda: nc.vector.tensor_tensor_scan(out=b64[:], data0=m64, data1=lf[:], initial=0.0, op0=ALU.mult, op1=ALU.add), ["lf", "C"], ["b64"])
                    yield
                    S.op("dve", lambda: nc.vector.tensor_tensor_scan(out=b16[:], data0=m16, data1=lf[:], initial=0.0, op0=ALU.mult, op1=ALU.add), ["lf", "C"], ["b16"])
                    yield
                    act(e16[:], b16[:], AF.Exp, ["b16"], ["e16"])
                    act(e64[:], b64[:], AF.Exp, ["b64"], ["e64"])
                    yield
                    tt(qh[s][:], qf[:], e16[:], ALU.mult, ["qf", "e16"], [("qh", s)])
                    yield
                    tt(qt[s][:], qf[:], e64[:], ALU.mult, ["qf", "e64"], [("qt", s)])
                    yield
                    tt(tmpd[:].rearrange("p (c j) -> p c j", j=64), b64v[:, :, 63:64].to_broadcast([128, 8, 64]), b64v, ALU.subtract, ["b64"], ["tmpd"])
                    act(tmpd[:], tmpd[:], AF.Exp, ["tmpd"], ["tmpd"])
                    yield
                    tt(kdec[s][:], kk[:], tmpd[:], ALU.mult, ["kk", "tmpd"], [("kdec", s)])
                    act(dec[s][:], b64v[:, :, 63], AF.Exp, ["b64"], [("dec", s)])
                    yield
                    for i in range(4):
                        n = 16 * (i + 1)
                        tkv = tk[:, i, :].rearrange("p (c j) -> p c j", j=64)
                        if i == 0:
                            act(tkv[:, :, 0:n], b64v[:, :, 0:n], AF.Exp, ["b64"], [("tk", i)], scale=-1.0)
                        else:
                            tt(tkv[:, :, 0:n], b64v[:, :, 16 * i - 1:16 * i].to_broadcast([128, 8, n]), b64v[:, :, 0:n], ALU.subtract, ["b64"], [("tk", i)])
                            act(tkv[:, :, 0:n], tkv[:, :, 0:n], AF.Exp, [("tk", i)], [("tk", i)])
                        tt(khat[s][:, :, i, 0:n], kkv[:, :, 0:n], tkv[:, :, 0:n], ALU.mult, ["kk", ("tk", i)], [("khat", s)])
                        yield

                def back(h):
                    s = h % 2
                    fns = []
                    for c in range(8):
                        j2, par = c // 2, c % 2
                        for i in range(4):
                            fns.append(mm(pbank[4][par * 64:(par + 1) * 64, j2 * 64 + 16 * i:j2 * 64 + 16 * i + 16], khat[s][:, c, i, :],
                                          qh[s][:, c * 64 + 16 * i:c * 64 + 16 * i + 16]))
                    S.mm_group(fns, reads=[("khat", s), ("qh", s)], writes=[PB(4)])
                    yield
                    pAv = pbank[4][:, 0:256].rearrange("p (j t) -> p j t", t=64)
                    for par in range(2):
                        sl = slice(par * 64, (par + 1) * 64)
                        tt(AT[sl, :, par * 64:(par + 1) * 64], pAv[sl, :, :], cmask[sl, :].unsqueeze(1).to_broadcast([64, 4, 64]), ALU.mult,
                           [PB(4), "C"], ["AT"])
                    yield
                    p4b = pbank[4][:].bitcast(BF16)
                    S.mm_group([tr(p4b[:, grp * 128:(grp + 1) * 128], kdec[s][:, grp * 128:(grp + 1) * 128], identb[:]) for grp in range(4)],
                               reads=[("kdec", s), "identb"], writes=[PB(4)])
                    cp("act", kdt[:], p4b[:, 0:512], [PB(4)], ["kdt"])
                    yield
                    for par in range(2):
                        fns = []
                        sl = slice(par * 64, (par + 1) * 64)
                        for j2 in range(4):
                            fns.append(mm(pbank[5 + par][:, j2 * 128:(j2 + 1) * 128], kdt[sl, j2 * 128:(j2 + 1) * 128], vt[s][sl, j2 * 128:(j2 + 1) * 128]))
                        S.mm_group(fns, reads=["kdt", ("vt", s)], writes=[PB(5 + par)])
                    yield
                    Sreg = ("S_hg", l, h)
                    for c in range(8):
                        cp("act", Sbf[:, c, :], S_hg[:, l, h, :], [Sreg, "S_hg"], [("Sbf", c)])
                        stt(S_hg[:, l, h, :], S_hg[:, l, h, :], dec[s][:, c:c + 1], pbank[5 + c % 2][:, (c // 2) * 128:(c // 2 + 1) * 128], ALU.mult, ALU.add,
                            [Sreg, "S_hg", ("dec", s), PB(5 + c % 2)], [Sreg])
                        yield
                    for j2 in range(4):
                        S.mm_group([mm(pbank[7][:, j2 * 128:(j2 + 1) * 128], vt[s][:, j2 * 128:(j2 + 1) * 128], AT[:, j2, :], True, False),
                                    mm(pbank[7][:, (2 * j2) * 64:(2 * j2 + 1) * 64], Sbf[:, 2 * j2, :], qt[s][:, (2 * j2) * 64:(2 * j2 + 1) * 64], False, False),
                                    mm(pbank[7][:, (2 * j2 + 1) * 64:(2 * j2 + 2) * 64], Sbf[:, 2 * j2 + 1, :], qt[s][:, (2 * j2 + 1) * 64:(2 * j2 + 2) * 64], False, True)],
                                   reads=[("vt", s), "AT", ("Sbf", 2 * j2), ("Sbf", 2 * j2 + 1), ("qt", s)], writes=[PB(7)])
                    yield
                    head_norm_out(l, 7, 4, gate[s][:], h, (sqb, rs, tmpo), [("gate", s)])
                    yield

                def drain(gens):
                    gens = list(gens)
                    while gens:
                        for g_ in list(gens):
                            try:
                                next(g_)
                            except StopIteration:
                                gens.remove(g_)

                drain([front(0)])
                for h in range(6):
                    drain([back(h)] + ([front(h + 1)] if h < 5 else []))
                S.barrier()

        def phase_ret(l, b, run_s5):
            with ExitStack() as es:
                F = lambda nm: sb(es, [128, 512], F32, nm)
                Bt = lambda nm: sb(es, [128, 512], BF16, nm)
                tab = sb(es, [128, 4, 512], F32, "rtab")
                ta, tb_ = F("ta"), F("tb")
                qd, kd, kdt, sqb = [Bt(n) for n in ("qd", "kd", "kdt", "sqb")]
                qdz = [Bt("qdz0"), Bt("qdz1")]
                vt = sb(es, [128, 4, 256], BF16, "vtr")
                gates = [F("g0"), F("g1")]
                PT = [sb(es, [128, 4, 128], BF16, "PT%d" % i) for i in range(2)]
                Sbf = sb(es, [128, 8, 128], BF16, "Sbfr")
                rs, tmpo = F("rs"), F("tmpo")
                for t_ in (qdz[0], qdz[1], PT[0], PT[1]):
                    mset("pool", t_[:], 0.0, ["rz"])
                for j in range(3):
                    wa = load_w(wsi[l, 6 + j, :, :])
                    wb = load_w(wsi[l, 9 + j, :, :])
                    S.dma_group("sp", [(tab[:, w, :], rope_d[j, w, :, b * TB:(b + 1) * TB]) for w in range(4)], reads=[], writes=["rtab"])
                    for w in range(4):
                        proj_fm(w, wa, 128 * w)
                    for (z, zp, cq, sq_, dst, nm) in ((0, 1, 0, 1, qd, "qd"), (2, 3, 2, 3, kd, "kd")):
                        tt(ta[:], pbank[z][:], tab[:, cq, :], ALU.mult, [PB(z), "rtab"], ["ta"])
                        tt(tb_[:], pbank[zp][:], tab[:, sq_, :], ALU.mult, [PB(zp), "rtab"], ["tb"])
                        tt(dst[:], ta[:], tb_[:], ALU.add, ["ta", "tb"], [nm])
                        if nm == "qd":
                            for hh in range(2):
                                sl = slice(64 * hh, 64 * hh + 64)
                                tt(qdz[hh][sl, :], ta[sl, :], tb_[sl, :], ALU.add, ["ta", "tb", "rz"], [("qdz", hh)])
                    wv = wbuf[wb][:].rearrange("p (k c) -> p k c", c=512)
                    fns = []
                    for grp in range(4):
                        pvv = pbank[4 - grp // 2][:, (grp % 2) * 256:(grp % 2 + 1) * 256]
                        for kt in range(16):
                            fns.append(mm(pvv, hT[:, kt, grp * 128:(grp + 1) * 128], wv[:, kt, 0:256], kt == 0, kt == 15))
                    S.mm_group(fns, reads=HT_ALL + [("wbuf", wb)], writes=[PB(4), PB(3)])
                    for half in range(2):
                        cp("act", vt[:, 2 * half:2 * half + 2, :], pbank[4 - half][:].rearrange("p (g c) -> p g c", c=256), [PB(4), PB(3)], [("vtr", half)])
                    for hh in range(2):
                        proj_fm(hh, wb, 256 + 128 * hh)
                        act(gates[hh][:], pbank[hh][:], AF.Silu, [PB(hh)], [("gate", hh)])
                    run_s5(3)
                    VT = [("vtr", 0), ("vtr", 1)]
                    p3b = pbank[2][:].bitcast(BF16)
                    S.mm_group([tr(p3b[:, grp * 128:(grp + 1) * 128], kd[:, grp * 128:(grp + 1) * 128], identb[:]) for grp in range(4)],
                               reads=["kd", "identb"], writes=[PB(2)])
                    cp("act", kdt[:], p3b[:, 0:512], [PB(2)], ["kdt"])
                    for par in range(2):
                        fns = []
                        sl = slice(par * 64, (par + 1) * 64)
                        for j2 in range(4):
                            for hh in range(2):
                                fns.append(mm(pbank[3 - par][64 * hh:64 * hh + 64, j2 * 128:(j2 + 1) * 128],
                                              kdt[sl, j2 * 128 + 64 * hh:j2 * 128 + 64 * hh + 64], vt[sl, j2, 128 * hh:128 * hh + 128]))
                        S.mm_group(fns, reads=["kdt"] + VT, writes=[PB(3 - par)])
                    run_s5(3)
                    Sreg = ("S_rt", l, j)
                    for c in range(8):
                        cp("act", Sbf[:, c, :], S_rt[:, l, j, :], [Sreg, "S_rt"], [("Sbf", c)])
                        stt(S_rt[:, l, j, :], S_rt[:, l, j, :], C[:, C_GDEC + j:C_GDEC + j + 1], pbank[3 - c % 2][:, (c // 2) * 128:(c // 2 + 1) * 128],
                            ALU.mult, ALU.add, [Sreg, "S_rt", "C", PB(3 - c % 2)], [Sreg])
                    for hh in range(2):
                        h = 2 * j + hh
                        fns = []
                        for c in range(8):
                            j2, par = c // 2, c % 2
                            fns.append(mm(pbank[4][par * 64:(par + 1) * 64, j2 * 64:(j2 + 1) * 64], kd[:, c * 64:(c + 1) * 64], qdz[hh][:, c * 64:(c + 1) * 64]))
                        S.mm_group(fns, reads=["kd", ("qdz", hh), "rz"], writes=[PB(4)])
                        pSv = pbank[4][:, 0:256].rearrange("p (j t) -> p j t", t=64)
                        rm = C[:, C_RMASK + 64 * h:C_RMASK + 64 * (h + 1)]
                        for par in range(2):
                            sl = slice(par * 64, (par + 1) * 64)
                            tt(PT[hh][sl, :, par * 64:(par + 1) * 64], pSv[sl, :, :], rm[sl, :].unsqueeze(1).to_broadcast([64, 4, 64]), ALU.mult,
                               [PB(4), "C", "rz"], [("PT", hh)])
                        ob = hh
                        for j2 in range(4):
                            S.mm_group([mm(pbank[ob][:, j2 * 128:(j2 + 1) * 128], vt[:, j2, 128 * hh:128 * hh + 128], PT[hh][:, j2, :], True, False),
                                        mm(pbank[ob][:, (2 * j2) * 64:(2 * j2 + 1) * 64], Sbf[:, 2 * j2, :], qdz[hh][:, (2 * j2) * 64:(2 * j2 + 1) * 64], False, False),
                                        mm(pbank[ob][:, (2 * j2 + 1) * 64:(2 * j2 + 2) * 64], Sbf[:, 2 * j2 + 1, :], qdz[hh][:, (2 * j2 + 1) * 64:(2 * j2 + 2) * 64], False, True)],
                                       reads=VT + [("PT", hh), ("Sbf", 2 * j2), ("Sbf", 2 * j2 + 1), ("qdz", hh), ("gate", hh)], writes=[PB(ob)])
                        head_norm_out(l, ob, 4, gates[hh][:], 10 + h, (sqb, rs, tmpo), [("gate", hh)])
                        run_s5(3)
                S.barrier()

        def s5_make(l, b, es):
            F = lambda nm: sb(es, [128, 512], F32, nm)
            tabs2 = [sb(es, [128, 3, 8, 128], BF16, "s5t") for _ in range(2)]
            r1, r2_ = F("r1"), F("r2")
            r3 = sb(es, [128, 2], F32, "r3")
            wg = sb(es, [128, 4, 512], BF16, "wglu")
            uf = sb(es, [128, 4, 512], F32, "uf")
            ub = sb(es, [128, 4, 512], BF16, "ub")
            et = [sb(es, [128, 2, 512], F32, "et") for _ in range(2)]
            t1, t2, t3, t4, xa, xb_, Wa, Wb = [F(n) for n in ("t1", "t2", "t3", "t4", "xa", "xb", "Wa", "Wb")]
            Sg = [sb(es, [128, 512], BF16, "Sg") for _ in range(2)]
            ya = F("ya")
            yg = sb(es, [128, 4, 512], F32, "yg")
            ygb = sb(es, [128, 4, 512], BF16, "ygb")
            sg_ = F("sg")
            S.dma("sp", wg[:].rearrange("p k c -> p (k c)"), wsg[l, :, :], reads=[], writes=["wglu"])
            ws = load_w(wsi[l, 12, :, :])
            for t in range(4):
                proj_fm(t, ws, 128 * t)
                cp("act", uf[:, t, :], pbank[t][:], [PB(t)], [("uf", t)])
                cp("dve", ub[:, t, :], uf[:, t, :], [("uf", t)], [("ub", t)])

            def y_accum(g):
                t, gl = g // 8, g % 8
                S.mm_group([mm(pbank[7][:], tabs_of(g)[:, 2, gl, :], Sg[g % 2][:], gl == 0, gl == 7)], reads=[("s5t", t % 2), ("Sg", g % 2)], writes=[PB(7)])
                if gl == 7:
                    stt(ya[:], uf[:, t, :], colsA[:, l, A_S5D + t:A_S5D + t + 1], pbank[7][:], ALU.mult, ALU.add, [("uf", t), "colsA", PB(7)], ["ya"])
                    act(yg[:, t, :], ya[:], AF.Gelu, ["ya"], [("yg", t)])
                    cp("dve", ygb[:, t, :], yg[:, t, :], [("yg", t)], [("ygb", t)])

            def tabs_of(g):
                return tabs2[(g // 8) % 2]

            def gen():
                for g in range(32):
                    t, gl = g // 8, g % 8
                    tb_ = tabs_of(g)
                    if gl == 0:
                        S.dma("sp", tb_[:].rearrange("p v g c -> p v (g c)"), s5tab[l, :, :, t * 1024:(t + 1) * 1024].rearrange("v p f -> p v f"),
                              reads=[], writes=[("s5t", t % 2)])
                    e = et[g % 2]
                    S.dma("sp", e[:], etab[l, g].rearrange("w p t -> p w t"), reads=[], writes=[("et", g % 2)])
                    ER = [("et", g % 2)]
                    Ec, Es = e[:, 0, :], e[:, 1, :]
                    S.mm_group([mm(pbank[5][:], tb_[:, 0, gl, :], ub[:, t, :])], reads=[("s5t", t % 2), ("ub", t)], writes=[PB(5)])
                    S.mm_group([mm(pbank[6][:], tb_[:, 1, gl, :], ub[:, t, :])], reads=[("s5t", t % 2), ("ub", t)], writes=[PB(6)])
                    if g >= 1:
                        y_accum(g - 1)
                    tt(t1[:], pbank[5][:], Ec, ALU.mult, [PB(5)] + ER, ["t1"])
                    tt(t4[:], pbank[5][:], Es, ALU.mult, [PB(5)] + ER, ["t4"])
                    tt(t2[:], pbank[6][:], Es, ALU.mult, [PB(6)] + ER, ["t2"])
                    tt(t3[:], pbank[6][:], Ec, ALU.mult, [PB(6)] + ER, ["t3"])
                    tt(xa[:], t1[:], t2[:], ALU.add, ["t1", "t2"], ["xa"], eng="pool")
                    tt(xb_[:], t3[:], t4[:], ALU.subtract, ["t3", "t4"], ["xb"], eng="pool")
                    rh = rho[:, l, g:g + 1].to_broadcast([128, 512])
                    S.op("dve", lambda: nc.vector.tensor_tensor_scan(out=Wa[:], data0=rh, data1=xa[:], initial=Wst[:, l, 0, g:g + 1], op0=ALU.mult, op1=ALU.add),
                         ["xa", "rho", ("Wst", l, g)], ["Wa"])
                    S.op("dve", lambda: nc.vector.tensor_tensor_scan(out=Wb[:], data0=rh, data1=xb_[:], initial=Wst[:, l, 1, g:g + 1], op0=ALU.mult, op1=ALU.add),
                         ["xb", "rho", ("Wst", l, g)], ["Wb"])
                    tt(r1[:], Wa[:], Ec, ALU.mult, ["Wa"] + ER, ["r1"], eng="pool")
                    tt(r2_[:], Wb[:], Es, ALU.mult, ["Wb"] + ER, ["r2"], eng="pool")
                    tt(Sg[g % 2][:], r1[:], r2_[:], ALU.subtract, ["r1", "r2"], [("Sg", g % 2)])
                    tt(Wst[:, l, 0, g:g + 1], r1[:, 511:512], r2_[:, 511:512], ALU.subtract, ["r1", "r2"], [("Wst", l, g)], eng="pool")
                    tt(r3[:, 0:1], Wb[:, 511:512], e[:, 0, 511:512], ALU.mult, ["Wb"] + ER, ["r3"], eng="pool")
                    tt(r3[:, 1:2], Wa[:, 511:512], e[:, 1, 511:512], ALU.mult, ["Wa"] + ER, ["r3"], eng="pool")
                    tt(Wst[:, l, 1, g:g + 1], r3[:, 0:1], r3[:, 1:2], ALU.add, ["r3", ("Wst", l, g)], [("Wst", l, g)], eng="pool")
                    yield
                y_accum(31)

            def end():
                for ot in range(4):
                    bk = 5 + ot % 2
                    S.mm_group([mm(pbank[bk][:], wg[:, kt, ot * 128:(ot + 1) * 128], ygb[:, kt, :], kt == 0, kt == 3) for kt in range(4)],
                               reads=["wglu"] + [("ygb", k_) for k_ in range(4)], writes=[PB(bk)])
                    act(sg_[:], pbank[bk][:], AF.Sigmoid, [PB(bk), "colsA"], ["sg"], bias=colsA[:, l, A_BGLU + ot:A_BGLU + ot + 1])
                    stt(mixT_h[0][:, 6 + ot, :], yg[:, ot, :], gsc[:, l, 6 + ot:7 + ot], sg_[:], ALU.mult, ALU.mult, [("yg", ot), "gsc", "sg"], [("mixT", 6 + ot)])

            return gen(), end

        def post_norm_residual(l, b, gcol0, xsrc, srcid, dstfn, es, yb):
            rstd = sb(es, [128, 512], F32, "rstd2")
            xr = [sb(es, [128, 512], F32, "xr") for _ in range(2)]
            tmp = [sb(es, [128, 512], F32, "pn_tmp") for _ in range(2)]
            rstd_from(7, D, rstd[:], ["rstd2"])
            for ot in range(16):
                s = ot % 2
                S.dma("sp", xr[s][:], xsrc[ot, :, b * TB:(b + 1) * TB], reads=[], writes=[("xr", s)])
                stt(tmp[s][:], yb[:, ot, :], colsA[:, l, gcol0 + ot:gcol0 + ot + 1], rstd[:], ALU.mult, ALU.mult, [("yb", ot), "colsA", "rstd2"], [("pn_tmp", s)])
                tt(tmp[s][:], tmp[s][:], xr[s][:], ALU.add, [("pn_tmp", s), ("xr", s)], [("pn_tmp", s)], eng="pool")
                dstfn(ot, tmp[s][:], [("pn_tmp", s)], s)

        def evac_y(bank, ot, yb, sq):
            cp("act", yb[:, ot, :], pbank[bank][:], [PB(bank)], [("yb", ot)])
            act(sq[ot % 3][:], pbank[bank][:], AF.Square, [PB(bank)], [("sq2", ot % 3)])
            if ot >= 1:
                o1 = ot - 1
                S.mm_group([mm(pbank[7][:], onesb[:], sq[o1 % 3][:], o1 == 0, False)], reads=[("sq2", o1 % 3), "onesb"], writes=[PB(7)])
            if ot == 15:
                S.mm_group([mm(pbank[7][:], onesb[:], sq[ot % 3][:], False, True)], reads=[("sq2", ot % 3), "onesb"], writes=[PB(7)])

        def phase_outproj(l, b, xsrc, srcid, xdst, dstid):
            with ExitStack() as es:
                yb = sb(es, [128, 16, 512], F32, "yb")
                sq = [sb(es, [128, 512], BF16, "sq2") for _ in range(3)]
                MIX_ALL = [("mixT", k_) for k_ in range(16)]
                ws_next = load_w(wso[l, 0, :, :])
                for oc in range(4):
                    ws = ws_next
                    if oc < 3:
                        ws_next = load_w(wso[l, oc + 1, :, :])
                    wv = wbuf[ws][:].rearrange("p (k c) -> p k c", c=512)
                    for oi in range(4):
                        ot = oc * 4 + oi
                        bk = ot % 4
                        S.mm_group([mm(pbank[bk][:], wv[:, kt, oi * 128:(oi + 1) * 128], mixT_h[0][:, kt, :], kt == 0, kt == 15) for kt in range(16)],
                                   reads=MIX_ALL + [("wbuf", ws)], writes=[PB(bk)])
                        evac_y(bk, ot, yb, sq)

                def dst(ot, ap, rd, s):
                    S.dma("pool", xdst[ot, :, b * TB:(b + 1) * TB], ap, reads=rd, writes=[("st_pn", s)])
                post_norm_residual(l, b, A_GPOST, xsrc, srcid, dst, es, yb)
                S.barrier()

        def phase_ffn(l, b, xsrc, srcid, xdst, dstid):
            norm_to_hT(l, b, xsrc, A_GPREF)
            with ExitStack() as es:
                actT = sb(es, [128, 44, 512], BF16, "actT")
                es_up = ExitStack()
                U = [sb(es_up, [128, 514], F32, "U") for _ in range(4)]
                acc = [sb(es_up, [128, 512], F32, "acc") for _ in range(4)]
                sgt = [sb(es_up, [128, 512], F32, "sgt") for _ in range(2)]
                cw = colsB[:, l, :]
                ws_next = load_w(wsu[l, 0, :, :])
                uc = 0
                for cu in range(22):
                    ws = ws_next
                    if cu < 21:
                        ws_next = load_w(wsu[l, cu + 1, :, :])
                    for i in range(2):
                        ffi = 2 * cu + i
                        accs = []
                        for kind in range(2):
                            tg = ffi + 44 * kind
                            bk = (2 * ffi + kind) % 6
                            proj_fm(bk, ws, 256 * kind + 128 * i)
                            u = U[uc % 4]
                            a = acc[uc % 4]
                            ur, ar_ = ("U", uc % 4), ("acc", uc % 4)
                            uc += 1
                            cp("act", u[:, 2:514], pbank[bk][:], [PB(bk)], [ur])
                            cp("pool", u[:, 0:2], halo[:, l, tg, :], [("halo", l, tg), "halo"], [(ur, "h")])
                            act(a[:], pbank[bk][:], AF.Identity, [PB(bk), "colsB"], [ar_], scale=cw[:, 2 * 88 + tg:2 * 88 + tg + 1], bias=cw[:, 264 + tg:264 + tg + 1])
                            stt(a[:], u[:, 1:513], cw[:, 88 + tg:88 + tg + 1], a[:], ALU.mult, ALU.add, [ur, (ur, "h"), ar_, "colsB"], [ar_])
                            stt(a[:], u[:, 0:512], cw[:, tg:tg + 1], a[:], ALU.mult, ALU.add, [ur, (ur, "h"), ar_, "colsB"], [ar_])
                            cp("pool", halo[:, l, tg, :], u[:, 512:514], [ur, "halo"], [("halo", l, tg)])
                            accs.append((a, ar_))
                        sg = sgt[ffi % 2]
                        act(sg[:], accs[0][0][:], AF.Silu, [accs[0][1]], [("sgt", ffi % 2)])
                        tt(actT[:, ffi, :], sg[:], accs[1][0][:], ALU.mult, [("sgt", ffi % 2), accs[1][1]], [("actT", ffi)], eng="pool")
                S.barrier()
                es_up.close()
                yb = sb(es, [128, 16, 512], F32, "ybf")
                sq = [sb(es, [128, 512], BF16, "sq2f") for _ in range(3)]
                ACT_ALL = [("actT", k_) for k_ in range(44)]
                ws_next = load_w(wsd[l, 0, :, :], 44 * 128)
                for ot in range(16):
                    ws = ws_next
                    if ot < 15:
                        ws_next = load_w(wsd[l, ot + 1, :, :], 44 * 128)
                    bk = ot % 4
                    S.mm_group([mm(pbank[bk][:], wbuf[ws][:, kt * 128:(kt + 1) * 128], actT[:, kt, :], kt == 0, kt == 43) for kt in range(44)],
                               reads=ACT_ALL + [("wbuf", ws)], writes=[PB(bk)])
                    evac_y(bk, ot, yb, sq)

                def dst(ot, ap, rd, s):
                    S.dma("pool", xdst[ot, :, b * TB:(b + 1) * TB], ap, reads=rd, writes=[("st_pn", s)])
                post_norm_residual(l, b, A_GPOSTF, xsrc, srcid, dst, es, yb)
                S.barrier()

        def xs_key(t):
            return id(t)

        def main_schedule():
            if stage("s5tab"):
                return
            for b in range(NB):
                phase_p0(b)
            if stage("p0"):
                return
            for l in range(depth):
                for b in range(NB):
                    with ExitStack() as mes:
                        mixT_h[0] = sb(mes, [128, 16, 512], BF16, "mixT")
                        norm_to_hT(l, b, xs[0], A_GPRE)
                        if stage("norm"):
                            return
                        phase_hgrn(l, b)
                        if stage("hgrn"):
                            return
                        s5gen, s5end = s5_make(l, b, mes)

                        def run_s5(n):
                            for _ in range(n):
                                try:
                                    next(s5gen)
                                except StopIteration:
                                    return
                        phase_ret(l, b, run_s5)
                        run_s5(32)
                        s5end()
                        S.barrier()
                        if stage("s5"):
                            return
                        if dbg and l == 0:
                            S.dma("pool", dbg_out["mix0"][:, :, b * TB:(b + 1) * TB].rearrange("k p t -> p k t"), mixT_h[0][:], reads=[], writes=["dbgmix"])
                            S.dma("pool", dbg_out["h0"][:, :, b * TB:(b + 1) * TB].rearrange("k p t -> p k t"), hT[:], reads=[], writes=["dbgh"])
                            S.barrier()
                        phase_outproj(l, b, xs[0], 0, xs[1], 1)
                        if dbg and l == 0:
                            S.dma("sp", dbg_out["xmid0"][:, :, b * TB:(b + 1) * TB], xs[1][:, :, b * TB:(b + 1) * TB], reads=[], writes=["dbgxm"])
                            S.barrier()
                    if stage("outproj"):
                        return
                    phase_ffn(l, b, xs[1], 1, xs[0], 0)
                    if stage("ffn"):
                        return
                    if dbg and l == 0:
                        S.dma("sp", dbg_out["xout0"][:, :, b * TB:(b + 1) * TB], xs[0][:, :, b * TB:(b + 1) * TB], reads=[], writes=["dbgxo"])
                        S.barrier()
            for b in range(NB):
                phase_p6(b)

        if not stopped[0]:
            main_schedule()
    except _Stop:
        pass
    S.barrier()
    top.close()
    return nc, S


_CACHE = {}


def kernel(**inputs):
    x = np.ascontiguousarray(inputs["x"], dtype=np.float32)
    B, T, _ = x.shape
    n_cores = B
    key = (T,)
    if key not in _CACHE:
        _CACHE[key] = build_program(T)
    nc, _ = _CACHE[key]
    cst = _build_consts()
    rope = _build_rope(0, T)
    in_maps = []
    for c in range(n_cores):
        m = {"x": x[c], "cst": cst, "rope": rope}
        for n in PARAM_NAMES:
            m[n] = np.ascontiguousarray(inputs[n], dtype=np.float32)
        in_maps.append(m)
    res = run_bass_kernel_spmd(nc, in_maps, core_ids=list(range(n_cores)))
    out = np.stack([np.asarray(res.results[c]["out"]) for c in range(n_cores)], axis=0)
    return out.astype(np.float32)
```

```python
import math
from contextlib import ExitStack

import numpy as np
import concourse.bass as bass
import concourse.mybir as mybir
from concourse.bass_utils import run_bass_kernel_spmd

F32 = mybir.dt.float32
BF16 = mybir.dt.bfloat16
I32 = mybir.dt.int32
AF = mybir.ActivationFunctionType
ALU = mybir.AluOpType

D = 2048
NIN = 5888
DFF = 5632
TB = 512
EPS = 1e-6
NCST = 2304
C_M64, C_M16, C_TPOS, C_CMASK, C_RMASK, C_ROWM, C_GDEC, C_IDENT = 0, 512, 1024, 1536, 1600, 1984, 1992, 2048


class Sched:
    ENG = ("pe", "act", "dve", "pool", "sp")

    def __init__(self, nc):
        self.nc = nc
        self.e = {"pe": nc.tensor, "act": nc.scalar, "dve": nc.vector, "pool": nc.gpsimd, "sp": nc.sync}
        self.sem = {}
        self.cnt = {}
        for k in self.ENG:
            self.sem[k] = nc.alloc_semaphore("sem_" + k)
            self.cnt[k] = 0
        self.seen = {k: {} for k in self.ENG}
        self.dsem = {}
        self.st = {}
        self.ninst = 0

    def _semof(self, key):
        if isinstance(key, str):
            return self.sem[key]
        return self.dsem[key[1]][0]

    def _wait(self, eng, tok):
        if tok is None:
            return
        key, val = tok
        if self.seen[eng].get(key, 0) >= val:
            return
        if key == eng and eng == "pe":
            return
        self.e[eng].wait_ge(self._semof(key), val)
        self.seen[eng][key] = val

    def _deps(self, eng, reads, writes):
        for r in reads:
            s = self.st.get(r)
            if s is not None:
                self._wait(eng, s["w"])
                if isinstance(r, tuple) and r[0] == "pb":
                    for t in s["r"]:
                        if t[0] != eng:
                            self._wait(eng, t)
        for w in writes:
            s = self.st.get(w)
            if s is not None:
                self._wait(eng, s["w"])
                for t in s["r"]:
                    self._wait(eng, t)

    def _commit(self, tok, reads, writes):
        for r in reads:
            s = self.st.setdefault(r, {"w": None, "r": []})
            s["r"] = [t for t in s["r"] if t[0] != tok[0]] + [tok]
        for w in writes:
            self.st[w] = {"w": tok, "r": []}

    def op(self, eng, fn, reads=(), writes=()):
        self._deps(eng, reads, writes)
        ins = fn()
        self.cnt[eng] += 1
        ins.then_inc(self.sem[eng], 1)
        self._commit((eng, self.cnt[eng]), reads, writes)
        self.ninst += 1
        return ins

    def mm_group(self, fns, reads=(), writes=()):
        self._deps("pe", reads, writes)
        ins = None
        for fn in fns:
            ins = fn()
        self.cnt["pe"] += 1
        ins.then_inc(self.sem["pe"], 1)
        self._commit(("pe", self.cnt["pe"]), reads, writes)
        self.ninst += len(fns)
        return ins

    def dma(self, q, out, in_, reads=(), writes=(), semreg=None, **kw):
        reg = semreg if semreg is not None else writes[0]
        if reg not in self.dsem:
            self.dsem[reg] = [self.nc.alloc_semaphore("dsem%d" % len(self.dsem)), 0]
        self._deps(q, reads, writes)
        ins = self.e[q].dma_start(out=out, in_=in_, **kw)
        self.dsem[reg][1] += 16
        ins.then_inc(self.dsem[reg][0], 16)
        self._commit((("d", reg), self.dsem[reg][1]), reads, writes)
        self.ninst += 1
        return ins

    def dma_group(self, q, pairs, reads=(), writes=(), semreg=None, **kw):
        reg = semreg if semreg is not None else writes[0]
        if reg not in self.dsem:
            self.dsem[reg] = [self.nc.alloc_semaphore("dsem%d" % len(self.dsem)), 0]
        self._deps(q, reads, writes)
        for (out, in_) in pairs:
            ins = self.e[q].dma_start(out=out, in_=in_, **kw)
            self.dsem[reg][1] += 16
            ins.then_inc(self.dsem[reg][0], 16)
            self.ninst += 1
        self._commit((("d", reg), self.dsem[reg][1]), reads, writes)

    def barrier(self):
        for e in self.ENG:
            for f in self.ENG:
                if f != e and self.cnt[f] > 0:
                    self._wait(e, (f, self.cnt[f]))
            for reg, (sem, tot) in self.dsem.items():
                if tot > 0:
                    self._wait(e, (("d", reg), tot))
        self.st = {}

    def finish(self, regions, eng="sp"):
        for r in regions:
            s = self.st.get(r)
            if s is not None:
                self._wait(eng, s["w"])


def _build_consts():
    c = np.zeros((128, NCST), np.float32)
    t = np.arange(512)
    c[:, C_M64:C_M64 + 512] = (t % 64 != 0).astype(np.float32)[None, :]
    c[:, C_M16:C_M16 + 512] = (t % 16 != 0).astype(np.float32)[None, :]
    c[:, C_TPOS:C_TPOS + 512] = (t + 1).astype(np.float32)[None, :]
    s = np.arange(128) % 64
    tt = np.arange(64)
    causal = (s[:, None] <= tt[None, :]).astype(np.float64)
    c[:, C_CMASK:C_CMASK + 64] = causal
    for h in range(6):
        lg = np.float32(np.log(np.float32(1.0) - np.float32(2.0) ** np.float32(-5.0 - h)))
        c[:, C_RMASK + 64 * h:C_RMASK + 64 * (h + 1)] = causal * np.exp(-64.0 * float(lg))
    r = np.arange(128)
    for gl in range(8):
        c[:, C_ROWM + gl] = (r // 16 == gl)
    for j in range(3):
        for hh in range(2):
            h = 2 * j + hh
            lg = np.float32(np.log(np.float32(1.0) - np.float32(2.0) ** np.float32(-5.0 - h)))
            c[64 * hh:64 * (hh + 1), C_GDEC + j] = np.exp(64.0 * float(lg))
    c[:, C_IDENT:C_IDENT + 128] = np.eye(128)
    return c


def _build_rope(pos0, nt):
    pos = (pos0 + np.arange(nt)).astype(np.float32)
    inv_freq = (np.float32(1.0) / (np.float32(10000.0) ** np.linspace(0.0, 1.0, 32, dtype=np.float32))).astype(np.float32)
    ang = (pos[None, :] * inv_freq[:, None]).astype(np.float32).astype(np.float64)
    cos, sin = np.cos(ang), np.sin(ang)
    tc = (np.arange(nt) % 64).astype(np.float64)
    out = np.zeros((3, 4, 128, nt), np.float32)
    for j in range(3):
        for hh in range(2):
            h = 2 * j + hh
            lg = float(np.float32(np.log(np.float32(1.0) - np.float32(2.0) ** np.float32(-5.0 - h))))
            gq = np.exp((tc + 1.0) * lg)[None, :]
            gk = np.exp((63.0 - tc) * lg)[None, :] * (64.0 ** -0.5)
            r0 = 64 * hh
            cc = np.concatenate([cos, cos], 0)
            ss = np.concatenate([-sin, sin], 0)
            out[j, 0, r0:r0 + 64] = cc * gq
            out[j, 1, r0:r0 + 64] = ss * gq
            out[j, 2, r0:r0 + 64] = cc * gk
            out[j, 3, r0:r0 + 64] = ss * gk
    return out


def _inproj_chunks():
    ch = []
    for h in range(6):
        ch.append([(h * 128, 128), (768 + h * 128, 128), (2304 + h * 128, 128), (1536 + h * 128, 128)])
    for j in range(3):
        segs = []
        for base in (3584, 3968):
            segs.append((base + 128 * j, 128))
            for hh in range(2):
                h = 2 * j + hh
                segs.append((base + h * 64 + 32, 32))
                segs.append((base + h * 64, 32))
        ch.append(segs)
    for j in range(3):
        ch.append([(4352 + 256 * j, 256), (5120 + 256 * j, 256)])
    ch.append([(3072, 512)])
    return ch


PARAM_NAMES = ["w_in", "w_out", "mix_beta", "hg_lb_logits", "hg_onorm", "s5_lam_re", "s5_lam_im", "s5_log_dt",
               "s5_b_re", "s5_b_im", "s5_c_re", "s5_c_im", "s5_d", "s5_w_glu", "s5_b_glu",
               "ffn_w_up", "ffn_conv_w", "ffn_conv_b", "ffn_w_down",
               "g_pre_mix", "g_post_mix", "g_pre_ffn", "g_post_ffn"]
PARAM_SHAPES = {
    "w_in": [2, 2048, 5888], "w_out": [2, 2048, 2048], "mix_beta": [2, 2048], "hg_lb_logits": [2, 768],
    "hg_onorm": [2, 128], "s5_lam_re": [2, 32, 64], "s5_lam_im": [2, 32, 64], "s5_log_dt": [2, 32],
    "s5_b_re": [2, 32, 64, 16], "s5_b_im": [2, 32, 64, 16], "s5_c_re": [2, 32, 16, 64], "s5_c_im": [2, 32, 16, 64],
    "s5_d": [2, 512], "s5_w_glu": [2, 512, 512], "s5_b_glu": [2, 512], "ffn_w_up": [2, 2048, 11264],
    "ffn_conv_w": [2, 3, 11264], "ffn_conv_b": [2, 11264], "ffn_w_down": [2, 5632, 2048],
    "g_pre_mix": [2, 2048], "g_post_mix": [2, 2048], "g_pre_ffn": [2, 2048], "g_post_ffn": [2, 2048],
}
A_GPRE, A_GPOST, A_GPREF, A_GPOSTF, A_BETA, A_LB0, A_LB1, A_ONORM, A_S5D, A_BGLU, A_N = 0, 16, 32, 48, 64, 80, 86, 92, 93, 97, 101


class _Stop(Exception):
    pass


def build_program(NT, depth=2, dbg=None, stop=None):
    nc = bass.Bass("TRN2", target_bir_lowering=False)
    S = Sched(nc)
    NB = NT // TB
    uid = [0]

    def din(name, shape, dt=F32):
        return nc.dram_tensor(name, list(shape), dt, kind="ExternalInput").ap()

    def dscr(name, shape, dt):
        return nc.dram_tensor(name, list(shape), dt).ap()

    x_in = din("x", [NT, D])
    P = {n: din(n, PARAM_SHAPES[n]) for n in PARAM_NAMES}
    cst_d = din("cst", [128, NCST])
    rope_d = din("rope", [3, 4, 128, NT])
    out_d = nc.dram_tensor("out", [NT, D], F32, kind="ExternalOutput").ap()

    wsi = dscr("wsi", [2, 13, 128, 8192], BF16)
    wso = dscr("wso", [2, 4, 128, 8192], BF16)
    wsu = dscr("wsu", [2, 22, 128, 8192], BF16)
    wsd = dscr("wsd", [2, 16, 128, 44 * 128], BF16)
    wsg = dscr("wsg", [2, 128, 2048], BF16)
    s5tab = dscr("s5tab", [2, 3, 128, 4096], BF16)
    etab = dscr("etab", [2, 32, 2, 128, 512], F32)
    xs = [dscr("xs0", [16, 128, NT], F32), dscr("xs1", [16, 128, NT], F32)]

    dbg_out = {}
    if dbg:
        for nm in ("mix0", "xmid0", "xout0", "h0"):
            dbg_out[nm] = nc.dram_tensor("dbg_" + nm, [16, 128, NT], F32, kind="ExternalOutput").ap()

    stopped = [False]

    def stage(name):
        if stop is not None and stop == name:
            stopped[0] = True
        return stopped[0]

    def sb(es, shape, dt, name="t"):
        uid[0] += 1
        return es.enter_context(nc.sbuf_tensor("%s_%d" % (name, uid[0]), list(shape), dt)).ap()

    top = ExitStack()
    pbank = [nc.alloc_psum_tensor("pb%d" % i, [128, 512], F32).ap() for i in range(8)]

    def PB(i):
        return ("pb", i)

    C = sb(top, [128, NCST], F32, "cst")
    identf = C[:, C_IDENT:C_IDENT + 128]
    identb = sb(top, [128, 128], BF16, "identb")
    onesb = sb(top, [128, 128], BF16, "onesb")
    colsA = sb(top, [128, 2, A_N], F32, "colsA")
    colsB = sb(top, [128, 2, 352], F32, "colsB")
    lbc = sb(top, [128, 2, 6], F32, "lbc")
    omlc = sb(top, [128, 2, 6], F32, "omlc")
    gsc = sb(top, [128, 2, 16], F32, "gsc")
    S_hg = sb(top, [128, 2, 6, 128], F32, "S_hg")
    S_rt = sb(top, [128, 2, 3, 128], F32, "S_rt")
    Wst = sb(top, [128, 2, 2, 32], F32, "Wst")
    halo = sb(top, [128, 2, 88, 2], F32, "halo")
    rho = sb(top, [128, 2, 32], F32, "rho")
    hT = sb(top, [128, 16, 512], BF16, "hT")
    mixT_h = [None]
    wbuf = [sb(top, [128, 8192], BF16, "wbuf%d" % i) for i in range(2)]
    wslot = [0]

    def load_w(src2d, ncols=8192):
        s = wslot[0] % 2
        wslot[0] += 1
        S.dma("sp", wbuf[s][:, 0:ncols], src2d, reads=[], writes=[("wbuf", s)])
        return s

    def act(out, in_, func, reads, writes, scale=1.0, bias=None):
        if bias is None:
            return S.op("act", lambda: nc.scalar.activation(out=out, in_=in_, func=func, scale=scale), reads, writes)
        return S.op("act", lambda: nc.scalar.activation(out=out, in_=in_, func=func, scale=scale, bias=bias), reads, writes)

    def tt(out, in0, in1, op, reads, writes, eng="dve"):
        e = nc.vector if eng == "dve" else nc.gpsimd
        return S.op(eng, lambda: e.tensor_tensor(out=out, in0=in0, in1=in1, op=op), reads, writes)

    def ts(out, in0, s1, s2, op0, op1, reads, writes, eng="dve"):
        e = nc.vector if eng == "dve" else nc.gpsimd
        if op1 is None:
            return S.op(eng, lambda: e.tensor_scalar(out=out, in0=in0, scalar1=s1, scalar2=None, op0=op0), reads, writes)
        return S.op(eng, lambda: e.tensor_scalar(out=out, in0=in0, scalar1=s1, scalar2=s2, op0=op0, op1=op1), reads, writes)

    def stt(out, in0, scalar, in1, op0, op1, reads, writes):
        return S.op("dve", lambda: nc.vector.scalar_tensor_tensor(out=out, in0=in0, scalar=scalar, in1=in1, op0=op0, op1=op1), reads, writes)

    def cp(eng, out, in_, reads, writes):
        if eng == "act":
            return S.op("act", lambda: nc.scalar.copy(out=out, in_=in_), reads, writes)
        e = nc.vector if eng == "dve" else nc.gpsimd
        return S.op(eng, lambda: e.tensor_copy(out=out, in_=in_), reads, writes)

    def mset(eng, ap, val, writes):
        e = nc.vector if eng == "dve" else nc.gpsimd
        return S.op(eng, lambda: e.memset(ap, val), (), writes)

    def mm(out, lhsT, rhs, start=True, stop=True):
        return lambda: nc.tensor.matmul(out, lhsT=lhsT, rhs=rhs, start=start, stop=stop)

    def tr(out, in_, ident):
        return lambda: nc.tensor.transpose(out=out, in_=in_, identity=ident)

    try:
        S.dma("sp", C[:], cst_d[:, :], reads=[], writes=["C"])
        cp("dve", identb[:], identf, ["C"], ["identb"])
        mset("dve", onesb[:], 1.0, ["onesb"])
        for nm, t_ in (("S_hg", S_hg), ("S_rt", S_rt), ("Wst", Wst), ("halo", halo)):
            mset("pool", t_[:], 0.0, [nm])

        with ExitStack() as es:
            stg = sb(es, [128, 128], F32, "stg")
            for l in range(depth):
                mset("dve", stg[:], 0.0, ["stg"])
                rows = [(P["g_pre_mix"][l], 16), (P["g_post_mix"][l], 16), (P["g_pre_ffn"][l], 16), (P["g_post_ffn"][l], 16),
                        (P["mix_beta"][l], 16), (P["hg_lb_logits"][0], 6), (P["hg_lb_logits"][1], 6),
                        (P["hg_onorm"][l], 1), (P["s5_d"][l], 4), (P["s5_b_glu"][l], 4)]
                r0 = 0
                pairs = []
                for src_, n in rows:
                    pairs.append((stg[r0:r0 + n, :], src_.rearrange("(r c) -> r c", c=128)))
                    r0 += n
                S.dma_group("sp", pairs, reads=[], writes=["stg"])
                S.mm_group([tr(pbank[0][:, 0:128], stg[:, :], identf)], reads=["stg", "C"], writes=[PB(0)])
                cp("dve", colsA[:, l, :], pbank[0][:, 0:A_N], [PB(0)], ["colsA"])
                for part in range(3):
                    nrow = 128 if part < 2 else 96
                    mset("dve", stg[:], 0.0, ["stg"])
                    pairs = []
                    r = 0
                    while r < nrow:
                        gr = part * 128 + r
                        if gr < 264:
                            j, t0 = gr // 88, gr % 88
                            n = min(88 - t0, nrow - r)
                            src_ = P["ffn_conv_w"][l, j, t0 * 128:(t0 + n) * 128]
                        else:
                            t0 = gr - 264
                            n = min(88 - t0, nrow - r)
                            src_ = P["ffn_conv_b"][l, t0 * 128:(t0 + n) * 128]
                        pairs.append((stg[r:r + n, :], src_.rearrange("(r c) -> r c", c=128)))
                        r += n
                    S.dma_group("sp", pairs, reads=[], writes=["stg"])
                    S.mm_group([tr(pbank[0][:, 0:128], stg[:, :], identf)], reads=["stg", "C"], writes=[PB(0)])
                    cp("dve", colsB[:, l, part * 128:part * 128 + nrow], pbank[0][:, 0:nrow], [PB(0)], ["colsB"])
            mset("dve", lbc[:], 0.0, ["lbc"])
            if depth > 1:
                tt(lbc[:, 1, :], colsA[:, 0, A_LB1:A_LB1 + 6], colsA[:, 0, A_LB0:A_LB0 + 6], ALU.subtract, ["colsA"], ["lbc"])
                act(lbc[:, 1, :], lbc[:, 1, :], AF.Sigmoid, ["lbc"], ["lbc"])
            ts(omlc[:], lbc[:], -1.0, 1.0, ALU.mult, ALU.add, ["lbc"], ["omlc"])
            for l in range(depth):
                cp("dve", gsc[:, l, :], colsA[:, l, A_BETA:A_BETA + 16], ["colsA"], ["gsc"])
                ts(gsc[:, l, 0:6], gsc[:, l, 0:6], colsA[:, l, A_ONORM:A_ONORM + 1], None, ALU.mult, None, ["gsc", "colsA"], ["gsc"])
            S.barrier()
        stage("setup0")

        with ExitStack() as es:
            NS = 6
            st32 = [sb(es, [128, 2048], F32, "st32") for _ in range(NS)]
            st16 = [sb(es, [128, 2048], BF16, "st16") for _ in range(NS)]
            pc = [0]
            ceng = ["dve", "act", "pool"]

            def conv_piece(srcs, dst, dst_view=None):
                s = pc[0] % NS
                pc[0] += 1
                pairs = []
                for (src_, off, n) in srcs:
                    if off is None:
                        pairs.append((st32[s][:], src_))
                    else:
                        pairs.append((st32[s][:].rearrange("p (k c) -> p k c", c=512)[:, :, off:off + n], src_))
                S.dma_group("sp", pairs, reads=[], writes=[("st32", s)])
                cp(ceng[s % 3], st16[s][:], st32[s][:], [("st32", s)], [("st16", s)])
                srcv = st16[s][:] if dst_view is None else dst_view(st16[s][:])
                S.dma("act", dst, srcv, reads=[("st16", s)], writes=[("wscr", s)])

            chunks = _inproj_chunks()
            for l in range(depth):
                for ci, segs in enumerate(chunks):
                    for kq in range(4):
                        srcs = []
                        off = 0
                        for (c0, n) in segs:
                            srcs.append((P["w_in"][l, kq * 512:(kq + 1) * 512, c0:c0 + n].rearrange("(k p) c -> p k c", p=128), off, n))
                            off += n
                        conv_piece(srcs, wsi[l, ci, :, kq * 2048:(kq + 1) * 2048])
                for ci in range(4):
                    for kq in range(4):
                        src_ = P["w_out"][l, kq * 512:(kq + 1) * 512, ci * 512:(ci + 1) * 512].rearrange("(k p) c -> p k c", p=128)
                        conv_piece([(src_, 0, 512)], wso[l, ci, :, kq * 2048:(kq + 1) * 2048])
                for ci in range(22):
                    for kq in range(4):
                        srcs = []
                        for i, c0 in enumerate((256 * ci, 5632 + 256 * ci)):
                            srcs.append((P["ffn_w_up"][l, kq * 512:(kq + 1) * 512, c0:c0 + 256].rearrange("(k p) c -> p k c", p=128), 256 * i, 256))
                        conv_piece(srcs, wsu[l, ci, :, kq * 2048:(kq + 1) * 2048])
                for kt in range(44):
                    conv_piece([(P["ffn_w_down"][l, kt * 128:(kt + 1) * 128, :], None, 0)],
                               wsd[l].rearrange("o p (k c) -> p o k c", c=128)[:, :, kt, :],
                               dst_view=lambda t: t.rearrange("p (o c) -> p o c", c=128))
                conv_piece([(P["s5_w_glu"][l].rearrange("(k p) c -> p k c", p=128), 0, 512)], wsg[l, :, :])
            S.barrier()

        stage("conv")
        for l in range(depth):
            with ExitStack() as es:
                lam = sb(es, [128, 3, 128], F32, "lam")
                mset("dve", lam[:], 0.0, ["lam"])
                S.dma_group("sp", [(lam[0:32, 0, 0:64], P["s5_lam_re"][l]), (lam[0:32, 0, 64:128], P["s5_lam_re"][l]),
                                   (lam[0:32, 1, 0:64], P["s5_lam_im"][l]), (lam[0:32, 1, 64:128], P["s5_lam_im"][l]),
                                   (lam[0:32, 2, 0:1], P["s5_log_dt"][l].rearrange("(g o) -> g o", o=1))], reads=[], writes=["lam"])
                gm = sb(es, [128, 5, 128], F32, "gm")
                act(gm[0:32, 0, 0:1], lam[0:32, 2, 0:1], AF.Exp, ["lam"], ["gm0"])
                ts(gm[0:32, 1, :], lam[0:32, 0, :], -1e-4, None, ALU.min, None, ["lam"], ["gm1"])
                ts(gm[0:32, 2, :], gm[0:32, 1, :], gm[0:32, 0, 0:1], None, ALU.mult, None, ["gm0", "gm1"], ["gm2"])
                act(gm[0:32, 3, :], gm[0:32, 2, :], AF.Exp, ["gm2"], ["gm3"])
                ts(gm[0:32, 4, :], lam[0:32, 1, :], gm[0:32, 0, 0:1], None, ALU.mult, None, ["lam", "gm0"], ["gm4"])
                cols = sb(es, [128, 4, 32], F32, "s5cols")
                for k_, src_ap in ((0, gm[0:32, 1, :]), (1, lam[0:32, 1, :]), (2, gm[0:32, 3, :]), (3, gm[0:32, 4, :])):
                    S.mm_group([tr(pbank[0][:, 0:32], src_ap, identf[0:32, 0:32])], reads=["gm1", "gm3", "gm4", "lam", "C"], writes=[PB(0)])
                    cp("dve", cols[:, k_, :], pbank[0][:, 0:32], [PB(0)], [("cols", k_)])
                cr_ = [("cols", k_) for k_ in range(4)]
                cp("dve", rho[:, l, :], cols[:, 2, :], cr_, ["rho"])
                wk = sb(es, [128, 12, 32], F32, "s5wk")
                wki = sb(es, [128, 32], I32, "s5wki")

                def sin_of(dst, src_, shift, rd, wr):
                    ts(wk[:, 10, :], src_, shift, 1.0 / (2 * math.pi), ALU.add, ALU.mult, rd, ["wk10"])
                    cp("dve", wki[:], wk[:, 10, :], ["wk10"], ["wki"])
                    cp("dve", wk[:, 11, :], wki[:], ["wki"], ["wk11"])
                    ts(wk[:, 10, :], wk[:, 11, :], -2 * math.pi, shift, ALU.mult, ALU.add, ["wk11"], ["wk10"])
                    tt(wk[:, 10, :], wk[:, 10, :], src_, ALU.add, ["wk10"] + rd, ["wk10"])
                    ts(wk[:, 10, :], wk[:, 10, :], math.pi, -math.pi, ALU.min, ALU.max, ["wk10"], ["wk10"])
                    act(dst, wk[:, 10, :], AF.Sin, ["wk10"], wr)

                sin_of(wk[:, 0, :], cols[:, 3, :], 0.0, cr_, ["wk0"])
                sin_of(wk[:, 1, :], cols[:, 3, :], math.pi / 2, cr_, ["wk1"])
                tt(wk[:, 2, :], wk[:, 1, :], cols[:, 2, :], ALU.mult, ["wk1"] + cr_, ["wk2"])
                tt(wk[:, 3, :], wk[:, 0, :], cols[:, 2, :], ALU.mult, ["wk0"] + cr_, ["wk3"])
                tt(wk[:, 4, :], cols[:, 0, :], cols[:, 0, :], ALU.mult, cr_, ["wk4"])
                tt(wk[:, 5, :], cols[:, 1, :], cols[:, 1, :], ALU.mult, cr_, ["wk5"])
                tt(wk[:, 4, :], wk[:, 4, :], wk[:, 5, :], ALU.add, ["wk4", "wk5"], ["wk4"])
                S.op("dve", lambda: nc.vector.reciprocal(out=wk[:, 4, :], in_=wk[:, 4, :]), ["wk4"], ["wk4"])
                ts(wk[:, 5, :], wk[:, 2, :], -1.0, None, ALU.add, None, ["wk2"], ["wk5"])
                tt(wk[:, 6, :], wk[:, 5, :], cols[:, 0, :], ALU.mult, ["wk5"] + cr_, ["wk6"])
                tt(wk[:, 7, :], wk[:, 3, :], cols[:, 1, :], ALU.mult, ["wk3"] + cr_, ["wk7"])
                tt(wk[:, 6, :], wk[:, 6, :], wk[:, 7, :], ALU.add, ["wk6", "wk7"], ["wk6"])
                tt(wk[:, 6, :], wk[:, 6, :], wk[:, 4, :], ALU.mult, ["wk6", "wk4"], ["wk6"])
                tt(wk[:, 7, :], wk[:, 3, :], cols[:, 0, :], ALU.mult, ["wk3"] + cr_, ["wk7"])
                tt(wk[:, 8, :], wk[:, 5, :], cols[:, 1, :], ALU.mult, ["wk5"] + cr_, ["wk8"])
                tt(wk[:, 7, :], wk[:, 7, :], wk[:, 8, :], ALU.subtract, ["wk7", "wk8"], ["wk7"])
                tt(wk[:, 7, :], wk[:, 7, :], wk[:, 4, :], ALU.mult, ["wk7", "wk4"], ["wk7"])
                bre = sb(es, [128, 32, 16], F32, "bre")
                bim = sb(es, [128, 32, 16], F32, "bim")
                S.dma_group("sp", [(bre[0:64, :, :], P["s5_b_re"][l].rearrange("g p c -> p g c")),
                                   (bim[0:64, :, :], P["s5_b_im"][l].rearrange("g p c -> p g c"))], reads=[], writes=["bre"])
                crb = wk[0:64, 6, :].unsqueeze(2).to_broadcast([64, 32, 16])
                cib = wk[0:64, 7, :].unsqueeze(2).to_broadcast([64, 32, 16])
                bb = sb(es, [128, 5, 32, 16], F32, "bb")
                tt(bb[0:64, 0], bre[0:64], crb, ALU.mult, ["bre", "wk6"], ["bb0"])
                tt(bb[0:64, 2], bim[0:64], cib, ALU.mult, ["bre", "wk7"], ["bb2"])
                tt(bb[0:64, 0], bb[0:64, 0], bb[0:64, 2], ALU.subtract, ["bb0", "bb2"], ["bb0"])
                tt(bb[0:64, 1], bim[0:64], crb, ALU.mult, ["bre", "wk6"], ["bb1"])
                tt(bb[0:64, 3], bre[0:64], cib, ALU.mult, ["bre", "wk7"], ["bb3"])
                tt(bb[0:64, 1], bb[0:64, 1], bb[0:64, 3], ALU.add, ["bb1", "bb3"], ["bb1"])
                ts(bb[0:64, 4], bb[0:64, 1], -1.0, None, ALU.mult, None, ["bb1"], ["bb4"])
                tabs = sb(es, [128, 3, 32, 128], BF16, "s5tabs")
                mset("pool", tabs[:], 0.0, ["tabs"])
                for t4 in range(4):
                    for vi, (ire, iim) in enumerate(((0, 1), (4, 0))):
                        S.mm_group([tr(pbank[1][:, 0:64], bb[0:64, ire, t4 * 8:(t4 + 1) * 8, :], identf[0:64, 0:64]),
                                    tr(pbank[1][:, 64:128], bb[0:64, iim, t4 * 8:(t4 + 1) * 8, :], identf[0:64, 0:64])],
                                   reads=["bb0", "bb1", "bb4", "C"], writes=[PB(1)])
                        for gl in range(8):
                            g = t4 * 8 + gl
                            ts(tabs[:, vi, g, :], pbank[1][:, 0:128], C[:, C_ROWM + gl:C_ROWM + gl + 1], None, ALU.mult, None, [PB(1), "C", "tabs"], [("tabs", vi, g)])
                cst_ = sb(es, [128, 4, 2, 64], F32, "cstage")
                S.dma_group("sp", [(cst_[:, :, 0, :], P["s5_c_re"][l].rearrange("(t g) c p -> (g c) t p", t=4)),
                                   (cst_[:, :, 1, :], P["s5_c_im"][l].rearrange("(t g) c p -> (g c) t p", t=4))], reads=[], writes=["cst_"])
                for t4 in range(4):
                    S.mm_group([tr(pbank[2][:, 0:128], cst_[:, t4, :, :], identf)], reads=["cst_", "C"], writes=[PB(2)])
                    for gl in range(8):
                        g = t4 * 8 + gl
                        cp("dve", tabs[0:64, 2, g, gl * 16:(gl + 1) * 16], pbank[2][0:64, gl * 16:(gl + 1) * 16], [PB(2), "tabs"], [("tabs", 2, g, 0)])
                        ts(tabs[64:128, 2, g, gl * 16:(gl + 1) * 16], pbank[2][64:128, gl * 16:(gl + 1) * 16], -1.0, None, ALU.mult, None, [PB(2), "tabs"], [("tabs", 2, g, 1)])
                S.barrier()
                S.dma("sp", s5tab[l].rearrange("v p f -> p v f"), tabs[:].rearrange("p v g c -> p v (g c)"), reads=[], writes=["s5tab"])
                ework = [sb(es, [128, 512], F32, "ew") for _ in range(3)]
                ewi = sb(es, [128, 512], I32, "ewi")
                eout = [sb(es, [128, 2, 512], F32, "eout") for _ in range(2)]
                tpos = C[:, C_TPOS:C_TPOS + 512]
                for g in range(32):
                    eo = eout[g % 2]
                    ts(ework[0][:], tpos, cols[:, 3, g:g + 1], None, ALU.mult, None, ["C"], ["ew0"])
                    for which, shift, sgn in ((0, math.pi / 2, 1.0), (1, 0.0, -1.0)):
                        ts(ework[1][:], ework[0][:], shift, 1.0 / (2 * math.pi), ALU.add, ALU.mult, ["ew0"], ["ew1"])
                        cp("dve", ewi[:], ework[1][:], ["ew1"], ["ewi"])
                        cp("dve", ework[2][:], ewi[:], ["ewi"], ["ew2"])
                        ts(ework[1][:], ework[2][:], -2 * math.pi, shift, ALU.mult, ALU.add, ["ew2"], ["ew1"])
                        tt(ework[1][:], ework[1][:], ework[0][:], ALU.add, ["ew1", "ew0"], ["ew1"])
                        ts(ework[1][:], ework[1][:], math.pi, -math.pi, ALU.min, ALU.max, ["ew1"], ["ew1"])
                        act(eo[:, which, :], ework[1][:], AF.Sin, ["ew1"], [("eo", g % 2, which)], scale=sgn)
                    S.dma("pool", etab[l, g].rearrange("w p t -> p w t"), eo[:], reads=[("eo", g % 2, 0), ("eo", g % 2, 1)], writes=[("etab", g % 2)])
                S.barrier()

        def rstd_from(pss_bank, n, out_ap, wr):
            act(out_ap, pbank[pss_bank][:], AF.Ln, [PB(pss_bank)], wr, scale=1.0 / n, bias=EPS)
            act(out_ap, out_ap, AF.Exp, wr, wr, scale=-0.5)

        def phase_p0(b):
            with ExitStack() as es:
                xin = sb(es, [128, 4, D], F32, "xin")
                xo = [sb(es, [128, 512], F32, "xo") for _ in range(4)]
                S.dma_group("sp", [(xin[:, grp, :], x_in[b * TB + grp * 128:b * TB + (grp + 1) * 128, :]) for grp in range(4)], reads=[], writes=["xin"])
                for ft in range(16):
                    bk = ft % 8
                    S.mm_group([tr(pbank[bk][:, grp * 128:(grp + 1) * 128], xin[:, grp, ft * 128:(ft + 1) * 128], identf) for grp in range(4)],
                               reads=["xin", "C"], writes=[PB(bk)])
                    cp("act" if ft % 2 else "dve", xo[ft % 4][:], pbank[bk][:], [PB(bk)], [("xo", ft % 4)])
                    S.dma("pool", xs[0][ft, :, b * TB:(b + 1) * TB], xo[ft % 4][:], reads=[("xo", ft % 4)], writes=[("st_xo", ft % 4)])
                S.barrier()

        def phase_p6(b):
            with ExitStack() as es:
                xf = sb(es, [128, 16, 512], F32, "xf")
                xt_ = [sb(es, [128, D], F32, "xtok") for _ in range(2)]
                S.dma_group("sp", [(xf[:, ft, :], xs[0][ft, :, b * TB:(b + 1) * TB]) for ft in range(16)], reads=[], writes=["xf"])
                for grp in range(4):
                    for q4 in range(4):
                        bk = (grp % 2) * 4 + q4
                        S.mm_group([tr(pbank[bk][:, i * 128:(i + 1) * 128], xf[:, q4 * 4 + i, grp * 128:(grp + 1) * 128], identf) for i in range(4)],
                                   reads=["xf", "C"], writes=[PB(bk)])
                        cp("act" if q4 % 2 else "dve", xt_[grp % 2][:, q4 * 512:(q4 + 1) * 512], pbank[bk][:], [PB(bk)], [("xtok", grp % 2, q4)])
                    S.dma("pool", out_d[b * TB + grp * 128:b * TB + (grp + 1) * 128, :], xt_[grp % 2][:],
                          reads=[("xtok", grp % 2, q4) for q4 in range(4)], writes=[("st_out", grp % 2)])
                S.barrier()

        def norm_to_hT(l, b, src, gcol0):
            with ExitStack() as es:
                xT = sb(es, [128, 16, 512], F32, "xT")
                sq = [sb(es, [128, 512], BF16, "sq") for _ in range(2)]
                rstd = sb(es, [128, 512], F32, "rstd")
                S.dma_group("sp", [(xT[:, ft, :], src[ft, :, b * TB:(b + 1) * TB]) for ft in range(16)], reads=[], writes=["xT"])
                for ft in range(16):
                    act(sq[ft % 2][:], xT[:, ft, :], AF.Square, ["xT"], [("sq", ft % 2)])
                    S.mm_group([mm(pbank[7][:], onesb[:], sq[ft % 2][:], ft == 0, ft == 15)], reads=[("sq", ft % 2), "onesb"], writes=[PB(7)])
                rstd_from(7, D, rstd[:], ["rstd"])
                for ft in range(16):
                    stt(hT[:, ft, :], xT[:, ft, :], colsA[:, l, gcol0 + ft:gcol0 + ft + 1], rstd[:], ALU.mult, ALU.mult,
                        ["xT", "rstd", "colsA"], [("hT", ft)])
                S.barrier()

        HT_ALL = [("hT", ft) for ft in range(16)]

        def proj_fm(bank, ws, col0, ncols=128):
            wv = wbuf[ws][:].rearrange("p (k c) -> p k c", c=512)
            S.mm_group([mm(pbank[bank][0:ncols, :], wv[:, kt, col0:col0 + ncols], hT[:, kt, :], kt == 0, kt == 15) for kt in range(16)],
                       reads=HT_ALL + [("wbuf", ws)], writes=[PB(bank)])

        def proj_tm(bank, ws, col0, ncols, width):
            wv = wbuf[ws][:].rearrange("p (k c) -> p k c", c=512)
            pv = pbank[bank][:].rearrange("p (g c) -> p g c", c=width)
            fns = []
            for grp in range(4):
                for kt in range(16):
                    fns.append(mm(pv[:, grp, 0:ncols], hT[:, kt, grp * 128:(grp + 1) * 128], wv[:, kt, col0:col0 + ncols], kt == 0, kt == 15))
            S.mm_group(fns, reads=HT_ALL + [("wbuf", ws)], writes=[PB(bank)])

        def head_norm_out(l, po_bank, pss_bank, gate, mt, es_tiles, rd_gate):
            sqb, rs, tmp = es_tiles
            act(sqb[:], pbank[po_bank][:], AF.Square, [PB(po_bank)], ["hn_sq"])
            S.mm_group([mm(pbank[pss_bank][:], onesb[:], sqb[:])], reads=["hn_sq", "onesb"], writes=[PB(pss_bank)])
            rstd_from(pss_bank, 128, rs[:], ["hn_rs"])
            tt(tmp[:], pbank[po_bank][:], rs[:], ALU.mult, [PB(po_bank), "hn_rs"], ["hn_tmp"])
            stt(mixT_h[0][:, mt, :], tmp[:], gsc[:, l, mt:mt + 1], gate, ALU.mult, ALU.mult, ["hn_tmp", "gsc"] + rd_gate, [("mixT", mt)])

        def phase_hgrn(l, b, run_s5):
            with ExitStack() as es:
                F = lambda nm: sb(es, [128, 512], F32, nm)
                Bt = lambda nm: sb(es, [128, 512], BF16, nm)
                qf, t1, lf, kk, gate, b64, b16, e16, e64, tmpd = [F(n) for n in ("qf", "t1", "lf", "kk", "gate", "b64", "b16", "e16", "e64", "tmpd")]
                tk = sb(es, [128, 4, 512], F32, "tk")
                qh, qt, kdec, vt, kdt, sqb = [Bt(n) for n in ("qh", "qt", "kdec", "vt", "kdt", "sqb")]
                khat = sb(es, [128, 8, 4, 64], BF16, "khat")
                AT = sb(es, [128, 4, 128], BF16, "AT")
                Sbf = sb(es, [128, 8, 128], BF16, "Sbf")
                dec = sb(es, [128, 8], F32, "dec")
                rs, tmpo = F("rs"), F("tmpo")
                mset("pool", khat[:], 0.0, ["khat"])
                mset("pool", AT[:], 0.0, ["AT"])
                m64 = C[:, C_M64:C_M64 + 512]
                m16 = C[:, C_M16:C_M16 + 512]
                cmask = C[:, C_CMASK:C_CMASK + 64]
                b64v = b64[:].rearrange("p (c j) -> p c j", j=64)
                kkv = kk[:].rearrange("p (c j) -> p c j", j=64)
                ws_next = load_w(wsi[l, 0, :, :])
                for h in range(6):
                    ws = ws_next
                    if h < 5:
                        ws_next = load_w(wsi[l, h + 1, :, :])
                    proj_fm(0, ws, 0)
                    proj_fm(1, ws, 128)
                    proj_fm(2, ws, 256)
                    proj_tm(3, ws, 384, 128, 128)
                    run_s5(1)
                    act(qf[:], pbank[0][:], AF.Silu, [PB(0)], ["qf"])
                    act(t1[:], pbank[1][:], AF.Sigmoid, [PB(1)], ["t1"])
                    act(kk[:], pbank[1][:], AF.Sigmoid, [PB(1)], ["kk"], scale=-1.0)
                    act(gate[:], pbank[2][:], AF.Silu, [PB(2)], ["gate"])
                    cp("act", vt[:], pbank[3][:], [PB(3)], ["vt"])
                    if stage("hg1"):
                        S.barrier()
                        return
                    ts(t1[:], t1[:], omlc[:, l, h:h + 1], lbc[:, l, h:h + 1], ALU.mult, ALU.add, ["t1", "omlc", "lbc"], ["t1"])
                    act(lf[:], t1[:], AF.Ln, ["t1"], ["lf"])
                    ts(kk[:], kk[:], omlc[:, l, h:h + 1], None, ALU.mult, None, ["kk", "omlc"], ["kk"])
                    S.op("dve", lambda: nc.vector.tensor_tensor_scan(out=b64[:], data0=m64, data1=lf[:], initial=0.0, op0=ALU.mult, op1=ALU.add), ["lf", "C"], ["b64"])
                    S.op("dve", lambda: nc.vector.tensor_tensor_scan(out=b16[:], data0=m16, data1=lf[:], initial=0.0, op0=ALU.mult, op1=ALU.add), ["lf", "C"], ["b16"])
                    act(e16[:], b16[:], AF.Exp, ["b16"], ["e16"])
                    act(e64[:], b64[:], AF.Exp, ["b64"], ["e64"])
                    tt(qh[:], qf[:], e16[:], ALU.mult, ["qf", "e16"], ["qh"])
                    tt(qt[:], qf[:], e64[:], ALU.mult, ["qf", "e64"], ["qt"])
                    tt(tmpd[:].rearrange("p (c j) -> p c j", j=64), b64v[:, :, 63:64].to_broadcast([128, 8, 64]), b64v, ALU.subtract, ["b64"], ["tmpd"])
                    act(tmpd[:], tmpd[:], AF.Exp, ["tmpd"], ["tmpd"])
                    tt(kdec[:], kk[:], tmpd[:], ALU.mult, ["kk", "tmpd"], ["kdec"])
                    act(dec[:], b64v[:, :, 63], AF.Exp, ["b64"], ["dec"])
                    if stage("hg2"):
                        S.barrier()
                        return
                    for i in range(4):
                        n = 16 * (i + 1)
                        tkv = tk[:, i, :].rearrange("p (c j) -> p c j", j=64)
                        if i == 0:
                            act(tkv[:, :, 0:n], b64v[:, :, 0:n], AF.Exp, ["b64"], [("tk", i)], scale=-1.0)
                        else:
                            tt(tkv[:, :, 0:n], b64v[:, :, 16 * i - 1:16 * i].to_broadcast([128, 8, n]), b64v[:, :, 0:n], ALU.subtract, ["b64"], [("tk", i)])
                            act(tkv[:, :, 0:n], tkv[:, :, 0:n], AF.Exp, [("tk", i)], [("tk", i)])
                        tt(khat[:, :, i, 0:n], kkv[:, :, 0:n], tkv[:, :, 0:n], ALU.mult, ["kk", ("tk", i)], ["khat"])
                    if stage("hg3"):
                        S.barrier()
                        return
                    fns = []
                    for c in range(8):
                        j2, par = c // 2, c % 2
                        for i in range(4):
                            fns.append(mm(pbank[4][par * 64:(par + 1) * 64, j2 * 64 + 16 * i:j2 * 64 + 16 * i + 16], khat[:, c, i, :],
                                          qh[:, c * 64 + 16 * i:c * 64 + 16 * i + 16]))
                    S.mm_group(fns, reads=["khat", "qh"], writes=[PB(4)])
                    run_s5(1)
                    pAv = pbank[4][:, 0:256].rearrange("p (j t) -> p j t", t=64)
                    for par in range(2):
                        sl = slice(par * 64, (par + 1) * 64)
                        tt(AT[sl, :, par * 64:(par + 1) * 64], pAv[sl, :, :], cmask[sl, :].unsqueeze(1).to_broadcast([64, 4, 64]), ALU.mult,
                           [PB(4), "C"], ["AT"])
                    if stage("hg4"):
                        S.barrier()
                        return
                    p7b = pbank[4][:].bitcast(BF16)
                    S.mm_group([tr(p7b[:, grp * 128:(grp + 1) * 128], kdec[:, grp * 128:(grp + 1) * 128], identb[:]) for grp in range(4)],
                               reads=["kdec", "identb"], writes=[PB(4)])
                    cp("act", kdt[:], p7b[:, 0:512], [PB(4)], ["kdt"])
                    if stage("hg5"):
                        S.barrier()
                        return
                    for par in range(2):
                        fns = []
                        sl = slice(par * 64, (par + 1) * 64)
                        for j2 in range(4):
                            fns.append(mm(pbank[2 + par][:, j2 * 128:(j2 + 1) * 128], kdt[sl, j2 * 128:(j2 + 1) * 128], vt[sl, j2 * 128:(j2 + 1) * 128]))
                        S.mm_group(fns, reads=["kdt", "vt"], writes=[PB(2 + par)])
                    if stage("hg6"):
                        S.barrier()
                        return
                    Sreg = ("S_hg", l, h)
                    for c in range(8):
                        cp("act", Sbf[:, c, :], S_hg[:, l, h, :], [Sreg, "S_hg"], [("Sbf", c)])
                        stt(S_hg[:, l, h, :], S_hg[:, l, h, :], dec[:, c:c + 1], pbank[2 + c % 2][:, (c // 2) * 128:(c // 2 + 1) * 128], ALU.mult, ALU.add,
                            [Sreg, "S_hg", "dec", PB(2 + c % 2)], [Sreg])
                    if stage("hg7"):
                        S.barrier()
                        return
                    for j2 in range(4):
                        S.mm_group([mm(pbank[0][:, j2 * 128:(j2 + 1) * 128], vt[:, j2 * 128:(j2 + 1) * 128], AT[:, j2, :], True, False),
                                    mm(pbank[0][:, (2 * j2) * 64:(2 * j2 + 1) * 64], Sbf[:, 2 * j2, :], qt[:, (2 * j2) * 64:(2 * j2 + 1) * 64], False, False),
                                    mm(pbank[0][:, (2 * j2 + 1) * 64:(2 * j2 + 2) * 64], Sbf[:, 2 * j2 + 1, :], qt[:, (2 * j2 + 1) * 64:(2 * j2 + 2) * 64], False, True)],
                                   reads=["vt", "AT", ("Sbf", 2 * j2), ("Sbf", 2 * j2 + 1), "qt"], writes=[PB(0)])
                    if stage("hg8"):
                        S.barrier()
                        return
                    head_norm_out(l, 0, 1, gate[:], h, (sqb, rs, tmpo), ["gate"])
                    run_s5(1)
                    if stage("hg9"):
                        S.barrier()
                        return
                S.barrier()

        def phase_ret(l, b, run_s5):
            with ExitStack() as es:
                F = lambda nm: sb(es, [128, 512], F32, nm)
                Bt = lambda nm: sb(es, [128, 512], BF16, nm)
                tab = sb(es, [128, 4, 512], F32, "rtab")
                ta, tb_ = F("ta"), F("tb")
                qd, kd, kdt, sqb = [Bt(n) for n in ("qd", "kd", "kdt", "sqb")]
                qdz = [Bt("qdz0"), Bt("qdz1")]
                vt = sb(es, [128, 4, 256], BF16, "vtr")
                gates = [F("g0"), F("g1")]
                PT = [sb(es, [128, 4, 128], BF16, "PT%d" % i) for i in range(2)]
                Sbf = sb(es, [128, 8, 128], BF16, "Sbfr")
                rs, tmpo = F("rs"), F("tmpo")
                for t_ in (qdz[0], qdz[1], PT[0], PT[1]):
                    mset("pool", t_[:], 0.0, ["rz"])
                for j in range(3):
                    wa = load_w(wsi[l, 6 + j, :, :])
                    wb = load_w(wsi[l, 9 + j, :, :])
                    S.dma_group("sp", [(tab[:, w, :], rope_d[j, w, :, b * TB:(b + 1) * TB]) for w in range(4)], reads=[], writes=["rtab"])
                    for w in range(4):
                        proj_fm(w, wa, 128 * w)
                    for (z, zp, cq, sq_, dst, nm) in ((0, 1, 0, 1, qd, "qd"), (2, 3, 2, 3, kd, "kd")):
                        tt(ta[:], pbank[z][:], tab[:, cq, :], ALU.mult, [PB(z), "rtab"], ["ta"])
                        tt(tb_[:], pbank[zp][:], tab[:, sq_, :], ALU.mult, [PB(zp), "rtab"], ["tb"])
                        tt(dst[:], ta[:], tb_[:], ALU.add, ["ta", "tb"], [nm])
                        if nm == "qd":
                            for hh in range(2):
                                sl = slice(64 * hh, 64 * hh + 64)
                                tt(qdz[hh][sl, :], ta[sl, :], tb_[sl, :], ALU.add, ["ta", "tb", "rz"], [("qdz", hh)])
                    wv = wbuf[wb][:].rearrange("p (k c) -> p k c", c=512)
                    fns = []
                    for grp in range(4):
                        pvv = pbank[4 - grp // 2][:, (grp % 2) * 256:(grp % 2 + 1) * 256]
                        for kt in range(16):
                            fns.append(mm(pvv, hT[:, kt, grp * 128:(grp + 1) * 128], wv[:, kt, 0:256], kt == 0, kt == 15))
                    S.mm_group(fns, reads=HT_ALL + [("wbuf", wb)], writes=[PB(4), PB(3)])
                    for half in range(2):
                        cp("act", vt[:, 2 * half:2 * half + 2, :], pbank[4 - half][:].rearrange("p (g c) -> p g c", c=256), [PB(4), PB(3)], [("vtr", half)])
                    for hh in range(2):
                        proj_fm(hh, wb, 256 + 128 * hh)
                        act(gates[hh][:], pbank[hh][:], AF.Silu, [PB(hh)], [("gate", hh)])
                    run_s5(1)
                    VT = [("vtr", 0), ("vtr", 1)]
                    p3b = pbank[2][:].bitcast(BF16)
                    S.mm_group([tr(p3b[:, grp * 128:(grp + 1) * 128], kd[:, grp * 128:(grp + 1) * 128], identb[:]) for grp in range(4)],
                               reads=["kd", "identb"], writes=[PB(2)])
                    cp("act", kdt[:], p3b[:, 0:512], [PB(2)], ["kdt"])
                    for par in range(2):
                        fns = []
                        sl = slice(par * 64, (par + 1) * 64)
                        for j2 in range(4):
                            for hh in range(2):
                                fns.append(mm(pbank[3 - par][64 * hh:64 * hh + 64, j2 * 128:(j2 + 1) * 128],
                                              kdt[sl, j2 * 128 + 64 * hh:j2 * 128 + 64 * hh + 64], vt[sl, j2, 128 * hh:128 * hh + 128]))
                        S.mm_group(fns, reads=["kdt"] + VT, writes=[PB(3 - par)])
                    run_s5(1)
                    Sreg = ("S_rt", l, j)
                    for c in range(8):
                        cp("act", Sbf[:, c, :], S_rt[:, l, j, :], [Sreg, "S_rt"], [("Sbf", c)])
                        stt(S_rt[:, l, j, :], S_rt[:, l, j, :], C[:, C_GDEC + j:C_GDEC + j + 1], pbank[3 - c % 2][:, (c // 2) * 128:(c // 2 + 1) * 128],
                            ALU.mult, ALU.add, [Sreg, "S_rt", "C", PB(3 - c % 2)], [Sreg])
                    for hh in range(2):
                        h = 2 * j + hh
                        fns = []
                        for c in range(8):
                            j2, par = c // 2, c % 2
                            fns.append(mm(pbank[4][par * 64:(par + 1) * 64, j2 * 64:(j2 + 1) * 64], kd[:, c * 64:(c + 1) * 64], qdz[hh][:, c * 64:(c + 1) * 64]))
                        S.mm_group(fns, reads=["kd", ("qdz", hh), "rz"], writes=[PB(4)])
                        pSv = pbank[4][:, 0:256].rearrange("p (j t) -> p j t", t=64)
                        rm = C[:, C_RMASK + 64 * h:C_RMASK + 64 * (h + 1)]
                        for par in range(2):
                            sl = slice(par * 64, (par + 1) * 64)
                            tt(PT[hh][sl, :, par * 64:(par + 1) * 64], pSv[sl, :, :], rm[sl, :].unsqueeze(1).to_broadcast([64, 4, 64]), ALU.mult,
                               [PB(4), "C", "rz"], [("PT", hh)])
                        ob = hh
                        for j2 in range(4):
                            S.mm_group([mm(pbank[ob][:, j2 * 128:(j2 + 1) * 128], vt[:, j2, 128 * hh:128 * hh + 128], PT[hh][:, j2, :], True, False),
                                        mm(pbank[ob][:, (2 * j2) * 64:(2 * j2 + 1) * 64], Sbf[:, 2 * j2, :], qdz[hh][:, (2 * j2) * 64:(2 * j2 + 1) * 64], False, False),
                                        mm(pbank[ob][:, (2 * j2 + 1) * 64:(2 * j2 + 2) * 64], Sbf[:, 2 * j2 + 1, :], qdz[hh][:, (2 * j2 + 1) * 64:(2 * j2 + 2) * 64], False, True)],
                                       reads=VT + [("PT", hh), ("Sbf", 2 * j2), ("Sbf", 2 * j2 + 1), ("qdz", hh), ("gate", hh)], writes=[PB(ob)])
                        head_norm_out(l, ob, 4, gates[hh][:], 10 + h, (sqb, rs, tmpo), [("gate", hh)])
                        run_s5(1)
                S.barrier()

        def s5_make(l, b, es):
            F = lambda nm: sb(es, [128, 512], F32, nm)
            tabs2 = [sb(es, [128, 3, 8, 128], BF16, "s5t") for _ in range(2)]
            r1, r2_ = F("r1"), F("r2")
            r3 = sb(es, [128, 2], F32, "r3")
            wg = sb(es, [128, 4, 512], BF16, "wglu")
            uf = sb(es, [128, 4, 512], F32, "uf")
            ub = sb(es, [128, 4, 512], BF16, "ub")
            et = [sb(es, [128, 2, 512], F32, "et") for _ in range(2)]
            t1, t2, t3, t4, xa, xb_, Wa, Wb = [F(n) for n in ("t1", "t2", "t3", "t4", "xa", "xb", "Wa", "Wb")]
            Sg = [sb(es, [128, 512], BF16, "Sg") for _ in range(2)]
            ya = F("ya")
            yg = sb(es, [128, 4, 512], F32, "yg")
            ygb = sb(es, [128, 4, 512], BF16, "ygb")
            sg_ = F("sg")
            S.dma("sp", wg[:].rearrange("p k c -> p (k c)"), wsg[l, :, :], reads=[], writes=["wglu"])
            ws = load_w(wsi[l, 12, :, :])
            for t in range(4):
                proj_fm(t, ws, 128 * t)
                cp("act", uf[:, t, :], pbank[t][:], [PB(t)], [("uf", t)])
                cp("dve", ub[:, t, :], uf[:, t, :], [("uf", t)], [("ub", t)])

            def y_accum(g):
                t, gl = g // 8, g % 8
                S.mm_group([mm(pbank[7][:], tabs_of(g)[:, 2, gl, :], Sg[g % 2][:], gl == 0, gl == 7)], reads=[("s5t", t % 2), ("Sg", g % 2)], writes=[PB(7)])
                if gl == 7:
                    stt(ya[:], uf[:, t, :], colsA[:, l, A_S5D + t:A_S5D + t + 1], pbank[7][:], ALU.mult, ALU.add, [("uf", t), "colsA", PB(7)], ["ya"])
                    act(yg[:, t, :], ya[:], AF.Gelu, ["ya"], [("yg", t)])
                    cp("dve", ygb[:, t, :], yg[:, t, :], [("yg", t)], [("ygb", t)])

            def tabs_of(g):
                return tabs2[(g // 8) % 2]

            def gen():
                for g in range(32):
                    t, gl = g // 8, g % 8
                    tb_ = tabs_of(g)
                    if gl == 0:
                        S.dma("sp", tb_[:].rearrange("p v g c -> p v (g c)"), s5tab[l, :, :, t * 1024:(t + 1) * 1024].rearrange("v p f -> p v f"),
                              reads=[], writes=[("s5t", t % 2)])
                    e = et[g % 2]
                    S.dma("sp", e[:], etab[l, g].rearrange("w p t -> p w t"), reads=[], writes=[("et", g % 2)])
                    ER = [("et", g % 2)]
                    Ec, Es = e[:, 0, :], e[:, 1, :]
                    S.mm_group([mm(pbank[5][:], tb_[:, 0, gl, :], ub[:, t, :])], reads=[("s5t", t % 2), ("ub", t)], writes=[PB(5)])
                    S.mm_group([mm(pbank[6][:], tb_[:, 1, gl, :], ub[:, t, :])], reads=[("s5t", t % 2), ("ub", t)], writes=[PB(6)])
                    if g >= 1:
                        y_accum(g - 1)
                    tt(t1[:], pbank[5][:], Ec, ALU.mult, [PB(5)] + ER, ["t1"])
                    tt(t4[:], pbank[5][:], Es, ALU.mult, [PB(5)] + ER, ["t4"])
                    tt(t2[:], pbank[6][:], Es, ALU.mult, [PB(6)] + ER, ["t2"])
                    tt(t3[:], pbank[6][:], Ec, ALU.mult, [PB(6)] + ER, ["t3"])
                    tt(xa[:], t1[:], t2[:], ALU.add, ["t1", "t2"], ["xa"], eng="pool")
                    tt(xb_[:], t3[:], t4[:], ALU.subtract, ["t3", "t4"], ["xb"], eng="pool")
                    rh = rho[:, l, g:g + 1].to_broadcast([128, 512])
                    S.op("dve", lambda: nc.vector.tensor_tensor_scan(out=Wa[:], data0=rh, data1=xa[:], initial=Wst[:, l, 0, g:g + 1], op0=ALU.mult, op1=ALU.add),
                         ["xa", "rho", ("Wst", l, g)], ["Wa"])
                    S.op("dve", lambda: nc.vector.tensor_tensor_scan(out=Wb[:], data0=rh, data1=xb_[:], initial=Wst[:, l, 1, g:g + 1], op0=ALU.mult, op1=ALU.add),
                         ["xb", "rho", ("Wst", l, g)], ["Wb"])
                    tt(r1[:], Wa[:], Ec, ALU.mult, ["Wa"] + ER, ["r1"], eng="pool")
                    tt(r2_[:], Wb[:], Es, ALU.mult, ["Wb"] + ER, ["r2"], eng="pool")
                    tt(Sg[g % 2][:], r1[:], r2_[:], ALU.subtract, ["r1", "r2"], [("Sg", g % 2)])
                    tt(Wst[:, l, 0, g:g + 1], r1[:, 511:512], r2_[:, 511:512], ALU.subtract, ["r1", "r2"], [("Wst", l, g)], eng="pool")
                    tt(r3[:, 0:1], Wb[:, 511:512], e[:, 0, 511:512], ALU.mult, ["Wb"] + ER, ["r3"], eng="pool")
                    tt(r3[:, 1:2], Wa[:, 511:512], e[:, 1, 511:512], ALU.mult, ["Wa"] + ER, ["r3"], eng="pool")
                    tt(Wst[:, l, 1, g:g + 1], r3[:, 0:1], r3[:, 1:2], ALU.add, ["r3", ("Wst", l, g)], [("Wst", l, g)], eng="pool")
                    yield
                y_accum(31)

            def end():
                for ot in range(4):
                    bk = 5 + ot % 2
                    S.mm_group([mm(pbank[bk][:], wg[:, kt, ot * 128:(ot + 1) * 128], ygb[:, kt, :], kt == 0, kt == 3) for kt in range(4)],
                               reads=["wglu"] + [("ygb", k_) for k_ in range(4)], writes=[PB(bk)])
                    act(sg_[:], pbank[bk][:], AF.Sigmoid, [PB(bk), "colsA"], ["sg"], bias=colsA[:, l, A_BGLU + ot:A_BGLU + ot + 1])
                    stt(mixT_h[0][:, 6 + ot, :], yg[:, ot, :], gsc[:, l, 6 + ot:7 + ot], sg_[:], ALU.mult, ALU.mult, [("yg", ot), "gsc", "sg"], [("mixT", 6 + ot)])

            return gen(), end

        def post_norm_residual(l, b, gcol0, xsrc, srcid, dstfn, es, yb):
            rstd = sb(es, [128, 512], F32, "rstd2")
            xr = [sb(es, [128, 512], F32, "xr") for _ in range(2)]
            tmp = [sb(es, [128, 512], F32, "pn_tmp") for _ in range(2)]
            rstd_from(7, D, rstd[:], ["rstd2"])
            for ot in range(16):
                s = ot % 2
                S.dma("sp", xr[s][:], xsrc[ot, :, b * TB:(b + 1) * TB], reads=[], writes=[("xr", s)])
                stt(tmp[s][:], yb[:, ot, :], colsA[:, l, gcol0 + ot:gcol0 + ot + 1], rstd[:], ALU.mult, ALU.mult, [("yb", ot), "colsA", "rstd2"], [("pn_tmp", s)])
                tt(tmp[s][:], tmp[s][:], xr[s][:], ALU.add, [("pn_tmp", s), ("xr", s)], [("pn_tmp", s)], eng="pool")
                dstfn(ot, tmp[s][:], [("pn_tmp", s)], s)

        def evac_y(bank, ot, yb, sq):
            cp("act", yb[:, ot, :], pbank[bank][:], [PB(bank)], [("yb", ot)])
            act(sq[ot % 3][:], pbank[bank][:], AF.Square, [PB(bank)], [("sq2", ot % 3)])
            if ot >= 1:
                o1 = ot - 1
                S.mm_group([mm(pbank[7][:], onesb[:], sq[o1 % 3][:], o1 == 0, False)], reads=[("sq2", o1 % 3), "onesb"], writes=[PB(7)])
            if ot == 15:
                S.mm_group([mm(pbank[7][:], onesb[:], sq[ot % 3][:], False, True)], reads=[("sq2", ot % 3), "onesb"], writes=[PB(7)])

        def phase_outproj(l, b, xsrc, srcid, xdst, dstid):
            with ExitStack() as es:
                yb = sb(es, [128, 16, 512], F32, "yb")
                sq = [sb(es, [128, 512], BF16, "sq2") for _ in range(3)]
                MIX_ALL = [("mixT", k_) for k_ in range(16)]
                ws_next = load_w(wso[l, 0, :, :])
                for oc in range(4):
                    ws = ws_next
                    if oc < 3:
                        ws_next = load_w(wso[l, oc + 1, :, :])
                    wv = wbuf[ws][:].rearrange("p (k c) -> p k c", c=512)
                    for oi in range(4):
                        ot = oc * 4 + oi
                        bk = ot % 4
                        S.mm_group([mm(pbank[bk][:], wv[:, kt, oi * 128:(oi + 1) * 128], mixT_h[0][:, kt, :], kt == 0, kt == 15) for kt in range(16)],
                                   reads=MIX_ALL + [("wbuf", ws)], writes=[PB(bk)])
                        evac_y(bk, ot, yb, sq)

                def dst(ot, ap, rd, s):
                    S.dma("pool", xdst[ot, :, b * TB:(b + 1) * TB], ap, reads=rd, writes=[("st_pn", s)])
                post_norm_residual(l, b, A_GPOST, xsrc, srcid, dst, es, yb)
                S.barrier()

        def phase_ffn(l, b, xsrc, srcid, xdst, dstid):
            norm_to_hT(l, b, xsrc, A_GPREF)
            with ExitStack() as es:
                actT = sb(es, [128, 44, 512], BF16, "actT")
                es_up = ExitStack()
                U = [sb(es_up, [128, 514], F32, "U") for _ in range(4)]
                acc = [sb(es_up, [128, 512], F32, "acc") for _ in range(4)]
                sgt = [sb(es_up, [128, 512], F32, "sgt") for _ in range(2)]
                cw = colsB[:, l, :]
                ws_next = load_w(wsu[l, 0, :, :])
                uc = 0
                for cu in range(22):
                    ws = ws_next
                    if cu < 21:
                        ws_next = load_w(wsu[l, cu + 1, :, :])
                    for i in range(2):
                        ffi = 2 * cu + i
                        accs = []
                        for kind in range(2):
                            tg = ffi + 44 * kind
                            bk = (2 * ffi + kind) % 6
                            proj_fm(bk, ws, 256 * kind + 128 * i)
                            u = U[uc % 4]
                            a = acc[uc % 4]
                            ur, ar_ = ("U", uc % 4), ("acc", uc % 4)
                            uc += 1
                            cp("act", u[:, 2:514], pbank[bk][:], [PB(bk)], [ur])
                            cp("pool", u[:, 0:2], halo[:, l, tg, :], [("halo", l, tg), "halo"], [(ur, "h")])
                            act(a[:], pbank[bk][:], AF.Identity, [PB(bk), "colsB"], [ar_], scale=cw[:, 2 * 88 + tg:2 * 88 + tg + 1], bias=cw[:, 264 + tg:264 + tg + 1])
                            stt(a[:], u[:, 1:513], cw[:, 88 + tg:88 + tg + 1], a[:], ALU.mult, ALU.add, [ur, (ur, "h"), ar_, "colsB"], [ar_])
                            stt(a[:], u[:, 0:512], cw[:, tg:tg + 1], a[:], ALU.mult, ALU.add, [ur, (ur, "h"), ar_, "colsB"], [ar_])
                            cp("pool", halo[:, l, tg, :], u[:, 512:514], [ur, "halo"], [("halo", l, tg)])
                            accs.append((a, ar_))
                        sg = sgt[ffi % 2]
                        act(sg[:], accs[0][0][:], AF.Silu, [accs[0][1]], [("sgt", ffi % 2)])
                        tt(actT[:, ffi, :], sg[:], accs[1][0][:], ALU.mult, [("sgt", ffi % 2), accs[1][1]], [("actT", ffi)], eng="pool")
                S.barrier()
                es_up.close()
                yb = sb(es, [128, 16, 512], F32, "ybf")
                sq = [sb(es, [128, 512], BF16, "sq2f") for _ in range(3)]
                ACT_ALL = [("actT", k_) for k_ in range(44)]
                ws_next = load_w(wsd[l, 0, :, :], 44 * 128)
                for ot in range(16):
                    ws = ws_next
                    if ot < 15:
                        ws_next = load_w(wsd[l, ot + 1, :, :], 44 * 128)
                    bk = ot % 4
                    S.mm_group([mm(pbank[bk][:], wbuf[ws][:, kt * 128:(kt + 1) * 128], actT[:, kt, :], kt == 0, kt == 43) for kt in range(44)],
                               reads=ACT_ALL + [("wbuf", ws)], writes=[PB(bk)])
                    evac_y(bk, ot, yb, sq)

                def dst(ot, ap, rd, s):
                    S.dma("pool", xdst[ot, :, b * TB:(b + 1) * TB], ap, reads=rd, writes=[("st_pn", s)])
                post_norm_residual(l, b, A_GPOSTF, xsrc, srcid, dst, es, yb)
                S.barrier()

        def xs_key(t):
            return id(t)

        def main_schedule():
            if stage("s5tab"):
                return
            for b in range(NB):
                phase_p0(b)
            if stage("p0"):
                return
            for l in range(depth):
                for b in range(NB):
                    with ExitStack() as mes:
                        mixT_h[0] = sb(mes, [128, 16, 512], BF16, "mixT")
                        norm_to_hT(l, b, xs[0], A_GPRE)
                        if stage("norm"):
                            return
                        s5gen, s5end = s5_make(l, b, mes)

                        def run_s5(n):
                            for _ in range(n):
                                try:
                                    next(s5gen)
                                except StopIteration:
                                    return
                        phase_hgrn(l, b, run_s5)
                        if stage("hgrn"):
                            S.barrier()
                            return
                        phase_ret(l, b, run_s5)
                        run_s5(32)
                        s5end()
                        S.barrier()
                        if stage("s5"):
                            return
                        if dbg and l == 0:
                            S.dma("pool", dbg_out["mix0"][:, :, b * TB:(b + 1) * TB].rearrange("k p t -> p k t"), mixT_h[0][:], reads=[], writes=["dbgmix"])
                            S.dma("pool", dbg_out["h0"][:, :, b * TB:(b + 1) * TB].rearrange("k p t -> p k t"), hT[:], reads=[], writes=["dbgh"])
                            S.barrier()
                        phase_outproj(l, b, xs[0], 0, xs[1], 1)
                        if dbg and l == 0:
                            S.dma("sp", dbg_out["xmid0"][:, :, b * TB:(b + 1) * TB], xs[1][:, :, b * TB:(b + 1) * TB], reads=[], writes=["dbgxm"])
                            S.barrier()
                    if stage("outproj"):
                        return
                    phase_ffn(l, b, xs[1], 1, xs[0], 0)
                    if stage("ffn"):
                        return
                    if dbg and l == 0:
                        S.dma("sp", dbg_out["xout0"][:, :, b * TB:(b + 1) * TB], xs[0][:, :, b * TB:(b + 1) * TB], reads=[], writes=["dbgxo"])
                        S.barrier()
            for b in range(NB):
                phase_p6(b)

        if not stopped[0]:
            main_schedule()
    except _Stop:
        pass
    S.barrier()
    top.close()
    return nc, S


_CACHE = {}


def kernel(**inputs):
    x = np.ascontiguousarray(inputs["x"], dtype=np.float32)
    B, T, _ = x.shape
    key = (T,)
    if key not in _CACHE:
        _CACHE[key] = build_program(T)
    nc, _ = _CACHE[key]
    cst = _build_consts()
    rope = _build_rope(0, T)
    n_cores = 8
    active = [0, 1, 4, 5][:B]
    real = {"cst": cst, "rope": rope}
    for n in PARAM_NAMES:
        real[n] = np.ascontiguousarray(inputs[n], dtype=np.float32)
    zeros = {k: np.zeros_like(v) for k, v in real.items()}
    zx = np.zeros_like(x[0])
    in_maps = []
    for c in range(n_cores):
        if c in active:
            m = dict(real)
            m["x"] = x[active.index(c)]
        else:
            m = dict(zeros)
            m["x"] = zx
        in_maps.append(m)
    res = run_bass_kernel_spmd(nc, in_maps, core_ids=list(range(n_cores)))
    out = np.stack([np.asarray(res.results[c]["out"]) for c in active], axis=0)
    return out.astype(np.float32)
```
